# Optimizing a Trainium2 kernel written in Bass

```python
import jax, jax.numpy as jnp
from jax import lax
import numpy as np

D_MODEL = 2048
BATCH = 4
SEQ = 8192
DEPTH = 1

D_MIX = D_MODEL
D_CONV = D_MIX // 2
CONV_WIDTH = 3
N_HEADS = 8
HEAD_DIM = 128
D_ATTN = N_HEADS * HEAD_DIM
N_KV_HEADS = 2
GQA_GROUP = N_HEADS // N_KV_HEADS
D_KV = N_KV_HEADS * HEAD_DIM
N_BRANCH = 3
CMP_LEN = 32
CMP_STRIDE = 16
CMP_HIDDEN = 256
SLC_LEN = 64
N_SLC = 16
WINDOW = 512
Q_BLOCK = 128
ROPE_THETA = 10000.0
D_FF = 4 * D_MODEL
EPS = 1e-6
FORCE_BONUS = 1e4
D_IN = 3 * D_CONV + D_ATTN + 2 * N_BRANCH * D_KV + N_BRANCH * N_HEADS

kernel_name = "hybrid_shortconv_nsa_adaln_block"


def rms_norm(x, g):
    xf = x.astype(jnp.float32)
    y = xf * lax.rsqrt(jnp.mean(xf * xf, axis=-1, keepdims=True) + EPS)
    return (y * g.astype(jnp.float32)).astype(x.dtype)


def modulate(h, shift, scale):
    return h * (1 + scale[:, None, :]) + shift[:, None, :]


def rope_tables(seq):
    inv = ROPE_THETA ** (-jnp.arange(0, HEAD_DIM, 2, dtype=jnp.float32) / HEAD_DIM)
    ang = jnp.arange(seq, dtype=jnp.float32)[:, None] * inv[None, :]
    return jnp.cos(ang), jnp.sin(ang)


def apply_rope(x, cos, sin):
    xf = x.astype(jnp.float32)
    x1, x2 = jnp.split(xf, 2, axis=-1)
    c = cos[:, None, :]
    s = sin[:, None, :]
    return jnp.concatenate([x1 * c - x2 * s, x2 * c + x1 * s], axis=-1).astype(x.dtype)


def masked_softmax(s, mask):
    s = jnp.where(mask, s, jnp.finfo(jnp.float32).min)
    p = jax.nn.softmax(s, axis=-1)
    return jnp.where(mask, p, 0.0)


def split_points():
    sizes = [D_CONV, D_CONV, D_CONV, D_ATTN] + [D_KV] * (2 * N_BRANCH) + [N_BRANCH * N_HEADS]
    return [int(v) for v in np.cumsum(sizes)[:-1]]


def short_conv_mixer(u_b, u_c, u_h, conv_w, conv_b):
    v = u_c * u_h
    S = v.shape[1]
    vp = jnp.pad(v, ((0, 0), (CONV_WIDTH - 1, 0), (0, 0)))
    z = conv_b + sum(conv_w[k] * vp[:, k:k + S] for k in range(CONV_WIDTH))
    return u_b * z


def compress(kv, pe, w1, w2):
    S = kv.shape[2]
    n_cmp = (S - CMP_LEN) // CMP_STRIDE + 1
    idx = jnp.arange(n_cmp)[:, None] * CMP_STRIDE + jnp.arange(CMP_LEN)[None, :]
    blocks = kv[:, :, idx, :] + pe
    flat = blocks.reshape(blocks.shape[:3] + (CMP_LEN * HEAD_DIM,))
    return jax.nn.gelu(flat @ w1) @ w2


def nsa_attention(q, k_cmp, v_cmp, k_slc, v_slc, k_win, v_win, gates):
    B, _, S, _ = q.shape
    n_cmp = k_cmp.shape[2]
    n_slc = S // SLC_LEN
    top = min(N_SLC, n_slc)
    n_qb = S // Q_BLOCK
    scale = HEAD_DIM ** -0.5
    cmp_end = jnp.arange(n_cmp) * CMP_STRIDE + CMP_LEN - 1
    ci = jnp.arange(n_cmp)[:, None]
    sj = jnp.arange(n_slc)[None, :]
    cmp_to_slc = ((ci * CMP_STRIDE <= sj * SLC_LEN + SLC_LEN - 1)
                  & (ci * CMP_STRIDE + CMP_LEN - 1 >= sj * SLC_LEN)).astype(jnp.float32)
    k_blocks = k_slc.reshape(B, N_KV_HEADS, n_slc, SLC_LEN, HEAD_DIM)
    v_blocks = v_slc.reshape(B, N_KV_HEADS, n_slc, SLC_LEN, HEAD_DIM)
    pad = ((0, 0), (0, 0), (WINDOW, 0), (0, 0))
    k_wp = jnp.pad(k_win, pad)
    v_wp = jnp.pad(v_win, pad)
    bi = jnp.arange(B)[:, None, None, None]
    gi = jnp.arange(N_KV_HEADS)[None, :, None, None]
    blk_ids = jnp.arange(n_slc)[None, :]
    in_blk = jnp.arange(SLC_LEN)

    def one_block(qb):
        qs = qb * Q_BLOCK
        t = qs + jnp.arange(Q_BLOCK)
        qbk = lax.dynamic_slice_in_dim(q, qs, Q_BLOCK, axis=2).reshape(
            B, N_KV_HEADS, GQA_GROUP, Q_BLOCK, HEAD_DIM)
        s = jnp.einsum('bgrqd,bgnd->bgrqn', qbk, k_cmp).astype(jnp.float32) * scale
        p_cmp = masked_softmax(s, cmp_end[None, :] <= t[:, None])
        o_cmp = jnp.einsum('bgrqn,bgnd->bgrqd', p_cmp.astype(v_cmp.dtype), v_cmp)
        imp = jnp.einsum('bgrqn,nj->bgqj', p_cmp, cmp_to_slc)
        cur = (t // SLC_LEN)[:, None]
        valid = blk_ids * SLC_LEN <= t[:, None]
        forced = (blk_ids == 0) | (blk_ids == cur) | (blk_ids == cur - 1)
        score = jnp.where(valid, imp + jnp.where(forced, FORCE_BONUS, 0.0), -1.0)
        _, sel = lax.top_k(score, top)
        k_sel = k_blocks[bi, gi, sel].reshape(B, N_KV_HEADS, Q_BLOCK, top * SLC_LEN, HEAD_DIM)
        v_sel = v_blocks[bi, gi, sel].reshape(B, N_KV_HEADS, Q_BLOCK, top * SLC_LEN, HEAD_DIM)
        pos = (sel[..., None] * SLC_LEN + in_blk).reshape(B, N_KV_HEADS, Q_BLOCK, top * SLC_LEN)
        s = jnp.einsum('bgrqd,bgqkd->bgrqk', qbk, k_sel).astype(jnp.float32) * scale
        p = masked_softmax(s, (pos <= t[:, None])[:, :, None])
        o_slc = jnp.einsum('bgrqk,bgqkd->bgrqd', p.astype(v_sel.dtype), v_sel)
        kw = lax.dynamic_slice_in_dim(k_wp, qs, WINDOW + Q_BLOCK, axis=2)
        vw = lax.dynamic_slice_in_dim(v_wp, qs, WINDOW + Q_BLOCK, axis=2)
        kpos = (qs - WINDOW + jnp.arange(WINDOW + Q_BLOCK))[None, :]
        wmask = (kpos >= 0) & (kpos <= t[:, None]) & (kpos > t[:, None] - WINDOW)
        s = jnp.einsum('bgrqd,bgkd->bgrqk', qbk, kw).astype(jnp.float32) * scale
        p = masked_softmax(s, wmask)
        o_win = jnp.einsum('bgrqk,bgkd->bgrqd', p.astype(vw.dtype), vw)
        g = lax.dynamic_slice_in_dim(gates, qs, Q_BLOCK, axis=1).reshape(
            B, Q_BLOCK, N_KV_HEADS, GQA_GROUP, N_BRANCH).transpose(0, 2, 3, 1, 4)
        return g[..., 0:1] * o_cmp + g[..., 1:2] * o_slc + g[..., 2:3] * o_win

    out = lax.map(one_block, jnp.arange(n_qb))
    return out.transpose(1, 0, 4, 2, 3, 5).reshape(B, S, D_ATTN)


def setup_inputs(seed: int = 0) -> dict:
    key = jax.random.key(seed)
    ks = jax.random.split(key, 24)
    f32 = jnp.float32

    def nrm(k, shape, s):
        return jax.random.normal(k, shape, f32) * s

    L = DEPTH
    return {
        "x": nrm(ks[0], (BATCH, SEQ, D_MODEL), 1.0),
        "c": nrm(ks[1], (BATCH, D_MODEL), 1.0),
        "w_ada": nrm(ks[2], (L, D_MODEL, 6 * D_MODEL), 0.5 * D_MODEL ** -0.5),
        "b_ada": nrm(ks[3], (L, 6 * D_MODEL), 0.02),
        "norm1_g": 1.0 + nrm(ks[4], (L, D_MODEL), 0.02),
        "w_in": nrm(ks[5], (L, D_MODEL, D_IN), D_MODEL ** -0.5),
        "conv_w": nrm(ks[6], (L, CONV_WIDTH, D_CONV), CONV_WIDTH ** -0.5),
        "conv_b": nrm(ks[7], (L, D_CONV), 0.02),
        "cmp_pe_k": nrm(ks[8], (L, CMP_LEN, HEAD_DIM), 0.1),
        "cmp_pe_v": nrm(ks[9], (L, CMP_LEN, HEAD_DIM), 0.1),
        "cmp_w1_k": nrm(ks[10], (L, CMP_LEN * HEAD_DIM, CMP_HIDDEN), (CMP_LEN * HEAD_DIM) ** -0.5),
        "cmp_w2_k": nrm(ks[11], (L, CMP_HIDDEN, HEAD_DIM), CMP_HIDDEN ** -0.5),
        "cmp_w1_v": nrm(ks[12], (L, CMP_LEN * HEAD_DIM, CMP_HIDDEN), (CMP_LEN * HEAD_DIM) ** -0.5),
        "cmp_w2_v": nrm(ks[13], (L, CMP_HIDDEN, HEAD_DIM), CMP_HIDDEN ** -0.5),
        "gnorm_conv_g": 1.0 + nrm(ks[14], (L, D_CONV), 0.02),
        "gnorm_attn_g": 1.0 + nrm(ks[15], (L, D_ATTN), 0.02),
        "w_out": nrm(ks[16], (L, D_MIX, D_MODEL), D_MIX ** -0.5),
        "norm2_g": 1.0 + nrm(ks[17], (L, D_MODEL), 0.02),
        "w_ff1": nrm(ks[18], (L, D_MODEL, D_FF), D_MODEL ** -0.5),
        "w_ff2": nrm(ks[19], (L, D_FF, D_MODEL), D_FF ** -0.5),
        "normf_g": 1.0 + nrm(ks[20], (D_MODEL,), 0.02),
    }


def reference(x, c, w_ada, b_ada, norm1_g, w_in, conv_w, conv_b, cmp_pe_k, cmp_pe_v,
              cmp_w1_k, cmp_w2_k, cmp_w1_v, cmp_w2_v, gnorm_conv_g, gnorm_attn_g,
              w_out, norm2_g, w_ff1, w_ff2, normf_g):
    B, S, _ = x.shape
    cos, sin = rope_tables(S)
    mod_all = jnp.einsum('bd,ldm->lbm', jax.nn.silu(c), w_ada) + b_ada[:, None, :]
    cuts = split_points()

    def kv_heads(t, rope):
        t = t.reshape(B, S, N_KV_HEADS, HEAD_DIM)
        if rope:
            t = apply_rope(t, cos, sin)
        return t.transpose(0, 2, 1, 3)

    for l in range(DEPTH):
        sh1, sc1, g1, sh2, sc2, g2 = jnp.split(mod_all[l], 6, axis=-1)
        h = modulate(rms_norm(x, norm1_g[l]), sh1, sc1)
        u = h @ w_in[l]
        u_b, u_c, u_h, q, kc, vc, ksl, vsl, kwn, vwn, gl = jnp.split(u, cuts, axis=-1)
        y_conv = short_conv_mixer(u_b, u_c, u_h, conv_w[l], conv_b[l])
        q = apply_rope(q.reshape(B, S, N_HEADS, HEAD_DIM), cos, sin).transpose(0, 2, 1, 3)
        k_cmp = compress(kv_heads(kc, True), cmp_pe_k[l], cmp_w1_k[l], cmp_w2_k[l])
        v_cmp = compress(kv_heads(vc, False), cmp_pe_v[l], cmp_w1_v[l], cmp_w2_v[l])
        gates = jax.nn.sigmoid(gl.astype(jnp.float32)).astype(x.dtype).reshape(
            B, S, N_HEADS, N_BRANCH)
        y_attn = nsa_attention(q, k_cmp, v_cmp, kv_heads(ksl, True), kv_heads(vsl, False),
                               kv_heads(kwn, True), kv_heads(vwn, False), gates)
        mixed = jnp.concatenate([rms_norm(y_conv, gnorm_conv_g[l]),
                                 rms_norm(y_attn, gnorm_attn_g[l])], axis=-1)
        x = x + g1[:, None, :] * (mixed @ w_out[l])
        h = modulate(rms_norm(x, norm2_g[l]), sh2, sc2)
        x = x + g2[:, None, :] * (jnp.square(jax.nn.relu(h @ w_ff1[l])) @ w_ff2[l])
    return rms_norm(x, normf_g)
```

```python
import contextlib
import numpy as np
import ml_dtypes
import concourse.bass as bass
import concourse.mybir as mybir
from concourse.bass_utils import run_bass_kernel_spmd

F32 = mybir.dt.float32
BF16 = mybir.dt.bfloat16
AF = mybir.ActivationFunctionType
ALU = mybir.AluOpType
BF = ml_dtypes.bfloat16

D = 2048
S = 8192
NOB = 32
NOWN = 4096
NHALO = 64
DFF = 8192
NEG = -30000.0
EPS = 1e-6
KC = 16


class TK:
    def __init__(self, nc, es):
        self.nc, self.es = nc, es
        self.eng = {}
        for nm, h in (("pe", nc.tensor), ("act", nc.scalar), ("dve", nc.vector),
                      ("pool", nc.gpsimd), ("sp", nc.sync)):
            self.eng[nm] = dict(h=h, sem=es.enter_context(nc.semaphore("se_" + nm)), cnt=0,
                                seen={}, name=nm)
        self.lastw = {}
        self.readers = {}
        self.dsem = {}
        self.free_sems = []
        self.nsd = 0

    def _deps(self, reads, writes):
        d = []
        for k in reads:
            t = self.lastw.get(k)
            if t:
                d.append(t)
        for k in writes:
            t = self.lastw.get(k)
            if t:
                d.append(t)
            d.extend(self.readers.get(k, {}).values())
        return d

    def _wait(self, e, deps):
        for (sid, sem, val, src) in deps:
            if src == "pe" and e["name"] == "pe":
                continue
            if e["seen"].get(sid, 0) >= val:
                continue
            e["h"].wait_ge(sem, val)
            e["seen"][sid] = val

    def _commit(self, tok, reads, writes):
        for k in writes:
            self.lastw[k] = tok
            self.readers[k] = {}
        for k in reads:
            self.readers.setdefault(k, {})[tok[0]] = tok

    def op(self, en, fn, reads=(), writes=()):
        e = self.eng[en]
        self._wait(e, self._deps(reads, writes))
        inst = fn(e["h"])
        e["cnt"] += 1
        inst.then_inc(e["sem"], 1)
        self._commit((en, e["sem"], e["cnt"], en), reads, writes)

    def dma(self, qn, out, in_, reads=(), writes=(), skey=None):
        e = self.eng[qn]
        self._wait(e, self._deps(reads, writes))
        if skey not in self.dsem:
            if self.free_sems:
                self.dsem[skey] = self.free_sems.pop()
            else:
                nm = "sd%d" % self.nsd
                self.nsd += 1
                self.dsem[skey] = dict(sem=self.es.enter_context(self.nc.semaphore(nm)), cnt=0, name=nm)
        s = self.dsem[skey]
        s["cnt"] += 16
        e["h"].dma_start(out=out, in_=in_).then_inc(s["sem"], 16)
        self._commit((s["name"], s["sem"], s["cnt"], "dma"), reads, writes)

    def barrier(self):
        allsems = list(self.dsem.values()) + self.free_sems
        for e in self.eng.values():
            for o in self.eng.values():
                if o is not e and o["cnt"] > 0 and e["seen"].get(o["name"], 0) < o["cnt"]:
                    e["h"].wait_ge(o["sem"], o["cnt"])
                    e["seen"][o["name"]] = o["cnt"]
            for s_ in allsems:
                if s_["cnt"] > 0 and e["seen"].get(s_["name"], 0) < s_["cnt"]:
                    e["h"].wait_ge(s_["sem"], s_["cnt"])
                    e["seen"][s_["name"]] = s_["cnt"]
        self.free_sems = allsems
        self.dsem = {}
        self.lastw = {}
        self.readers = {}

    def final_wait(self, en, keys):
        e = self.eng[en]
        self._wait(e, self._deps(keys, ()))


def build_nc(stop_after=99, debug=(), p1_tiles=16):
    nc = bass.Bass("TRN2", target_bir_lowering=False)

    def din(name, shape, dt=F32):
        return nc.dram_tensor(name, list(shape), dt, kind="ExternalInput").ap()

    def dscr(name, shape, dt, out=False):
        kind = "ExternalOutput" if (out or name in debug) else "Internal"
        return nc.dram_tensor(name, list(shape), dt, kind=kind).ap()

    x_ctx = din("x_ctx", [S, D])
    x_own = din("x_own", [NOWN + NHALO, D])
    c_fm = din("c_fm", [128, KC])
    w_ada = din("w_ada", [D, 6 * D])
    b_ada_fm = din("b_ada_fm", [128, 96])
    n1g_fm = din("n1g_fm", [128, KC])
    n2g_fm = din("n2g_fm", [128, KC])
    w_kv = din("w_kv", [D, 2304])
    w_own = din("w_own", [D, 5120])
    w_g = din("w_g", [D, 24])
    convw_fm = din("convw_fm", [128, 8, 3])
    convb_fm = din("convb_fm", [128, 8])
    pe_k = din("pe_k", [32, 128])
    pe_v = din("pe_v", [32, 128])
    w1_k = din("w1_k", [4096, 256])
    w1_v = din("w1_v", [4096, 256])
    w2_k = din("w2_k", [256, 128])
    w2_v = din("w2_v", [256, 128])
    gcg_fm = din("gcg_fm", [128, 8])
    gag_fm = din("gag_fm", [128, 8])
    w_out = din("w_out", [D, D])
    w_ff1 = din("w_ff1", [D, DFF])
    w_ff2 = din("w_ff2", [DFF, D])
    nfg_row = din("nfg_row", [128, D])
    cos_ctx = din("cos_ctx", [128, S])
    sin_ctx = din("sin_ctx", [128, S])
    cosq_own = din("cosq_own", [128, NOWN])
    sinq_own = din("sinq_own", [128, NOWN])
    E_in = din("E_in", [128, S], BF16)
    ident_f_in = din("ident_f_in", [128, 128])
    ident_b_in = din("ident_b_in", [128, 128], BF16)
    C_in = din("C_in", [128, 4, 128], BF16)
    cmpbias_in = din("cmpbias_in", [NOB, 128, 2, 128], BF16)
    winbias_in = din("winbias_in", [128, 6, 128], BF16)
    diagbias_in = din("diagbias_in", [128, 2, 128], BF16)
    vm_in = din("vm_in", [NOB, 128, 128])
    bv_in = din("bv_in", [NOB, 128, 128])
    halomask_in = din("halomask_in", [128, NHALO])

    out = nc.dram_tensor("out", [NOWN, D], F32, kind="ExternalOutput").ap()

    kslc_d = dscr("kslc_d", [2, 128, S], BF16)
    kwin_d = dscr("kwin_d", [2, 128, S], BF16)
    kc_d = dscr("kc_d", [2, 128, S], BF16)
    vc_d = dscr("vc_d", [2, 128, S], BF16)
    vslc_d = dscr("vslc_d", [2, 128, 64, 129], BF16)
    vwin_d = dscr("vwin_d", [2, 128, 64, 129], BF16)
    q_d = dscr("q_d", [NOB, 128, 8, 128], BF16)
    yc_d = dscr("yc_d", [8, 128, NOWN], BF16)
    ya_d = dscr("ya_d", [NOB, 128, 1024], F32)
    x1_d = dscr("x1_d", [NOWN, D], F32)
    h2_d = dscr("h2_d", [KC, 128, NOWN], BF16)
    y2_d = dscr("y2_d", [NOWN, D], F32)
    dbg_d = dscr("dbg_d", [128, 4096], F32, out=True) if "dbg_d" in debug else None

    with contextlib.ExitStack() as es:
        tk = TK(nc, es)

        def sb(name, shape, dt, stack=es):
            return stack.enter_context(nc.sbuf_tensor(name, list(shape), dt))

        PS = [es.enter_context(nc.psum_tensor("ps%d" % i, [128, 512], F32)) for i in range(8)]

        def pk(i):
            return ("ps", i)

        def dump(name, tile, shape, dt, keys):
            if name not in debug:
                return
            d = nc.dram_tensor(name, list(shape), dt, kind="ExternalOutput").ap()
            tk.dma("sp", d, tile, reads=keys, writes=[name], skey=name)

        ident_f = sb("ident_f", [128, 128], F32)
        ident_b = sb("ident_b", [128, 128], BF16)
        ones_f = sb("ones_f", [128, 128], F32)
        ones_b = sb("ones_b", [128, 128], BF16)
        A1 = sb("A1", [128, KC], F32)
        B1 = sb("B1", [128, KC], F32)
        A2 = sb("A2", [128, KC], F32)
        B2 = sb("B2", [128, KC], F32)
        G2 = sb("G2", [128, KC], F32)
        G1row = sb("G1row", [128, D], F32)
        gates_sb = sb("gates_sb", [128, NOB, 24], F32)
        kcmpT = sb("kcmpT", [128, 2, 512], BF16)
        vcmp = sb("vcmp", [128, 2, 4, 257], BF16)
        small = sb("small", [128, 64], F32)

        tk.dma("sp", ident_f[:], ident_f_in, writes=["ident_f"], skey="ident_f")
        tk.dma("sp", ident_b[:], ident_b_in, writes=["ident_b"], skey="ident_b")
        tk.op("dve", lambda e: e.memset(ones_f[:], 1.0), writes=["ones_f"])
        tk.op("dve", lambda e: e.memset(ones_b[:], 1.0), writes=["ones_b"])

        def norm_transpose(xt, xkey, npart, nblk, Asc, Bsc, hT, hkeyf, col0, xh, xhkey, junk, ssq, pb0):
            for blk in range(nblk):
                tk.op("act", lambda e, blk=blk: e.activation(out=xh[:npart, blk, :], in_=xt[:npart, blk, :], func=AF.Square,
                                                            accum_out=ssq[:npart, blk:blk + 1]),
                      reads=[(xkey, blk)], writes=[(xhkey, blk), ("ssq", blk)])
                tk.op("dve", lambda e, blk=blk: e.tensor_scalar(out=ssq[:npart, 8 + blk:9 + blk], in0=ssq[:npart, blk:blk + 1],
                                                               scalar1=1.0 / D, scalar2=EPS, op0=ALU.mult, op1=ALU.add),
                      reads=[("ssq", blk)], writes=[("ssq2", blk)])
                tk.op("act", lambda e, blk=blk: e.activation(out=ssq[:npart, 16 + blk:17 + blk], in_=ssq[:npart, 8 + blk:9 + blk],
                                                            func=AF.Sqrt),
                      reads=[("ssq2", blk)], writes=[("ssq3", blk)])
                tk.op("dve", lambda e, blk=blk: e.reciprocal(out=ssq[:npart, 24 + blk:25 + blk], in_=ssq[:npart, 16 + blk:17 + blk]),
                      reads=[("ssq3", blk)], writes=[("rstd", blk)])
                tk.op("act", lambda e, blk=blk: e.activation(out=xh[:npart, blk, :], in_=xt[:npart, blk, :], func=AF.Copy,
                                                            scale=ssq[:npart, 24 + blk:25 + blk]),
                      reads=[(xkey, blk), ("rstd", blk)], writes=[(xhkey, blk)])
            ncol = nblk * npart
            for kc in range(KC):
                pb = pb0 + (kc % 2)
                for blk in range(nblk):
                    tk.op("pe", lambda e, kc=kc, blk=blk, pb=pb: e.transpose(
                        out=PS[pb][:, blk * npart:(blk + 1) * npart], in_=xh[:npart, blk, kc * 128:(kc + 1) * 128],
                        identity=ident_f[:npart, :npart]),
                        reads=[(xhkey, blk), "ident_f"], writes=[pk(pb)])
                tk.op("dve", lambda e, kc=kc, pb=pb: e.tensor_scalar(
                    out=hT[:, kc, col0:col0 + ncol], in0=PS[pb][:, 0:ncol], scalar1=Asc[:, kc:kc + 1], scalar2=Bsc[:, kc:kc + 1],
                    op0=ALU.mult, op1=ALU.add),
                    reads=[pk(pb), "AB"], writes=[hkeyf(kc)])

        with contextlib.ExitStack() as p0:
            cf = sb("cf", [128, KC], F32, p0)
            siluc = sb("siluc", [128, KC], F32, p0)
            modfm = sb("modfm", [128, 96], F32, p0)
            bada = sb("bada", [128, 96], F32, p0)
            n1g = sb("n1g", [128, KC], F32, p0)
            n2g = sb("n2g", [128, KC], F32, p0)
            diag = sb("diag", [128, 2, 128], F32, p0)
            wa = [sb("wa%d" % i, [128, KC, 1024], F32, p0) for i in range(2)]
            tk.dma("sp", cf[:], c_fm, writes=["cf"], skey="cf")
            tk.dma("sp", bada[:], b_ada_fm, writes=["bada"], skey="bada")
            tk.dma("sp", n1g[:], n1g_fm, writes=["n1g"], skey="n1g")
            tk.dma("sp", n2g[:], n2g_fm, writes=["n2g"], skey="n2g")
            tk.op("act", lambda e: e.activation(out=siluc[:], in_=cf[:], func=AF.Silu), reads=["cf"], writes=["siluc"])
            wav = w_ada.rearrange("(kc p) n -> p kc n", p=128)
            for G in range(12):
                s = G % 2
                tk.dma("sp" if G % 2 == 0 else "act", wa[s][:], wav[:, :, G * 1024:(G + 1) * 1024], writes=[("wa", s)], skey=("wa", s))
                for fc in range(8):
                    col = G * 8 + fc
                    for k in range(KC):
                        tk.op("pe", lambda e, s=s, fc=fc, k=k, col=col: e.matmul(
                            PS[0][:, col:col + 1], lhsT=wa[s][:, k, fc * 128:(fc + 1) * 128], rhs=siluc[:, k:k + 1],
                            start=(k == 0), stop=(k == KC - 1)),
                            reads=[("wa", s), "siluc"], writes=[pk(0)])
            tk.op("dve", lambda e: e.tensor_tensor(out=modfm[:], in0=PS[0][:, 0:96], in1=bada[:], op=ALU.add),
                  reads=[pk(0), "bada"], writes=["modfm"])
            tk.op("dve", lambda e: e.tensor_copy(out=B1[:], in_=modfm[:, 0:16]), reads=["modfm"], writes=["B1"])
            tk.op("dve", lambda e: e.scalar_tensor_tensor(out=A1[:], in0=modfm[:, 16:32], scalar=1.0, in1=n1g[:], op0=ALU.add, op1=ALU.mult),
                  reads=["modfm", "n1g"], writes=["A1"])
            tk.op("dve", lambda e: e.tensor_copy(out=B2[:], in_=modfm[:, 48:64]), reads=["modfm"], writes=["B2"])
            tk.op("dve", lambda e: e.scalar_tensor_tensor(out=A2[:], in0=modfm[:, 64:80], scalar=1.0, in1=n2g[:], op0=ALU.add, op1=ALU.mult),
                  reads=["modfm", "n2g"], writes=["A2"])
            tk.op("dve", lambda e: e.tensor_copy(out=G2[:], in_=modfm[:, 80:96]), reads=["modfm"], writes=["G2", "AB"])
            for kc in range(KC):
                s = kc % 2
                tk.op("dve", lambda e, kc=kc, s=s: e.tensor_scalar(out=diag[:, s, :], in0=ident_f[:], scalar1=modfm[:, 32 + kc:33 + kc],
                                                                 scalar2=None, op0=ALU.mult),
                      reads=["ident_f", "modfm"], writes=[("diag", s)])
                pb = 1 + (kc // 4) % 2
                tk.op("pe", lambda e, kc=kc, s=s, pb=pb: e.matmul(PS[pb][:, (kc % 4) * 128:(kc % 4 + 1) * 128], lhsT=ones_f[:], rhs=diag[:, s, :],
                                                                 start=True, stop=True),
                      reads=["ones_f", ("diag", s)], writes=[pk(pb)])
                if kc % 4 == 3:
                    tk.op("act", lambda e, kc=kc, pb=pb: e.activation(out=G1row[:, (kc - 3) * 128:(kc + 1) * 128], in_=PS[pb][:], func=AF.Copy),
                          reads=[pk(pb)], writes=["G1row"])
            dump("dbg_modfm", modfm[:], [128, 96], F32, ["modfm"])
            dump("dbg_A1", A1[:], [128, KC], F32, ["A1"])
            dump("dbg_G1row", G1row[:], [128, D], F32, ["G1row"])
            dump("dbg_siluc", siluc[:], [128, KC], F32, ["siluc"])
        tk.barrier()
        if stop_after <= 0:
            return _finish(nc, tk, out, es, ["G1row", "AB"])

        with contextlib.ExitStack() as p1:
            wkv = sb("wkv", [128, KC, 2304], BF16, p1)
            xraw = sb("xraw1", [128, 4, D], F32, p1)
            xh = sb("xh1", [128, 4, D], F32, p1)
            hT = [sb("hT1_%d" % i, [128, KC, 512], BF16, p1) for i in range(2)]
            junk = None
            ssq = sb("ssq1", [128, 32], F32, p1)
            cst = [sb("cst1_%d" % i, [128, 512], F32, p1) for i in range(2)]
            snt = [sb("snt1_%d" % i, [128, 512], F32, p1) for i in range(2)]
            t1 = sb("t1_1", [128, 512], F32, p1)
            t2 = sb("t2_1", [128, 512], F32, p1)
            ko = [sb("ko1_%d" % i, [128, 512], BF16, p1) for i in range(2)]
            vaug = [sb("vaug1_%d" % i, [128, 4, 129], BF16, p1) for i in range(2)]
            wkvv = w_kv.rearrange("(kc p) n -> p kc n", p=128)
            for j in range(0, 2304, 384):
                tk.dma("pool", wkv[:, :, j:j + 384], wkvv[:, :, j:j + 384], writes=[("wkv", j)], skey=("wkv", j))
            wkv_keys = [("wkv", j) for j in range(0, 2304, 384)]
            for i in range(2):
                tk.op("dve", lambda e, i=i: e.memset(vaug[i][:, :, 128:129], 1.0), writes=[("vaug", i)])
            xcv = x_ctx.rearrange("(cb p) d -> cb p d", p=128)
            kout = 0
            def front1(tt):
                hs_ = tt % 2
                for blk in range(4):
                    tk.dma("sp", xraw[:, blk, :], xcv[tt * 4 + blk], writes=[("xraw", blk)], skey=("xraw", blk))
                tk.dma("sp", cst[hs_][:], cos_ctx[:, tt * 512:(tt + 1) * 512], writes=[("cst", hs_)], skey=("cst", hs_))
                tk.dma("sp", snt[hs_][:], sin_ctx[:, tt * 512:(tt + 1) * 512], writes=[("snt", hs_)], skey=("snt", hs_))
                norm_transpose(xraw, "xraw", 128, 4, A1, B1, hT[hs_], lambda kc, hs_=hs_: ("hT", hs_, kc), 0, xh, "xh", junk, ssq, 0)
                if tt == 0:
                    dump("dbg_ssq", ssq[:], [128, 32], F32, [("rstd", b_) for b_ in range(4)])
                    dump("dbg_xh", xh[:, 0, :], [128, D], F32, [("xh", 0)])
                    dump("dbg_hT", hT[0][:], [128, KC, 512], BF16, [("hT", 0, kc) for kc in range(KC)])

            def back1(tt):
                nonlocal kout
                hs_ = tt % 2
                for kind, dst in ((0, kc_d), (2, kslc_d), (4, kwin_d)):
                    for g in range(2):
                        m_main = kind * 2 + g
                        m_sw = (kind + 1) * 2 + g
                        pa, pb_ = 2 + 2 * (kout % 3), 3 + 2 * (kout % 3)
                        for (m, pbk) in ((m_main, pa), (m_sw, pb_)):
                            for k in range(KC):
                                tk.op("pe", lambda e, m=m, pbk=pbk, k=k: e.matmul(PS[pbk][:], lhsT=wkv[:, k, m * 128:(m + 1) * 128], rhs=hT[hs_][:, k, :],
                                                                                 start=(k == 0), stop=(k == KC - 1)),
                                      reads=wkv_keys + [("hT", hs_, k)], writes=[pk(pbk)])
                        tk.op("dve", lambda e, pa=pa: e.tensor_tensor(out=t1[:], in0=PS[pa][:], in1=cst[hs_][:], op=ALU.mult),
                              reads=[pk(pa), ("cst", hs_)], writes=["t1"])
                        tk.op("dve", lambda e, pb_=pb_: e.tensor_tensor(out=t2[:], in0=PS[pb_][:], in1=snt[hs_][:], op=ALU.mult),
                              reads=[pk(pb_), ("snt", hs_)], writes=["t2"])
                        ks = kout % 2
                        tk.op("pool", lambda e, ks=ks: e.tensor_tensor(out=ko[ks][:], in0=t1[:], in1=t2[:], op=ALU.add),
                              reads=["t1", "t2"], writes=[("ko", ks)])
                        tk.dma("act", dst[g, :, tt * 512:(tt + 1) * 512], ko[ks][:], reads=[("ko", ks)],
                               writes=[(dst.tensor.name, g, tt)], skey=("ko", ks))
                        kout += 1
                for g in range(2):
                    m = 12 + g
                    pa = 2 + 2 * (kout % 3)
                    for k in range(KC):
                        tk.op("pe", lambda e, m=m, pa=pa, k=k: e.matmul(PS[pa][:], lhsT=wkv[:, k, m * 128:(m + 1) * 128], rhs=hT[hs_][:, k, :],
                                                                       start=(k == 0), stop=(k == KC - 1)),
                              reads=wkv_keys + [("hT", hs_, k)], writes=[pk(pa)])
                    ks = kout % 2
                    tk.op("act", lambda e, ks=ks, pa=pa: e.activation(out=ko[ks][:], in_=PS[pa][:], func=AF.Copy),
                          reads=[pk(pa)], writes=[("ko", ks)])
                    tk.dma("act", vc_d[g, :, tt * 512:(tt + 1) * 512], ko[ks][:], reads=[("ko", ks)],
                           writes=[("vc_d", g, tt)], skey=("ko", ks))
                    kout += 1
                for blk in range(4):
                    cb = tt * 4 + blk
                    pa = 2 + 2 * (kout % 3)
                    vs = kout % 2
                    for k in range(KC):
                        tk.op("pe", lambda e, pa=pa, k=k, blk=blk: e.matmul(PS[pa][:], lhsT=hT[hs_][:, k, blk * 128:(blk + 1) * 128], rhs=wkv[:, k, 1792:2304],
                                                                           start=(k == 0), stop=(k == KC - 1)),
                              reads=wkv_keys + [("hT", hs_, k)], writes=[pk(pa)])
                    tk.op("act", lambda e, vs=vs, pa=pa: e.activation(out=vaug[vs][:, :, 0:128], in_=PS[pa][:].rearrange("p (a b) -> p a b", a=4),
                                                                     func=AF.Copy),
                          reads=[pk(pa)], writes=[("vaug", vs)])
                    tk.dma("act", vslc_d[:, :, cb, :].rearrange("g p e -> p g e"), vaug[vs][:, 0:2, :], reads=[("vaug", vs)],
                           writes=[("vslc_d", cb)], skey=("vaug", vs))
                    tk.dma("act", vwin_d[:, :, cb, :].rearrange("g p e -> p g e"), vaug[vs][:, 2:4, :], reads=[("vaug", vs)],
                           writes=[("vwin_d", cb)], skey=("vaug", vs, 1))
                    kout += 1
            front1(0)
            for tt in range(p1_tiles):
                if tt + 1 < p1_tiles:
                    front1(tt + 1)
                back1(tt)
        tk.barrier()
        if stop_after <= 1:
            return _finish(nc, tk, out, es, [])

        with contextlib.ExitStack() as p2:
            w1 = sb("w1c", [128, 32, 256], BF16, p2)
            w2 = sb("w2c", [128, 2, 128], BF16, p2)
            pet = sb("pet", [32, 128], F32, p2)
            peT = sb("peT", [128, 32], BF16, p2)
            src = sb("srcc", [128, S], BF16, p2)
            hb = sb("hbias", [128, 2], F32, p2)
            hx = sb("hx", [128, 512], F32, p2)
            hx2 = sb("hx2", [128, 512], F32, p2)
            hx3 = sb("hx3", [128, 512], F32, p2)
            gl = sb("gelu", [128, 2, 512], BF16, p2)
            tk.op("dve", lambda e: e.memset(gl[:], 0.0), writes=["gelu"])
            for g in range(2):
                tk.dma("sp", vcmp[:, g, :, 129:257], C_in, writes=[("vcmpC", g)], skey=("vcmpC", g))
            tk.op("dve", lambda e: e.memset(vcmp[:, :, :, 128:129], 1.0), writes=["vcmp1"])
            for (kv, w1_in, w2_in, pe_in, src_d) in ((0, w1_k, w2_k, pe_k, kc_d), (1, w1_v, w2_v, pe_v, vc_d)):
                tk.dma("pool", w1[:, :, :], w1_in.rearrange("(p d) h -> d p h", d=128), writes=["w1c"], skey="w1c")
                tk.dma("pool", w2[:, :, :], w2_in.rearrange("(a h) d -> h a d", h=128), writes=["w2c"], skey="w2c")
                tk.dma("sp", pet[:], pe_in, writes=["pet"], skey="pet")
                tk.op("pe", lambda e: e.transpose(out=PS[0][:, 0:32], in_=pet[:, :], identity=ident_f[:32, :32]),
                      reads=["pet", "ident_f"], writes=[pk(0)])
                tk.op("dve", lambda e: e.tensor_copy(out=peT[:], in_=PS[0][:, 0:32]), reads=[pk(0)], writes=["peT"])
                for hm in range(2):
                    for p in range(32):
                        tk.op("pe", lambda e, hm=hm, p=p: e.matmul(PS[1][:, hm:hm + 1], lhsT=w1[:, p, hm * 128:(hm + 1) * 128], rhs=peT[:, p:p + 1],
                                                                  start=(p == 0), stop=(p == 31)),
                              reads=["w1c", "peT"], writes=[pk(1)])
                tk.op("dve", lambda e: e.tensor_copy(out=hb[:], in_=PS[1][:, 0:2]), reads=[pk(1)], writes=["hb"])
                for g in range(2):
                    srck = [(src_d.tensor.name, g, tt) for tt in range(16)]
                    tk.dma("sp", src[:], src_d[g], reads=srck, writes=["srcc"], skey="srcc")
                    srcv = src[:].rearrange("p (i s) -> p i s", s=16)
                    for hm in range(2):
                        pb = 2 + hm
                        for p in range(32):
                            i0, sft = divmod(p, 16)
                            tk.op("pe", lambda e, hm=hm, p=p, pb=pb, i0=i0, sft=sft: e.matmul(
                                PS[pb][:, 0:511], lhsT=w1[:, p, hm * 128:(hm + 1) * 128], rhs=srcv[:, i0:i0 + 511, sft],
                                start=(p == 0), stop=(p == 31)),
                                reads=["w1c", "srcc"], writes=[pk(pb)])
                        tk.op("act", lambda e, hm=hm, pb=pb: e.activation(out=hx[:, 0:511], in_=PS[pb][:, 0:511], func=AF.Identity,
                                                                         bias=hb[:, hm:hm + 1], scale=1.0),
                              reads=[pk(pb), "hb"], writes=["hx"])
                        tk.op("act", lambda e: e.activation(out=hx2[:, 0:511], in_=hx[:, 0:511], func=AF.Square), reads=["hx"], writes=["hx2"])
                        tk.op("dve", lambda e: e.tensor_scalar(out=hx2[:, 0:511], in0=hx2[:, 0:511], scalar1=0.044715, scalar2=1.0,
                                                               op0=ALU.mult, op1=ALU.add), reads=["hx2"], writes=["hx2"])
                        tk.op("dve", lambda e: e.tensor_tensor(out=hx2[:, 0:511], in0=hx2[:, 0:511], in1=hx[:, 0:511], op=ALU.mult),
                              reads=["hx2", "hx"], writes=["hx2"])
                        tk.op("act", lambda e: e.activation(out=hx3[:, 0:511], in_=hx2[:, 0:511], func=AF.Tanh, scale=0.7978845608028654),
                              reads=["hx2"], writes=["hx3"])
                        tk.op("dve", lambda e: e.tensor_scalar(out=hx3[:, 0:511], in0=hx3[:, 0:511], scalar1=1.0, scalar2=0.5,
                                                               op0=ALU.add, op1=ALU.mult), reads=["hx3"], writes=["hx3"])
                        tk.op("dve", lambda e, hm=hm: e.tensor_tensor(out=gl[:, hm, 0:511], in0=hx3[:, 0:511], in1=hx[:, 0:511], op=ALU.mult),
                              reads=["hx3", "hx"], writes=["gelu"])
                    if kv == 0:
                        for hm in range(2):
                            tk.op("pe", lambda e, hm=hm: e.matmul(PS[4][:], lhsT=w2[:, hm, :], rhs=gl[:, hm, :], start=(hm == 0), stop=(hm == 1)),
                                  reads=["w2c", "gelu"], writes=[pk(4)])
                        tk.op("act", lambda e, g=g: e.activation(out=kcmpT[:, g, :], in_=PS[4][:], func=AF.Copy),
                              reads=[pk(4)], writes=["kcmpT"])
                    else:
                        for c in range(4):
                            for hm in range(2):
                                tk.op("pe", lambda e, hm=hm, c=c: e.matmul(PS[5][:, c * 128:(c + 1) * 128], lhsT=gl[:, hm, c * 128:(c + 1) * 128],
                                                                          rhs=w2[:, hm, :], start=(hm == 0), stop=(hm == 1)),
                                      reads=["w2c", "gelu"], writes=[pk(5)])
                        tk.op("act", lambda e, g=g: e.activation(out=vcmp[:, g, :, 0:128], in_=PS[5][:].rearrange("p (c d) -> p c d", c=4),
                                                                func=AF.Copy),
                              reads=[pk(5)], writes=["vcmp"])
        if True:
            if "dbg_cmp" in debug:
                dk = nc.dram_tensor("dbg_kcmp", [128, 2, 512], BF16, kind="ExternalOutput").ap()
                dv = nc.dram_tensor("dbg_vcmp", [128, 2, 4, 257], BF16, kind="ExternalOutput").ap()
                tk.dma("sp", dk, kcmpT[:], reads=["kcmpT"], writes=["dbgk"], skey="dbgk")
                tk.dma("sp", dv, vcmp[:], reads=["vcmp", ("vcmpC", 0), ("vcmpC", 1), "vcmp1"], writes=["dbgv"], skey="dbgv")
        tk.barrier()
        if stop_after <= 2:
            return _finish(nc, tk, out, es, [])
        def bc4(ap2d):
            return ap2d.rearrange("p (o n) -> p o n", o=1).broadcast_to([128, 4, 128])

        with contextlib.ExitStack() as p3:
            hTo = sb("hTo", [128, KC, NOWN + NHALO], BF16, p3)
            with contextlib.ExitStack() as p3x:
                xraw = sb("xraw3", [128, 2, D], F32, p3x)
                xh = sb("xh3", [128, 2, D], F32, p3x)
                junk = None
                ssq = sb("ssq3", [128, 32], F32, p3x)
                xov = x_own[0:NOWN, :].rearrange("(cb p) d -> cb p d", p=128)
                for tt in range(16):
                    for blk in range(2):
                        tk.dma("sp", xraw[:, blk, :], xov[tt * 2 + blk], writes=[("xraw", blk)], skey=("xraw", blk))
                    norm_transpose(xraw, "xraw", 128, 2, A1, B1, hTo, lambda kc: ("hTo", kc), tt * 256, xh, "xh", junk, ssq, 0)
                tk.dma("sp", xraw[:NHALO, 0, :], x_own[NOWN:NOWN + NHALO, :], writes=[("xraw", 0)], skey=("xraw", 0))
                norm_transpose(xraw, "xraw", NHALO, 1, A1, B1, hTo, lambda kc: ("hTo", kc), NOWN, xh, "xh", junk, ssq, 0)
            tk.barrier()
            wbuf = [sb("wbuf3_%d" % i, [128, KC, 512], BF16, p3) for i in range(2)]
            cq = sb("cq3", [128, 512], F32, p3)
            sq = sb("sq3", [128, 512], F32, p3)
            hs = sb("hs3", [128, 512], F32, p3)
            vv = sb("vv3", [128, 4, 130], F32, p3)
            zt = sb("zt3", [128, 512], F32, p3)
            t1 = sb("t1_3", [128, 512], F32, p3)
            t2 = sb("t2_3", [128, 512], F32, p3)
            yo = [sb("yo3_%d" % i, [128, 512], BF16, p3) for i in range(2)]
            qo = [sb("qo3_%d" % i, [128, 512], BF16, p3) for i in range(2)]
            vhalo = sb("vhalo", [128, 8, NHALO], F32, p3)
            hmask = sb("hmask", [128, NHALO], F32, p3)
            cw = sb("cw3", [128, 8, 3], F32, p3)
            cb_ = sb("cb3", [128, 8], F32, p3)
            wg = sb("wg3", [128, KC, 24], BF16, p3)
            tk.dma("sp", hmask[:], halomask_in, writes=["hmask"], skey="hmask")
            tk.dma("sp", cw[:], convw_fm, writes=["cw"], skey="cw")
            tk.dma("sp", cb_[:], convb_fm, writes=["cb"], skey="cb")
            tk.dma("pool", wg[:], w_g.rearrange("(kc p) n -> p kc n", p=128), writes=["wg"], skey="wg")
            hkeys = [("hTo", kc) for kc in range(KC)]
            wov = w_own.rearrange("(kc p) n -> p kc n", p=128)
            groups = [(ch * 384, 3, "conv", ch) for ch in range(8)] + [(3072 + j * 512, 4, "q", j) for j in range(4)]
            it = 0
            yoc = 0
            qoc = 0
            for gi, (c0, nm, kind, idx) in enumerate(groups):
                s = gi % 2
                tk.dma("pool", wbuf[s][:, :, 0:nm * 128], wov[:, :, c0:c0 + nm * 128], writes=[("wb", s)], skey=("wb", s))
                tiles = ([8] if kind == "conv" else []) + list(range(8))
                for tt in tiles:
                    ncol = NHALO if tt == 8 else 512
                    cs = NOWN if tt == 8 else tt * 512
                    banks = [4 * (it % 2) + m for m in range(nm)]
                    it += 1
                    ms = [1, 2] if tt == 8 else list(range(nm))
                    for m in ms:
                        for k in range(KC):
                            tk.op("pe", lambda e, m=m, k=k, s=s, b=banks[m], cs=cs, ncol=ncol: e.matmul(
                                PS[b][:, 0:ncol], lhsT=wbuf[s][:, k, m * 128:(m + 1) * 128], rhs=hTo[:, k, cs:cs + ncol],
                                start=(k == 0), stop=(k == KC - 1)),
                                reads=[("wb", s), ("hTo", k)], writes=[pk(banks[m])])
                    if kind == "conv":
                        ch = idx
                        bB, bC, bH = banks
                        if tt == 8:
                            tk.op("act", lambda e, bH=bH: e.activation(out=hs[:, 0:NHALO], in_=PS[bH][:, 0:NHALO], func=AF.Copy),
                                  reads=[pk(bH)], writes=["hs"])
                            tk.op("dve", lambda e, bC=bC: e.tensor_tensor(out=t1[:, 0:NHALO], in0=PS[bC][:, 0:NHALO], in1=hs[:, 0:NHALO], op=ALU.mult),
                                  reads=[pk(bC), "hs"], writes=["t1"])
                            tk.op("dve", lambda e, ch=ch: e.tensor_tensor(out=vhalo[:, ch, :], in0=t1[:, 0:NHALO], in1=hmask[:], op=ALU.mult),
                                  reads=["t1", "hmask"], writes=[("vhalo", ch)])
                            continue
                        tk.op("act", lambda e, bH=bH: e.activation(out=hs[:], in_=PS[bH][:], func=AF.Copy), reads=[pk(bH)], writes=["hs"])
                        tk.op("pool", lambda e, ch=ch, tt=tt: e.tensor_copy(out=vv[:, :, 0:2],
                                                                            in_=vhalo[:, ch, tt * 8:(tt + 1) * 8].rearrange("p (a b) -> p a b", b=2)),
                              reads=[("vhalo", ch)], writes=["vvh"])
                        tk.op("dve", lambda e, bC=bC: e.tensor_tensor(out=vv[:, :, 2:130], in0=PS[bC][:].rearrange("p (a b) -> p a b", a=4),
                                                                      in1=hs[:].rearrange("p (a b) -> p a b", a=4), op=ALU.mult),
                              reads=[pk(bC), "hs"], writes=["vv"])
                        ztv = zt[:].rearrange("p (a b) -> p a b", a=4)
                        tk.op("dve", lambda e, ch=ch, ztv=ztv: e.tensor_scalar(out=ztv, in0=vv[:, :, 2:130], scalar1=cw[:, ch, 2:3], scalar2=cb_[:, ch:ch + 1],
                                                                           op0=ALU.mult, op1=ALU.add),
                              reads=["vv", "cw", "cb"], writes=["zt"])
                        tk.op("dve", lambda e, ch=ch, ztv=ztv: e.scalar_tensor_tensor(out=ztv, in0=vv[:, :, 1:129], scalar=cw[:, ch, 1:2], in1=ztv,
                                                                                  op0=ALU.mult, op1=ALU.add),
                              reads=["vv", "vvh", "cw", "zt"], writes=["zt"])
                        tk.op("dve", lambda e, ch=ch, ztv=ztv: e.scalar_tensor_tensor(out=ztv, in0=vv[:, :, 0:128], scalar=cw[:, ch, 0:1], in1=ztv,
                                                                                  op0=ALU.mult, op1=ALU.add),
                              reads=["vv", "vvh", "cw", "zt"], writes=["zt"])
                        ys = yoc % 2
                        yoc += 1
                        tk.op("dve", lambda e, bB=bB, ys=ys: e.tensor_tensor(out=yo[ys][:], in0=PS[bB][:], in1=zt[:], op=ALU.mult),
                              reads=[pk(bB), "zt"], writes=[("yo", ys)])
                        tk.dma("act", yc_d[ch, :, tt * 512:(tt + 1) * 512], yo[ys][:], reads=[("yo", ys)], writes=[], skey=("yo", ys))
                    else:
                        tk.dma("sp", cq[:], cosq_own[:, tt * 512:(tt + 1) * 512], writes=["cq"], skey="cq")
                        tk.dma("sp", sq[:], sinq_own[:, tt * 512:(tt + 1) * 512], writes=["sq"], skey="sq")
                        for hh in range(2):
                            head = idx * 2 + hh
                            bq, bs = banks[2 * hh], banks[2 * hh + 1]
                            tk.op("dve", lambda e, bq=bq: e.tensor_tensor(out=t1[:], in0=PS[bq][:], in1=cq[:], op=ALU.mult),
                                  reads=[pk(bq), "cq"], writes=["t1"])
                            tk.op("dve", lambda e, bs=bs: e.tensor_tensor(out=t2[:], in0=PS[bs][:], in1=sq[:], op=ALU.mult),
                                  reads=[pk(bs), "sq"], writes=["t2"])
                            qs_ = qoc % 2
                            qoc += 1
                            tk.op("pool", lambda e, qs_=qs_: e.tensor_tensor(out=qo[qs_][:], in0=t1[:], in1=t2[:], op=ALU.add),
                                  reads=["t1", "t2"], writes=[("qo", qs_)])
                            tk.dma("act", q_d[tt * 4:(tt + 1) * 4, :, head, :].rearrange("b p q -> p b q"),
                                   qo[qs_][:].rearrange("p (b q) -> p b q", b=4), reads=[("qo", qs_)], writes=[], skey=("qo", qs_))
            for blk in range(NOB):
                b = 4 * (it % 2)
                it += 1
                for k in range(KC):
                    tk.op("pe", lambda e, k=k, b=b, blk=blk: e.matmul(PS[b][:, 0:24], lhsT=hTo[:, k, blk * 128:(blk + 1) * 128], rhs=wg[:, k, :],
                                                                     start=(k == 0), stop=(k == KC - 1)),
                          reads=["wg", ("hTo", k)], writes=[pk(b)])
                tk.op("act", lambda e, b=b, blk=blk: e.activation(out=gates_sb[:, blk, :], in_=PS[b][:, 0:24], func=AF.Sigmoid),
                      reads=[pk(b)], writes=["gates"])
        tk.barrier()
        if stop_after <= 3:
            return _finish(nc, tk, out, es, [])

        with contextlib.ExitStack() as p4:
            ksl = sb("ksl", [128, S], BF16, p4)
            vsl = sb("vsl", [128, 64, 129], BF16, p4)
            Et = sb("Et", [128, S], BF16, p4)
            winb = sb("winb", [128, 6, 128], BF16, p4)
            diagb = sb("diagb", [128, 2, 128], BF16, p4)
            qT = [sb("qT%d" % i, [128, 512], BF16, p4) for i in range(2)]
            kw = [sb("kw%d" % i, [128, 768], BF16, p4) for i in range(2)]
            vw = [sb("vw%d" % i, [128, 6, 129], BF16, p4) for i in range(2)]
            cmpb = [sb("cmpb%d" % i, [128, 2, 128], BF16, p4) for i in range(2)]
            vmt = [sb("vmt%d" % i, [128, 128], F32, p4) for i in range(2)]
            bvt = [sb("bvt%d" % i, [128, 128], F32, p4) for i in range(2)]
            pT = [sb("pT%d" % i, [128, 512], BF16, p4) for i in range(3)]
            ya = [sb("ya%d" % i, [128, 512], F32, p4) for i in range(2)]
            imp = sb("imp", [128, 128], F32, p4)
            score = sb("score", [128, 128], F32, p4)
            wk = sb("wk", [128, 128], F32, p4)
            m8 = sb("m8", [128, 16], F32, p4)
            selb = sb("selb", [128, 128], F32, p4)
            selbT = sb("selbT", [128, 128], BF16, p4)
            rs = sb("rs", [128, 16], F32, p4)
            tmpo = sb("tmpo", [128, 2, 256], F32, p4)
            tmpC = sb("tmpC", [128, 4, 127], F32, p4)
            obset = [0]
            tk.op("dve", lambda e: e.memset(imp[:], 0.0), writes=["imp"])
            tk.dma("sp", Et[:], E_in, writes=["Et"], skey="Et")
            tk.dma("sp", winb[:], winbias_in, writes=["winb"], skey="winb")
            tk.dma("sp", diagb[:], diagbias_in, writes=["diagb"], skey="diagb")
            pcount = [0]
            for g in range(2):
                tk.dma("sp", ksl[:], kslc_d[g], writes=["ksl"], skey="ksl")
                tk.dma("sp", vsl[:], vslc_d[g], writes=["vsl"], skey="vsl")
                for n in range(NOB):
                    s = n % 2
                    lo = max(0, 2 * n - 4)
                    w0 = lo - (2 * n - 4)
                    tk.dma("sp", qT[s][:].rearrange("p (h q) -> p h q", h=4), q_d[n, :, g * 4:(g + 1) * 4, :], writes=[("qT", s)], skey=("qT", s))
                    tk.dma("sp", kw[s][:, w0 * 128:768], kwin_d[g, :, lo * 128:(2 * n + 2) * 128], writes=[("kw", s)], skey=("kw", s))
                    tk.dma("sp", vw[s][:, w0:6, :], vwin_d[g, :, lo:2 * n + 2, :], writes=[("vw", s)], skey=("vw", s))
                    tk.dma("sp", cmpb[s][:], cmpbias_in[n], writes=[("cmpb", s)], skey=("cmpb", s))
                    tk.dma("sp", vmt[s][:], vm_in[n], writes=[("vmt", s)], skey=("vmt", s))
                    tk.dma("sp", bvt[s][:], bv_in[n], writes=[("bvt", s)], skey=("bvt", s))

                    def branch(chunks, br, ncolO, yas, first):
                        cmpmode = (ncolO == 257)
                        W = 256 if cmpmode else 129
                        ob = (2, 3) if obset[0] % 2 == 0 else (4, 5)
                        obset[0] += 1
                        nch = len(chunks)
                        slots = []

                        def qk(ci):
                            (kT, kkeys, biases, vap, vkeys) = chunks[ci]
                            sbk = pcount[0] % 2
                            ps_ = pcount[0] % 3
                            pcount[0] += 1
                            slots.append(ps_)
                            mms = [(kT, qT[s][:], PS[sbk][:], kkeys + [("qT", s)])]
                            for (bl, br_, bkeys) in biases:
                                mms.append((bl, bc4(br_), PS[sbk][:].rearrange("p (o n) -> p o n", o=4), bkeys))
                            for mi, (l_, r_, o_, keys_) in enumerate(mms):
                                tk.op("pe", lambda e, l_=l_, r_=r_, o_=o_, mi=mi, nm_=len(mms): e.matmul(o_, lhsT=l_, rhs=r_, start=(mi == 0), stop=(mi == nm_ - 1)),
                                      reads=keys_, writes=[pk(sbk)])
                            tk.op("act", lambda e, sbk=sbk, ps_=ps_: e.activation(out=pT[ps_][:], in_=PS[sbk][:], func=AF.Exp),
                                  reads=[pk(sbk)], writes=[("pT", ps_)])

                        def pv(ci):
                            (kT, kkeys, biases, vap, vkeys) = chunks[ci]
                            ps_ = slots[ci]
                            for r in range(4):
                                b_ = ob[r // 2]
                                c0 = (r % 2) * W
                                tk.op("pe", lambda e, r=r, ps_=ps_, vap=vap, ci=ci, b_=b_, c0=c0: e.matmul(
                                    PS[b_][:, c0:c0 + W], lhsT=pT[ps_][:, r * 128:(r + 1) * 128], rhs=vap[:, 0:W],
                                    start=(ci == 0 and r % 2 == 0), stop=(ci == nch - 1), skip_group_check=True),
                                    reads=[("pT", ps_)] + vkeys, writes=[pk(b_)])

                        qk(0)
                        for ci in range(nch):
                            if ci + 1 < nch:
                                qk(ci + 1)
                            pv(ci)

                        def hv(b_):
                            return PS[b_][:, 0:2 * W].rearrange("p (h c) -> p h c", c=W)

                        for bi, b_ in enumerate(ob):
                            tk.op("dve", lambda e, bi=bi, b_=b_: e.tensor_scalar(out=rs[:, 2 * bi:2 * bi + 2], in0=hv(b_)[:, :, 128], scalar1=1e-30, scalar2=None, op0=ALU.max),
                                  reads=[pk(b_)], writes=[("rs", bi)])
                        tk.op("dve", lambda e: e.reciprocal(out=rs[:, 4:8], in_=rs[:, 0:4]), reads=[("rs", 0), ("rs", 1)], writes=["rinv"])
                        gview = gates_sb[:, n, g * 12:(g + 1) * 12].rearrange("p (h b) -> p h b", b=3)[:, :, br]
                        tk.op("dve", lambda e: e.tensor_tensor(out=rs[:, 8:12], in0=rs[:, 4:8], in1=gview, op=ALU.mult), reads=["rinv", "gates"], writes=["rg"])
                        for bi, b_ in enumerate(ob):
                            src = hv(b_)[:, :, 0:128]
                            scb = rs[:, 8 + 2 * bi:10 + 2 * bi].rearrange("p (h o) -> p h o", o=1).broadcast_to([128, 2, 128])
                            dst = ya[yas][:, bi * 256:(bi + 1) * 256].rearrange("p (h c) -> p h c", c=128)
                            if first:
                                tk.op("dve", lambda e, src=src, scb=scb, dst=dst: e.tensor_tensor(out=dst, in0=src, in1=scb, op=ALU.mult),
                                      reads=[pk(b_), "rg"], writes=[("ya", yas)])
                                srcC = hv(b_)[:, :, 129:256]
                                rb = rs[:, 4 + 2 * bi:6 + 2 * bi].rearrange("p (h o) -> p h o", o=1).broadcast_to([128, 2, 127])
                                tk.op("dve", lambda e, srcC=srcC, rb=rb, bi=bi: e.tensor_tensor(out=tmpC[:, 2 * bi:2 * bi + 2, :], in0=srcC, in1=rb, op=ALU.mult),
                                      reads=[pk(b_), "rinv"], writes=[("tmpC", bi)])
                            else:
                                tv = tmpo[:, bi, :].rearrange("p (h c) -> p h c", c=128)
                                tk.op("dve", lambda e, src=src, scb=scb, tv=tv: e.tensor_tensor(out=tv, in0=src, in1=scb, op=ALU.mult),
                                      reads=[pk(b_), "rg"], writes=[("tmpo", bi)])
                                tk.op("pool", lambda e, dst=dst, tv=tv: e.tensor_tensor(out=dst, in0=dst, in1=tv, op=ALU.add),
                                      reads=[("tmpo", bi), ("ya", yas)], writes=[("ya", yas)])
                        if first:
                            tk.op("pool", lambda e: e.tensor_tensor(out=tmpC[:, 0:2, :], in0=tmpC[:, 0:2, :], in1=tmpC[:, 2:4, :], op=ALU.add),
                                  reads=[("tmpC", 0), ("tmpC", 1)], writes=[("tmpC", 0)])
                            tk.op("pool", lambda e: e.tensor_tensor(out=imp[:, 0:127], in0=tmpC[:, 0, :], in1=tmpC[:, 1, :], op=ALU.add),
                                  reads=[("tmpC", 0)], writes=["imp"])

                    yas = n % 2
                    C = n // 8 + 1
                    chunks = []
                    for c in range(C):
                        j = c - (C - 2)
                        biases = [(ident_b[:], cmpb[s][:, j, :], ["ident_b", ("cmpb", s)])] if j >= 0 else []
                        chunks.append((kcmpT[:, g, c * 128:(c + 1) * 128], ["kcmpT"], biases, vcmp[:, g, c, :], ["vcmp", ("vcmpC", g), "vcmp1"]))
                    branch(chunks, 0, 257, yas, True)
                    tk.op("dve", lambda e: e.tensor_tensor(out=score[:], in0=imp[:], in1=vmt[s][:], op=ALU.mult), reads=["imp", ("vmt", s)], writes=["score"])
                    tk.op("dve", lambda e: e.tensor_tensor(out=score[:], in0=score[:], in1=bvt[s][:], op=ALU.add), reads=["score", ("bvt", s)], writes=["score"])
                    tk.op("dve", lambda e: e.max(out=m8[:, 0:8], in_=score[:]), reads=["score"], writes=["m8a"])
                    tk.op("dve", lambda e: e.match_replace(out=wk[:], in_to_replace=m8[:, 0:8], in_values=score[:], imm_value=-1e30),
                          reads=["score", "m8a"], writes=["wk"])
                    tk.op("dve", lambda e: e.max(out=m8[:, 8:16], in_=wk[:]), reads=["wk"], writes=["m8b"])
                    tk.op("dve", lambda e: e.tensor_scalar(out=wk[:], in0=score[:], scalar1=m8[:, 15:16], scalar2=None, op0=ALU.is_ge),
                          reads=["score", "m8b", "wk"], writes=["wk"])
                    tk.op("dve", lambda e: e.tensor_scalar(out=selb[:], in0=wk[:], scalar1=-1.0, scalar2=-NEG, op0=ALU.add, op1=ALU.mult),
                          reads=["wk"], writes=["selb"])
                    chunks = []
                    for w in range(w0, 6):
                        chunks.append((kw[s][:, w * 128:(w + 1) * 128], [("kw", s)], [(ident_b[:], winb[:, w, :], ["ident_b", "winb"])],
                                       vw[s][:, w, :], [("vw", s)]))
                    branch(chunks, 2, 129, yas, False)
                    tk.op("pe", lambda e: e.transpose(out=PS[6][:, 0:128], in_=selb[:], identity=ident_f[:]), reads=["selb", "ident_f"], writes=[pk(6)])
                    tk.op("act", lambda e: e.activation(out=selbT[:], in_=PS[6][:, 0:128], func=AF.Copy), reads=[pk(6)], writes=["selbT"])
                    chunks = []
                    for c in range(2 * n + 2):
                        biases = [(Et[:, c * 128:(c + 1) * 128], selbT[:], ["Et", "selbT"])]
                        if c >= 2 * n:
                            biases.append((ident_b[:], diagb[:, c - 2 * n, :], ["ident_b", "diagb"]))
                        chunks.append((ksl[:, c * 128:(c + 1) * 128], ["ksl"], biases, vsl[:, c, :], ["vsl"]))
                    branch(chunks, 1, 129, yas, False)
                    tk.dma("act", ya_d[n, :, g * 512:(g + 1) * 512], ya[yas][:], reads=[("ya", yas)], writes=[], skey=("ya", yas))
        tk.barrier()
        if stop_after <= 4:
            return _finish(nc, tk, out, es, [])

        with contextlib.ExitStack() as p5:
            wout = sb("wout", [128, KC, D], BF16, p5)
            gcg = sb("gcg", [128, 8], F32, p5)
            gag = sb("gag", [128, 8], F32, p5)
            yat = [sb("yat%d" % i, [128, 1024], F32, p5) for i in range(2)]
            yct = [sb("yct%d" % i, [128, 8, 128], BF16, p5) for i in range(2)]
            xt = [sb("xt5_%d" % i, [128, D], F32, p5) for i in range(2)]
            x1 = [sb("x1t%d" % i, [128, 1, D], F32, p5) for i in range(2)]
            xh = sb("xh5", [128, 1, D], F32, p5)
            junk = sb("junk5", [128, D], F32, p5)
            ssq = sb("ssq5", [128, 32], F32, p5)
            st = sb("st5", [128, 16], F32, p5)
            sqc = [sb("sqc%d" % i, [128, 8, 128], BF16, p5) for i in range(2)]
            ycg = [sb("ycg%d" % i, [128, 8, 128], BF16, p5) for i in range(2)]
            yaT = [sb("yaT%d" % i, [128, 8, 128], BF16, p5) for i in range(2)]
            tA = [sb("tA%d" % i, [128, 512], F32, p5) for i in range(2)]
            tB = [sb("tB%d" % i, [128, 512], F32, p5) for i in range(2)]
            h2T = [sb("h2T%d" % i, [128, KC, 128], BF16, p5) for i in range(2)]
            wouv = w_out.rearrange("(kc p) n -> p kc n", p=128)
            for j in range(0, D, 512):
                tk.dma("pool", wout[:, :, j:j + 512], wouv[:, :, j:j + 512], writes=[("wout", j)], skey=("wout", j))
            wout_keys = [("wout", j) for j in range(0, D, 512)]
            tk.dma("sp", gcg[:], gcg_fm, writes=["gcg"], skey="gcg")
            tk.dma("sp", gag[:], gag_fm, writes=["gag"], skey="gag")

            def rstd_chain(src_col, dst_col, nfeat):
                tk.op("dve", lambda e: e.tensor_scalar(out=st[:, dst_col:dst_col + 1], in0=src_col, scalar1=1.0 / nfeat, scalar2=EPS,
                                                       op0=ALU.mult, op1=ALU.add), reads=["st_in"], writes=["st_a"])
                tk.op("act", lambda e: e.activation(out=st[:, dst_col + 1:dst_col + 2], in_=st[:, dst_col:dst_col + 1], func=AF.Sqrt),
                      reads=["st_a"], writes=["st_b"])
                tk.op("dve", lambda e: e.reciprocal(out=st[:, dst_col + 2:dst_col + 3], in_=st[:, dst_col + 1:dst_col + 2]),
                      reads=["st_b"], writes=[("st_r", dst_col)])
                return st[:, dst_col + 2:dst_col + 3]

            bkc = [0]

            def front(n):
                s = n % 2
                tk.dma("sp", yat[s][:], ya_d[n], writes=[("yat", s)], skey=("yat", s))
                tk.dma("sp", yct[s][:], yc_d[:, :, n * 128:(n + 1) * 128].rearrange("c p t -> p c t"), writes=[("yct", s)], skey=("yct", s))
                tk.dma("sp", xt[s][:], x_own[n * 128:(n + 1) * 128, :], writes=[("xt", s)], skey=("xt", s))
                tk.op("act", lambda e: e.activation(out=junk[:, 0:1024], in_=yat[s][:], func=AF.Square, accum_out=st[:, 0:1]),
                      reads=[("yat", s)], writes=["junk", "st_in"])
                rstd_a = rstd_chain(st[:, 0:1], 1 + 8 * s, 1024)
                for cc in range(8):
                    pb = (bkc[0] // 4) % 2
                    tk.op("pe", lambda e, cc=cc, pb=pb: e.transpose(out=PS[pb][:, (cc % 4) * 128:(cc % 4 + 1) * 128], in_=yat[s][:, cc * 128:(cc + 1) * 128],
                                                                   identity=ident_f[:]), reads=[("yat", s), "ident_f"], writes=[pk(pb)])
                    tk.op("dve", lambda e, cc=cc, pb=pb: e.tensor_scalar(out=yaT[s][:, cc, :], in0=PS[pb][:, (cc % 4) * 128:(cc % 4 + 1) * 128],
                                                                        scalar1=gag[:, cc:cc + 1], scalar2=None, op0=ALU.mult),
                          reads=[pk(pb), "gag"], writes=[("yaT", s)])
                    bkc[0] += 1
                tk.op("act", lambda e: e.activation(out=sqc[s][:], in_=yct[s][:], func=AF.Square), reads=[("yct", s)], writes=[("sqc", s)])
                for cc in range(8):
                    tk.op("pe", lambda e, cc=cc: e.matmul(PS[2][:, 0:1], lhsT=sqc[s][:, cc, :], rhs=ones_b[:, 0:1], start=(cc == 0), stop=(cc == 7)),
                          reads=[("sqc", s), "ones_b"], writes=[pk(2)])
                tk.op("dve", lambda e: e.tensor_copy(out=st[:, 4:5], in_=PS[2][:, 0:1]), reads=[pk(2), ("st_r", 1 + 8 * s)], writes=["st_in"])
                rstd_c = rstd_chain(st[:, 4:5], 5 + 8 * s, 1024)
                for cc in range(8):
                    tk.op("act", lambda e, cc=cc: e.activation(out=ycg[s][:, cc, :], in_=yct[s][:, cc, :], func=AF.Copy, scale=gcg[:, cc:cc + 1]),
                          reads=[("yct", s), "gcg"], writes=[("ycg", s)])
                for dc in range(4):
                    d0 = dc * 512
                    pc_, pa_ = 3 + 2 * (dc % 2), 4 + 2 * (dc % 2)
                    ts_ = dc % 2
                    for cc in range(8):
                        tk.op("pe", lambda e, cc=cc, d0=d0, pc_=pc_: e.matmul(PS[pc_][:], lhsT=ycg[s][:, cc, :], rhs=wout[:, cc, d0:d0 + 512], start=(cc == 0), stop=(cc == 7)),
                              reads=[("ycg", s)] + wout_keys, writes=[pk(pc_)])
                    for cc in range(8):
                        tk.op("pe", lambda e, cc=cc, d0=d0, pa_=pa_: e.matmul(PS[pa_][:], lhsT=yaT[s][:, cc, :], rhs=wout[:, 8 + cc, d0:d0 + 512], start=(cc == 0), stop=(cc == 7)),
                              reads=[("yaT", s)] + wout_keys, writes=[pk(pa_)])
                    tk.op("act", lambda e, pc_=pc_, ts_=ts_: e.activation(out=tA[ts_][:], in_=PS[pc_][:], func=AF.Copy, scale=rstd_c),
                          reads=[pk(pc_), ("st_r", 5 + 8 * s)], writes=[("tA", ts_)])
                    tk.op("dve", lambda e, pa_=pa_, ts_=ts_: e.scalar_tensor_tensor(out=tB[ts_][:], in0=PS[pa_][:], scalar=rstd_a, in1=tA[ts_][:], op0=ALU.mult, op1=ALU.add),
                          reads=[pk(pa_), ("st_r", 1 + 8 * s), ("tA", ts_)], writes=[("tB", ts_)])
                    tk.op("dve", lambda e, d0=d0, ts_=ts_: e.tensor_tensor(out=tB[ts_][:], in0=tB[ts_][:], in1=G1row[:, d0:d0 + 512], op=ALU.mult),
                          reads=[("tB", ts_), "G1row"], writes=[("tB", ts_)])
                    tk.op("pool", lambda e, d0=d0, ts_=ts_: e.tensor_tensor(out=x1[s][:, 0, d0:d0 + 512], in0=tB[ts_][:], in1=xt[s][:, d0:d0 + 512], op=ALU.add),
                          reads=[("tB", ts_), ("xt", s)], writes=[(("x1", s), 0)])

            def back(n):
                s = n % 2
                tk.dma("act", x1_d[n * 128:(n + 1) * 128, :], x1[s][:, 0, :], reads=[(("x1", s), 0)], writes=[], skey=("x1", s))
                norm_transpose(x1[s], ("x1", s), 128, 1, A2, B2, h2T[s], lambda kc, s=s: ("h2T", s), 0, xh, "xh", junk, ssq, 0)
                tk.dma("act", h2_d[:, :, n * 128:(n + 1) * 128].rearrange("k p t -> p k t"), h2T[s][:], reads=[("h2T", s)], writes=[], skey=("h2T", s))

            for n in range(NOB + 1):
                if n < NOB:
                    front(n)
                if n >= 1:
                    back(n - 1)
        tk.barrier()
        if stop_after <= 5:
            return _finish(nc, tk, out, es, [])

        with contextlib.ExitStack() as p6:
            h2 = sb("h2", [128, KC, 512], BF16, p6)
            aT = sb("aT", [128, 64, 512], BF16, p6)
            w1b = [sb("w1b%d" % i, [128, KC, 512], BF16, p6) for i in range(2)]
            w2b = [sb("w2b%d" % i, [128, 16, 512], BF16, p6) for i in range(2)]
            rl = [sb("rl%d" % i, [128, 512], F32, p6) for i in range(2)]
            ytg = sb("ytg", [128, 4, 512], F32, p6)
            yo2 = [sb("yo2_%d" % i, [128, 512], F32, p6) for i in range(2)]
            w1v = w_ff1.rearrange("(kc p) n -> p kc n", p=128)
            w2v = w_ff2.rearrange("(fc p) n -> p fc n", p=128)
            cnt1 = 0
            cnt2 = 0
            cnt3 = 0
            for tt in range(8):
                tk.dma("sp", h2[:], h2_d[:, :, tt * 512:(tt + 1) * 512].rearrange("k p t -> p k t"), writes=["h2"], skey="h2")
                for fg in range(16):
                    s = cnt1 % 2
                    tk.dma("pool", w1b[s][:], w1v[:, :, fg * 512:(fg + 1) * 512], writes=[("w1b", s)], skey=("w1b", s))
                    for m in range(4):
                        f = fg * 4 + m
                        pb = cnt1 * 4 % 4 + m
                        for k in range(KC):
                            tk.op("pe", lambda e, k=k, m=m, s=s, pb=pb: e.matmul(PS[pb][:], lhsT=w1b[s][:, k, m * 128:(m + 1) * 128], rhs=h2[:, k, :],
                                                                               start=(k == 0), stop=(k == KC - 1)),
                                  reads=[("w1b", s), "h2"], writes=[pk(pb)])
                        rs_ = f % 2
                        tk.op("act", lambda e, pb=pb, rs_=rs_: e.activation(out=rl[rs_][:], in_=PS[pb][:], func=AF.Relu), reads=[pk(pb)], writes=[("rl", rs_)])
                        tk.op("dve", lambda e, f=f, rs_=rs_: e.tensor_tensor(out=aT[:, f, :], in0=rl[rs_][:], in1=rl[rs_][:], op=ALU.mult),
                              reads=[("rl", rs_)], writes=[("aT", f)])
                    cnt1 += 1
                for dg in range(4):
                    for fq in range(4):
                        s = cnt2 % 2
                        cnt2 += 1
                        tk.dma("pool", w2b[s][:], w2v[:, fq * 16:(fq + 1) * 16, dg * 512:(dg + 1) * 512], writes=[("w2b", s)], skey=("w2b", s))
                        for dc in range(4):
                            for fi in range(16):
                                f = fq * 16 + fi
                                tk.op("pe", lambda e, dc=dc, fi=fi, f=f, s=s: e.matmul(PS[4 + dc][:], lhsT=w2b[s][:, fi, dc * 128:(dc + 1) * 128], rhs=aT[:, f, :],
                                                                                     start=(f == 0), stop=(f == 63)),
                                      reads=[("w2b", s), ("aT", f)], writes=[pk(4 + dc)])
                    for dc in range(4):
                        tk.op("dve", lambda e, dc=dc, dg=dg: e.tensor_scalar(out=ytg[:, dc, :], in0=PS[4 + dc][:], scalar1=G2[:, dg * 4 + dc:dg * 4 + dc + 1],
                                                                           scalar2=None, op0=ALU.mult),
                              reads=[pk(4 + dc), "G2"], writes=[("ytg", dc)])
                    for blk in range(4):
                        pb = cnt3 % 4
                        ys = cnt3 % 2
                        cnt3 += 1
                        for dc in range(4):
                            tk.op("pe", lambda e, dc=dc, blk=blk, pb=pb: e.transpose(out=PS[pb][:, dc * 128:(dc + 1) * 128], in_=ytg[:, dc, blk * 128:(blk + 1) * 128],
                                                                                   identity=ident_f[:]),
                                  reads=[("ytg", dc), "ident_f"], writes=[pk(pb)])
                        tk.op("act", lambda e, pb=pb, ys=ys: e.activation(out=yo2[ys][:], in_=PS[pb][:], func=AF.Copy), reads=[pk(pb)], writes=[("yo2", ys)])
                        r0 = (tt * 4 + blk) * 128
                        tk.dma("act", y2_d[r0:r0 + 128, dg * 512:(dg + 1) * 512], yo2[ys][:], reads=[("yo2", ys)], writes=[], skey=("yo2", ys))
        tk.barrier()
        if stop_after <= 6:
            return _finish(nc, tk, out, es, [])

        with contextlib.ExitStack() as p7:
            xa = [sb("xa%d" % i, [128, D], F32, p7) for i in range(3)]
            yb = [sb("yb%d" % i, [128, D], F32, p7) for i in range(3)]
            ot = [sb("ot%d" % i, [128, D], F32, p7) for i in range(3)]
            nfg = sb("nfg", [128, D], F32, p7)
            junk = sb("junk7", [128, D], F32, p7)
            st = sb("st7", [128, 8], F32, p7)
            tk.dma("sp", nfg[:], nfg_row, writes=["nfg"], skey="nfg")
            for n in range(NOB):
                s = n % 3
                tk.dma("sp", xa[s][:], x1_d[n * 128:(n + 1) * 128, :], writes=[("xa", s)], skey=("xa", s))
                tk.dma("sp", yb[s][:], y2_d[n * 128:(n + 1) * 128, :], writes=[("yb", s)], skey=("yb", s))
                tk.op("dve", lambda e: e.tensor_tensor(out=xa[s][:], in0=xa[s][:], in1=yb[s][:], op=ALU.add), reads=[("xa", s), ("yb", s)], writes=[("xa", s)])
                tk.op("act", lambda e: e.activation(out=junk[:], in_=xa[s][:], func=AF.Square, accum_out=st[:, 0:1]), reads=[("xa", s)], writes=["junk", "st0"])
                tk.op("dve", lambda e: e.tensor_scalar(out=st[:, 1:2], in0=st[:, 0:1], scalar1=1.0 / D, scalar2=EPS, op0=ALU.mult, op1=ALU.add),
                      reads=["st0"], writes=["st1"])
                tk.op("act", lambda e: e.activation(out=st[:, 2:3], in_=st[:, 1:2], func=AF.Sqrt), reads=["st1"], writes=["st2"])
                tk.op("dve", lambda e: e.reciprocal(out=st[:, 3:4], in_=st[:, 2:3]), reads=["st2"], writes=["st3"])
                tk.op("act", lambda e: e.activation(out=ot[s][:], in_=xa[s][:], func=AF.Copy, scale=st[:, 3:4]), reads=[("xa", s), "st3"], writes=[("ot", s)])
                tk.op("dve", lambda e: e.tensor_tensor(out=ot[s][:], in0=ot[s][:], in1=nfg[:], op=ALU.mult), reads=[("ot", s), "nfg"], writes=[("ot", s)])
                tk.dma("act", out[n * 128:(n + 1) * 128, :], ot[s][:], reads=[("ot", s)], writes=[], skey=("ot", s))
        return _finish(nc, tk, out, es, [])


def _finish(nc, tk, out, es, keys):
    tk.final_wait("sp", keys)
    for e in tk.eng.values():
        if e["name"] != "sp" and e["cnt"] > 0:
            tk.eng["sp"]["h"].wait_ge(e["sem"], e["cnt"])
    for s in list(tk.dsem.values()) + tk.free_sems:
        if s["cnt"] > 0:
            tk.eng["sp"]["h"].wait_ge(s["sem"], s["cnt"])
    return nc


def _const_tables(hf):
    inv = (10000.0 ** (-np.arange(0, 128, 2, dtype=np.float32) / 128)).astype(np.float32)
    ang = np.arange(S, dtype=np.float32)[:, None] * inv[None, :]
    cos = np.cos(ang).astype(np.float32).T
    sin = np.sin(ang).astype(np.float32).T
    cosT = np.concatenate([cos, cos], 0)
    sinT = np.concatenate([-sin, sin], 0)
    own_pos = np.concatenate([np.arange(128) + 128 * (2 * n + hf) for n in range(NOB)])
    sc = np.float32(128 ** -0.5)
    t = {}
    t["cos_ctx"] = np.ascontiguousarray(cosT)
    t["sin_ctx"] = np.ascontiguousarray(sinT)
    t["cosq_own"] = np.ascontiguousarray(cosT[:, own_pos] * sc)
    t["sinq_own"] = np.ascontiguousarray(sinT[:, own_pos] * sc)
    key = np.arange(S)
    t["E_in"] = (key[None, :] // 64 == np.arange(128)[:, None]).astype(np.float32).astype(BF)
    t["ident_f_in"] = np.eye(128, dtype=np.float32)
    t["ident_b_in"] = np.eye(128, dtype=np.float32).astype(BF)
    ci = np.arange(512)[:, None]
    sj = np.arange(128)[None, :]
    C = ((ci * 16 <= sj * 64 + 63) & (ci * 16 + 31 >= sj * 64) & (ci < 511)).astype(np.float32)
    t["C_in"] = np.ascontiguousarray(C.reshape(4, 128, 128).transpose(1, 0, 2)).astype(BF)
    kl = np.arange(128)[:, None]
    ql = np.arange(128)[None, :]
    cmpb = np.zeros((NOB, 128, 2, 128), np.float32)
    vm = np.zeros((NOB, 128, 128), np.float32)
    bv = np.zeros((NOB, 128, 128), np.float32)
    for n in range(NOB):
        tq = 128 * (2 * n + hf) + ql
        for j in range(2):
            c = n // 8 - 1 + j
            i = 128 * c + kl
            valid = (c >= 0) & (16 * i + 31 <= tq) & (i <= 510)
            cmpb[n, :, j, :] = np.where(valid, 0.0, NEG)
        tcol = (128 * (2 * n + hf) + np.arange(128))[:, None]
        jb = np.arange(128)[None, :]
        valid = (jb * 64 <= tcol)
        cur = tcol // 64
        forced = (jb == 0) | (jb == cur) | (jb == cur - 1)
        vmn = valid.astype(np.float32)
        vm[n] = vmn
        bv[n] = vmn * 1e4 * forced + (vmn - 1.0)
    t["cmpbias_in"] = cmpb.astype(BF)
    t["vm_in"] = vm
    t["bv_in"] = bv
    wb = np.zeros((128, 6, 128), np.float32)
    for w in range(6):
        off = 128 * (w - 4 - hf)
        valid = (off + kl <= ql) & (off + kl > ql - 512)
        wb[:, w, :] = np.where(valid, 0.0, NEG)
    t["winbias_in"] = wb.astype(BF)
    db = np.zeros((128, 2, 128), np.float32)
    for j in range(2):
        valid = (128 * (j - hf) + kl <= ql)
        db[:, j, :] = np.where(valid, 0.0, NEG)
    t["diagbias_in"] = db.astype(BF)
    hm = np.ones((128, NHALO), np.float32)
    if hf == 0:
        hm[:, 0:2] = 0.0
    t["halomask_in"] = hm
    return t


def _swap_halves(cols):
    c = cols.reshape(-1, 2, 64)
    return c[:, ::-1, :].reshape(-1)


def prep_inputs(inp):
    f = lambda a: np.ascontiguousarray(np.asarray(a, dtype=np.float32))
    x = f(inp["x"])
    w_in = f(inp["w_in"])[0]
    cu = np.arange(5656)
    ub, uc, uh, q = cu[0:1024], cu[1024:2048], cu[2048:3072], cu[3072:4096]
    kc, vc, ksl, vsl, kwn, vwn, gl = (cu[4096:4352], cu[4352:4608], cu[4608:4864], cu[4864:5120],
                                      cu[5120:5376], cu[5376:5632], cu[5632:5656])
    kv_cols = np.concatenate([kc, _swap_halves(kc), ksl, _swap_halves(ksl), kwn, _swap_halves(kwn), vc, vsl, vwn])
    own_cols = []
    for ch in range(8):
        own_cols += [ub[ch * 128:(ch + 1) * 128], uc[ch * 128:(ch + 1) * 128], uh[ch * 128:(ch + 1) * 128]]
    for h in range(8):
        qh = q[h * 128:(h + 1) * 128]
        own_cols += [qh, _swap_halves(qh)]
    own_cols = np.concatenate(own_cols)
    fm = lambda v, n: np.ascontiguousarray(f(v).reshape(n, 128).T)
    shared = {
        "w_ada": f(inp["w_ada"])[0],
        "b_ada_fm": fm(inp["b_ada"][0], 96),
        "n1g_fm": fm(inp["norm1_g"][0], 16),
        "n2g_fm": fm(inp["norm2_g"][0], 16),
        "w_kv": np.ascontiguousarray(w_in[:, kv_cols]),
        "w_own": np.ascontiguousarray(w_in[:, own_cols]),
        "w_g": np.ascontiguousarray(w_in[:, gl]),
        "convw_fm": np.ascontiguousarray(f(inp["conv_w"])[0].reshape(3, 8, 128).transpose(2, 1, 0)),
        "convb_fm": fm(inp["conv_b"][0], 8),
        "pe_k": f(inp["cmp_pe_k"])[0], "pe_v": f(inp["cmp_pe_v"])[0],
        "w1_k": f(inp["cmp_w1_k"])[0], "w1_v": f(inp["cmp_w1_v"])[0],
        "w2_k": f(inp["cmp_w2_k"])[0], "w2_v": f(inp["cmp_w2_v"])[0],
        "gcg_fm": fm(inp["gnorm_conv_g"][0], 8), "gag_fm": fm(inp["gnorm_attn_g"][0], 8),
        "w_out": f(inp["w_out"])[0], "w_ff1": f(inp["w_ff1"])[0], "w_ff2": f(inp["w_ff2"])[0],
        "nfg_row": np.ascontiguousarray(np.broadcast_to(f(inp["normf_g"])[None, :], (128, D))),
    }
    tabs = [_const_tables(0), _const_tables(1)]
    c = f(inp["c"])
    in_maps = []
    for core in range(8):
        b, hf = core // 2, core % 2
        xb = x[b]
        own_pos = np.concatenate([np.arange(128) + 128 * (2 * n + hf) for n in range(NOB)])
        halo_pos = np.concatenate([np.array([128 * (2 * n + hf) - 2, 128 * (2 * n + hf) - 1]) for n in range(NOB)])
        xo = np.empty((NOWN + NHALO, D), np.float32)
        xo[:NOWN] = xb[own_pos]
        xo[NOWN:] = xb[np.maximum(halo_pos, 0)]
        m = dict(shared)
        m.update(tabs[hf])
        m["x_ctx"] = xb
        m["x_own"] = xo
        m["c_fm"] = np.ascontiguousarray(c[b].reshape(16, 128).T)
        in_maps.append(m)
    return in_maps


def kernel(**inputs):
    in_maps = prep_inputs(inputs)
    nc = build_nc()
    res = run_bass_kernel_spmd(nc, in_maps, core_ids=list(range(8)))
    outf = np.empty((4, S, D), np.float32)
    for core in range(8):
        b, hf = core // 2, core % 2
        o = np.asarray(res.results[core]["out"]).reshape(NOB, 128, D)
        outf[b].reshape(64, 128, D)[hf::2] = o
    return outf
```

```python
import contextlib
import numpy as np
import ml_dtypes
import concourse.bass as bass
import concourse.mybir as mybir
from concourse.bass_utils import run_bass_kernel_spmd

F32 = mybir.dt.float32
BF16 = mybir.dt.bfloat16
AF = mybir.ActivationFunctionType
ALU = mybir.AluOpType
BF = ml_dtypes.bfloat16

D = 2048
S = 8192
NOB = 32
NOWN = 4096
NHALO = 64
DFF = 8192
NEG = -30000.0
EPS = 1e-6
KC = 16


class TK:
    def __init__(self, nc, es):
        self.nc, self.es = nc, es
        self.eng = {}
        for nm, h in (("pe", nc.tensor), ("act", nc.scalar), ("dve", nc.vector),
                      ("pool", nc.gpsimd), ("sp", nc.sync)):
            self.eng[nm] = dict(h=h, sem=es.enter_context(nc.semaphore("se_" + nm)), cnt=0,
                                seen={}, name=nm)
        self.lastw = {}
        self.readers = {}
        self.dsem = {}
        self.free_sems = []
        self.nsd = 0

    def _deps(self, reads, writes):
        d = []
        for k in reads:
            t = self.lastw.get(k)
            if t:
                d.append(t)
        for k in writes:
            t = self.lastw.get(k)
            if t:
                d.append(t)
            d.extend(self.readers.get(k, {}).values())
        return d

    def _wait(self, e, deps):
        for (sid, sem, val, src) in deps:
            if src == "pe" and e["name"] == "pe":
                continue
            if e["seen"].get(sid, 0) >= val:
                continue
            e["h"].wait_ge(sem, val)
            e["seen"][sid] = val

    def _commit(self, tok, reads, writes):
        for k in writes:
            self.lastw[k] = tok
            self.readers[k] = {}
        for k in reads:
            self.readers.setdefault(k, {})[tok[0]] = tok

    def op(self, en, fn, reads=(), writes=()):
        e = self.eng[en]
        self._wait(e, self._deps(reads, writes))
        inst = fn(e["h"])
        e["cnt"] += 1
        inst.then_inc(e["sem"], 1)
        self._commit((en, e["sem"], e["cnt"], en), reads, writes)

    def dma(self, qn, out, in_, reads=(), writes=(), skey=None):
        e = self.eng[qn]
        self._wait(e, self._deps(reads, writes))
        if skey not in self.dsem:
            if self.free_sems:
                self.dsem[skey] = self.free_sems.pop()
            else:
                nm = "sd%d" % self.nsd
                self.nsd += 1
                self.dsem[skey] = dict(sem=self.es.enter_context(self.nc.semaphore(nm)), cnt=0, name=nm)
        s = self.dsem[skey]
        s["cnt"] += 16
        e["h"].dma_start(out=out, in_=in_).then_inc(s["sem"], 16)
        self._commit((s["name"], s["sem"], s["cnt"], "dma"), reads, writes)

    def barrier(self):
        allsems = list(self.dsem.values()) + self.free_sems
        for e in self.eng.values():
            for o in self.eng.values():
                if o is not e and o["cnt"] > 0 and e["seen"].get(o["name"], 0) < o["cnt"]:
                    e["h"].wait_ge(o["sem"], o["cnt"])
                    e["seen"][o["name"]] = o["cnt"]
            for s_ in allsems:
                if s_["cnt"] > 0 and e["seen"].get(s_["name"], 0) < s_["cnt"]:
                    e["h"].wait_ge(s_["sem"], s_["cnt"])
                    e["seen"][s_["name"]] = s_["cnt"]
        self.free_sems = allsems
        self.dsem = {}
        self.lastw = {}
        self.readers = {}

    def final_wait(self, en, keys):
        e = self.eng[en]
        self._wait(e, self._deps(keys, ()))


def build_nc(stop_after=99, debug=(), p1_tiles=16):
    nc = bass.Bass("TRN2", target_bir_lowering=False)

    def din(name, shape, dt=F32):
        return nc.dram_tensor(name, list(shape), dt, kind="ExternalInput").ap()

    def dscr(name, shape, dt, out=False):
        kind = "ExternalOutput" if (out or name in debug) else "Internal"
        return nc.dram_tensor(name, list(shape), dt, kind=kind).ap()

    x_ctx = din("x_ctx", [S, D])
    x_own = din("x_own", [NOWN + NHALO, D])
    c_fm = din("c_fm", [128, KC])
    w_ada = din("w_ada", [D, 6 * D])
    b_ada_fm = din("b_ada_fm", [128, 96])
    n1g_fm = din("n1g_fm", [128, KC])
    n2g_fm = din("n2g_fm", [128, KC])
    w_kv = din("w_kv", [D, 2304])
    w_own = din("w_own", [D, 5120])
    w_g = din("w_g", [D, 24])
    convw_fm = din("convw_fm", [128, 8, 3])
    convb_fm = din("convb_fm", [128, 8])
    pe_k = din("pe_k", [32, 128])
    pe_v = din("pe_v", [32, 128])
    w1_k = din("w1_k", [4096, 256])
    w1_v = din("w1_v", [4096, 256])
    w2_k = din("w2_k", [256, 128])
    w2_v = din("w2_v", [256, 128])
    gcg_fm = din("gcg_fm", [128, 8])
    gag_fm = din("gag_fm", [128, 8])
    w_out = din("w_out", [D, D])
    w_ff1 = din("w_ff1", [D, DFF])
    w_ff2 = din("w_ff2", [DFF, D])
    nfg_row = din("nfg_row", [128, D])
    cos_ctx = din("cos_ctx", [128, S])
    sin_ctx = din("sin_ctx", [128, S])
    cosq_own = din("cosq_own", [128, NOWN])
    sinq_own = din("sinq_own", [128, NOWN])
    E_in = din("E_in", [128, S], BF16)
    ident_f_in = din("ident_f_in", [128, 128])
    ident_b_in = din("ident_b_in", [128, 128], BF16)
    C_in = din("C_in", [128, 4, 128], BF16)
    cmpbias_in = din("cmpbias_in", [NOB, 128, 2, 128], BF16)
    winbias_in = din("winbias_in", [128, 6, 128], BF16)
    diagbias_in = din("diagbias_in", [128, 2, 128], BF16)
    vm_in = din("vm_in", [NOB, 128, 128])
    bv_in = din("bv_in", [NOB, 128, 128])
    halomask_in = din("halomask_in", [128, NHALO])

    out = nc.dram_tensor("out", [NOWN, D], F32, kind="ExternalOutput").ap()

    kslc_d = dscr("kslc_d", [2, 128, S], BF16)
    kwin_d = dscr("kwin_d", [2, 128, S], BF16)
    kc_d = dscr("kc_d", [2, 128, S], BF16)
    vc_d = dscr("vc_d", [2, 128, S], BF16)
    vslc_d = dscr("vslc_d", [2, 128, 64, 129], BF16)
    vwin_d = dscr("vwin_d", [2, 128, 64, 129], BF16)
    q_d = dscr("q_d", [NOB, 128, 8, 128], BF16)
    yc_d = dscr("yc_d", [8, 128, NOWN], BF16)
    ya_d = dscr("ya_d", [NOB, 128, 1024], F32)
    x1_d = dscr("x1_d", [NOWN, D], F32)
    h2_d = dscr("h2_d", [KC, 128, NOWN], BF16)
    y2_d = dscr("y2_d", [NOWN, D], F32)
    dbg_d = dscr("dbg_d", [128, 4096], F32, out=True) if "dbg_d" in debug else None

    with contextlib.ExitStack() as es:
        tk = TK(nc, es)

        def sb(name, shape, dt, stack=es):
            return stack.enter_context(nc.sbuf_tensor(name, list(shape), dt))

        PS = [es.enter_context(nc.psum_tensor("ps%d" % i, [128, 512], F32)) for i in range(8)]

        def pk(i):
            return ("ps", i)

        def dump(name, tile, shape, dt, keys):
            if name not in debug:
                return
            d = nc.dram_tensor(name, list(shape), dt, kind="ExternalOutput").ap()
            tk.dma("sp", d, tile, reads=keys, writes=[name], skey=name)

        ident_f = sb("ident_f", [128, 128], F32)
        ident_b = sb("ident_b", [128, 128], BF16)
        ones_f = sb("ones_f", [128, 128], F32)
        ones_b = sb("ones_b", [128, 128], BF16)
        A1 = sb("A1", [128, KC], F32)
        B1 = sb("B1", [128, KC], F32)
        A2 = sb("A2", [128, KC], F32)
        B2 = sb("B2", [128, KC], F32)
        G2 = sb("G2", [128, KC], F32)
        G1row = sb("G1row", [128, D], F32)
        gates_sb = sb("gates_sb", [128, NOB, 24], F32)
        kcmpT = sb("kcmpT", [128, 2, 512], BF16)
        vcmp = sb("vcmp", [128, 2, 4, 257], BF16)
        small = sb("small", [128, 64], F32)

        tk.dma("sp", ident_f[:], ident_f_in, writes=["ident_f"], skey="ident_f")
        tk.dma("sp", ident_b[:], ident_b_in, writes=["ident_b"], skey="ident_b")
        tk.op("dve", lambda e: e.memset(ones_f[:], 1.0), writes=["ones_f"])
        tk.op("dve", lambda e: e.memset(ones_b[:], 1.0), writes=["ones_b"])

        def norm_transpose(xt, xkey, npart, nblk, Asc, Bsc, hT, hkeyf, col0, xh, xhkey, junk, ssq, pb0, do_transposes=True):
            for blk in range(nblk):
                tk.op("act", lambda e, blk=blk: e.activation(out=xh[:npart, blk, :], in_=xt[:npart, blk, :], func=AF.Square,
                                                            accum_out=ssq[:npart, blk:blk + 1]),
                      reads=[(xkey, blk)], writes=[(xhkey, blk), ("ssq", blk)])
                tk.op("dve", lambda e, blk=blk: e.tensor_scalar(out=ssq[:npart, 8 + blk:9 + blk], in0=ssq[:npart, blk:blk + 1],
                                                               scalar1=1.0 / D, scalar2=EPS, op0=ALU.mult, op1=ALU.add),
                      reads=[("ssq", blk)], writes=[("ssq2", blk)])
                tk.op("act", lambda e, blk=blk: e.activation(out=ssq[:npart, 16 + blk:17 + blk], in_=ssq[:npart, 8 + blk:9 + blk],
                                                            func=AF.Sqrt),
                      reads=[("ssq2", blk)], writes=[("ssq3", blk)])
                tk.op("dve", lambda e, blk=blk: e.reciprocal(out=ssq[:npart, 24 + blk:25 + blk], in_=ssq[:npart, 16 + blk:17 + blk]),
                      reads=[("ssq3", blk)], writes=[("rstd", blk)])
                tk.op("act", lambda e, blk=blk: e.activation(out=xh[:npart, blk, :], in_=xt[:npart, blk, :], func=AF.Copy,
                                                            scale=ssq[:npart, 24 + blk:25 + blk]),
                      reads=[(xkey, blk), ("rstd", blk)], writes=[(xhkey, blk)])
            if not do_transposes:
                return
            for kc in range(KC):
                transpose_piece(kc, npart, nblk, Asc, Bsc, hT, hkeyf, col0, xh, xhkey, pb0)

        def transpose_piece(kc, npart, nblk, Asc, Bsc, hT, hkeyf, col0, xh, xhkey, pb0):
            ncol = nblk * npart
            if True:
                pb = pb0 + (kc % 2)
                for blk in range(nblk):
                    tk.op("pe", lambda e, kc=kc, blk=blk, pb=pb: e.transpose(
                        out=PS[pb][:, blk * npart:(blk + 1) * npart], in_=xh[:npart, blk, kc * 128:(kc + 1) * 128],
                        identity=ident_f[:npart, :npart]),
                        reads=[(xhkey, blk), "ident_f"], writes=[pk(pb)])
                tk.op("dve", lambda e, kc=kc, pb=pb: e.tensor_scalar(
                    out=hT[:, kc, col0:col0 + ncol], in0=PS[pb][:, 0:ncol], scalar1=Asc[:, kc:kc + 1], scalar2=Bsc[:, kc:kc + 1],
                    op0=ALU.mult, op1=ALU.add),
                    reads=[pk(pb), "AB"], writes=[hkeyf(kc)])

        with contextlib.ExitStack() as p0:
            cf = sb("cf", [128, KC], F32, p0)
            siluc = sb("siluc", [128, KC], F32, p0)
            modfm = sb("modfm", [128, 96], F32, p0)
            bada = sb("bada", [128, 96], F32, p0)
            n1g = sb("n1g", [128, KC], F32, p0)
            n2g = sb("n2g", [128, KC], F32, p0)
            diag = sb("diag", [128, 2, 128], F32, p0)
            wa = [sb("wa%d" % i, [128, KC, 1024], F32, p0) for i in range(2)]
            tk.dma("sp", cf[:], c_fm, writes=["cf"], skey="cf")
            tk.dma("sp", bada[:], b_ada_fm, writes=["bada"], skey="bada")
            tk.dma("sp", n1g[:], n1g_fm, writes=["n1g"], skey="n1g")
            tk.dma("sp", n2g[:], n2g_fm, writes=["n2g"], skey="n2g")
            tk.op("act", lambda e: e.activation(out=siluc[:], in_=cf[:], func=AF.Silu), reads=["cf"], writes=["siluc"])
            wav = w_ada.rearrange("(kc p) n -> p kc n", p=128)
            for G in range(12):
                s = G % 2
                tk.dma("sp" if G % 2 == 0 else "act", wa[s][:], wav[:, :, G * 1024:(G + 1) * 1024], writes=[("wa", s)], skey=("wa", s))
                for fc in range(8):
                    col = G * 8 + fc
                    for k in range(KC):
                        tk.op("pe", lambda e, s=s, fc=fc, k=k, col=col: e.matmul(
                            PS[0][:, col:col + 1], lhsT=wa[s][:, k, fc * 128:(fc + 1) * 128], rhs=siluc[:, k:k + 1],
                            start=(k == 0), stop=(k == KC - 1)),
                            reads=[("wa", s), "siluc"], writes=[pk(0)])
            tk.op("dve", lambda e: e.tensor_tensor(out=modfm[:], in0=PS[0][:, 0:96], in1=bada[:], op=ALU.add),
                  reads=[pk(0), "bada"], writes=["modfm"])
            tk.op("dve", lambda e: e.tensor_copy(out=B1[:], in_=modfm[:, 0:16]), reads=["modfm"], writes=["B1"])
            tk.op("dve", lambda e: e.scalar_tensor_tensor(out=A1[:], in0=modfm[:, 16:32], scalar=1.0, in1=n1g[:], op0=ALU.add, op1=ALU.mult),
                  reads=["modfm", "n1g"], writes=["A1"])
            tk.op("dve", lambda e: e.tensor_copy(out=B2[:], in_=modfm[:, 48:64]), reads=["modfm"], writes=["B2"])
            tk.op("dve", lambda e: e.scalar_tensor_tensor(out=A2[:], in0=modfm[:, 64:80], scalar=1.0, in1=n2g[:], op0=ALU.add, op1=ALU.mult),
                  reads=["modfm", "n2g"], writes=["A2"])
            tk.op("dve", lambda e: e.tensor_copy(out=G2[:], in_=modfm[:, 80:96]), reads=["modfm"], writes=["G2", "AB"])
            for kc in range(KC):
                s = kc % 2
                tk.op("dve", lambda e, kc=kc, s=s: e.tensor_scalar(out=diag[:, s, :], in0=ident_f[:], scalar1=modfm[:, 32 + kc:33 + kc],
                                                                 scalar2=None, op0=ALU.mult),
                      reads=["ident_f", "modfm"], writes=[("diag", s)])
                pb = 1 + (kc // 4) % 2
                tk.op("pe", lambda e, kc=kc, s=s, pb=pb: e.matmul(PS[pb][:, (kc % 4) * 128:(kc % 4 + 1) * 128], lhsT=ones_f[:], rhs=diag[:, s, :],
                                                                 start=True, stop=True),
                      reads=["ones_f", ("diag", s)], writes=[pk(pb)])
                if kc % 4 == 3:
                    tk.op("act", lambda e, kc=kc, pb=pb: e.activation(out=G1row[:, (kc - 3) * 128:(kc + 1) * 128], in_=PS[pb][:], func=AF.Copy),
                          reads=[pk(pb)], writes=["G1row"])
            dump("dbg_modfm", modfm[:], [128, 96], F32, ["modfm"])
            dump("dbg_A1", A1[:], [128, KC], F32, ["A1"])
            dump("dbg_G1row", G1row[:], [128, D], F32, ["G1row"])
            dump("dbg_siluc", siluc[:], [128, KC], F32, ["siluc"])
        tk.barrier()
        if stop_after <= 0:
            return _finish(nc, tk, out, es, ["G1row", "AB"])

        with contextlib.ExitStack() as p1:
            wkv = sb("wkv", [128, KC, 2304], BF16, p1)
            xraw = sb("xraw1", [128, 4, D], F32, p1)
            xh = sb("xh1", [128, 4, D], F32, p1)
            hT = [sb("hT1_%d" % i, [128, KC, 512], BF16, p1) for i in range(2)]
            junk = None
            ssq = sb("ssq1", [128, 32], F32, p1)
            cst = [sb("cst1_%d" % i, [128, 512], F32, p1) for i in range(2)]
            snt = [sb("snt1_%d" % i, [128, 512], F32, p1) for i in range(2)]
            t1 = sb("t1_1", [128, 512], F32, p1)
            t2 = sb("t2_1", [128, 512], F32, p1)
            ko = [sb("ko1_%d" % i, [128, 512], BF16, p1) for i in range(2)]
            vaug = [sb("vaug1_%d" % i, [128, 4, 129], BF16, p1) for i in range(2)]
            wkvv = w_kv.rearrange("(kc p) n -> p kc n", p=128)
            for j in range(0, 2304, 384):
                tk.dma("pool", wkv[:, :, j:j + 384], wkvv[:, :, j:j + 384], writes=[("wkv", j)], skey=("wkv", j))
            wkv_keys = [("wkv", j) for j in range(0, 2304, 384)]
            for i in range(2):
                tk.op("dve", lambda e, i=i: e.memset(vaug[i][:, :, 128:129], 1.0), writes=[("vaug", i)])
            xcv = x_ctx.rearrange("(cb p) d -> cb p d", p=128)
            kout = 0
            def front1(tt):
                hs_ = tt % 2
                for blk in range(4):
                    tk.dma("sp", xraw[:, blk, :], xcv[tt * 4 + blk], writes=[("xraw", blk)], skey=("xraw", blk))
                tk.dma("sp", cst[hs_][:], cos_ctx[:, tt * 512:(tt + 1) * 512], writes=[("cst", hs_)], skey=("cst", hs_))
                tk.dma("sp", snt[hs_][:], sin_ctx[:, tt * 512:(tt + 1) * 512], writes=[("snt", hs_)], skey=("snt", hs_))
                norm_transpose(xraw, "xraw", 128, 4, A1, B1, hT[hs_], lambda kc, hs_=hs_: ("hT", hs_, kc), 0, xh, "xh", junk, ssq, 0,
                               do_transposes=False)
                pend = [(lambda kc=kc, hs_=hs_: transpose_piece(kc, 128, 4, A1, B1, hT[hs_], lambda kc2, hs_=hs_: ("hT", hs_, kc2), 0, xh, "xh", 0))
                        for kc in range(KC)]
                if tt == 0:
                    for f_ in pend:
                        f_()
                    pend = []
                pending[0] = pend
                if tt == 0:
                    dump("dbg_ssq", ssq[:], [128, 32], F32, [("rstd", b_) for b_ in range(4)])
                    dump("dbg_xh", xh[:, 0, :], [128, D], F32, [("xh", 0)])
                    dump("dbg_hT", hT[0][:], [128, KC, 512], BF16, [("hT", 0, kc) for kc in range(KC)])

            pending = [[]]
            ncall = [0]

            def maybe_piece():
                ncall[0] += 1
                if ncall[0] >= 3 and pending[0]:
                    pending[0].pop(0)()

            def back1(tt):
                nonlocal kout
                hs_ = tt % 2
                ncall[0] = 0
                for kind, dst in ((0, kc_d), (2, kslc_d), (4, kwin_d)):
                    for g in range(2):
                        m_main = kind * 2 + g
                        m_sw = (kind + 1) * 2 + g
                        pa, pb_ = 2 + 2 * (kout % 3), 3 + 2 * (kout % 3)
                        for (m, pbk) in ((m_main, pa), (m_sw, pb_)):
                            for k in range(KC):
                                tk.op("pe", lambda e, m=m, pbk=pbk, k=k: e.matmul(PS[pbk][:], lhsT=wkv[:, k, m * 128:(m + 1) * 128], rhs=hT[hs_][:, k, :],
                                                                                 start=(k == 0), stop=(k == KC - 1)),
                                      reads=wkv_keys + [("hT", hs_, k)], writes=[pk(pbk)])
                            maybe_piece()
                        tk.op("dve", lambda e, pa=pa: e.tensor_tensor(out=t1[:], in0=PS[pa][:], in1=cst[hs_][:], op=ALU.mult),
                              reads=[pk(pa), ("cst", hs_)], writes=["t1"])
                        tk.op("dve", lambda e, pb_=pb_: e.tensor_tensor(out=t2[:], in0=PS[pb_][:], in1=snt[hs_][:], op=ALU.mult),
                              reads=[pk(pb_), ("snt", hs_)], writes=["t2"])
                        ks = kout % 2
                        tk.op("pool", lambda e, ks=ks: e.tensor_tensor(out=ko[ks][:], in0=t1[:], in1=t2[:], op=ALU.add),
                              reads=["t1", "t2"], writes=[("ko", ks)])
                        tk.dma("act", dst[g, :, tt * 512:(tt + 1) * 512], ko[ks][:], reads=[("ko", ks)],
                               writes=[(dst.tensor.name, g, tt)], skey=("ko", ks))
                        kout += 1
                for g in range(2):
                    m = 12 + g
                    pa = 2 + 2 * (kout % 3)
                    for k in range(KC):
                        tk.op("pe", lambda e, m=m, pa=pa, k=k: e.matmul(PS[pa][:], lhsT=wkv[:, k, m * 128:(m + 1) * 128], rhs=hT[hs_][:, k, :],
                                                                       start=(k == 0), stop=(k == KC - 1)),
                              reads=wkv_keys + [("hT", hs_, k)], writes=[pk(pa)])
                    maybe_piece()
                    ks = kout % 2
                    tk.op("act", lambda e, ks=ks, pa=pa: e.activation(out=ko[ks][:], in_=PS[pa][:], func=AF.Copy),
                          reads=[pk(pa)], writes=[("ko", ks)])
                    tk.dma("act", vc_d[g, :, tt * 512:(tt + 1) * 512], ko[ks][:], reads=[("ko", ks)],
                           writes=[("vc_d", g, tt)], skey=("ko", ks))
                    kout += 1
                for blk in range(4):
                    cb = tt * 4 + blk
                    pa = 2 + 2 * (kout % 3)
                    vs = kout % 2
                    for k in range(KC):
                        tk.op("pe", lambda e, pa=pa, k=k, blk=blk: e.matmul(PS[pa][:], lhsT=hT[hs_][:, k, blk * 128:(blk + 1) * 128], rhs=wkv[:, k, 1792:2304],
                                                                           start=(k == 0), stop=(k == KC - 1)),
                              reads=wkv_keys + [("hT", hs_, k)], writes=[pk(pa)])
                    maybe_piece()
                    tk.op("act", lambda e, vs=vs, pa=pa: e.activation(out=vaug[vs][:, :, 0:128], in_=PS[pa][:].rearrange("p (a b) -> p a b", a=4),
                                                                     func=AF.Copy),
                          reads=[pk(pa)], writes=[("vaug", vs)])
                    tk.dma("act", vslc_d[:, :, cb, :].rearrange("g p e -> p g e"), vaug[vs][:, 0:2, :], reads=[("vaug", vs)],
                           writes=[("vslc_d", cb)], skey=("vaug", vs))
                    tk.dma("act", vwin_d[:, :, cb, :].rearrange("g p e -> p g e"), vaug[vs][:, 2:4, :], reads=[("vaug", vs)],
                           writes=[("vwin_d", cb)], skey=("vaug", vs, 1))
                    kout += 1
            front1(0)
            for tt in range(p1_tiles):
                if tt + 1 < p1_tiles:
                    front1(tt + 1)
                back1(tt)
                while pending[0]:
                    pending[0].pop(0)()
        tk.barrier()
        if stop_after <= 1:
            return _finish(nc, tk, out, es, [])

        with contextlib.ExitStack() as p2:
            w1 = sb("w1c", [128, 32, 256], BF16, p2)
            w2 = sb("w2c", [128, 2, 128], BF16, p2)
            pet = sb("pet", [32, 128], F32, p2)
            peT = sb("peT", [128, 32], BF16, p2)
            src = sb("srcc", [128, S], BF16, p2)
            hb = sb("hbias", [128, 2], F32, p2)
            hx = sb("hx", [128, 512], F32, p2)
            hx2 = sb("hx2", [128, 512], F32, p2)
            hx3 = sb("hx3", [128, 512], F32, p2)
            gl = sb("gelu", [128, 2, 512], BF16, p2)
            tk.op("dve", lambda e: e.memset(gl[:], 0.0), writes=["gelu"])
            for g in range(2):
                tk.dma("sp", vcmp[:, g, :, 129:257], C_in, writes=[("vcmpC", g)], skey=("vcmpC", g))
            tk.op("dve", lambda e: e.memset(vcmp[:, :, :, 128:129], 1.0), writes=["vcmp1"])
            for (kv, w1_in, w2_in, pe_in, src_d) in ((0, w1_k, w2_k, pe_k, kc_d), (1, w1_v, w2_v, pe_v, vc_d)):
                tk.dma("pool", w1[:, :, :], w1_in.rearrange("(p d) h -> d p h", d=128), writes=["w1c"], skey="w1c")
                tk.dma("pool", w2[:, :, :], w2_in.rearrange("(a h) d -> h a d", h=128), writes=["w2c"], skey="w2c")
                tk.dma("sp", pet[:], pe_in, writes=["pet"], skey="pet")
                tk.op("pe", lambda e: e.transpose(out=PS[0][:, 0:32], in_=pet[:, :], identity=ident_f[:32, :32]),
                      reads=["pet", "ident_f"], writes=[pk(0)])
                tk.op("dve", lambda e: e.tensor_copy(out=peT[:], in_=PS[0][:, 0:32]), reads=[pk(0)], writes=["peT"])
                for hm in range(2):
                    for p in range(32):
                        tk.op("pe", lambda e, hm=hm, p=p: e.matmul(PS[1][:, hm:hm + 1], lhsT=w1[:, p, hm * 128:(hm + 1) * 128], rhs=peT[:, p:p + 1],
                                                                  start=(p == 0), stop=(p == 31)),
                              reads=["w1c", "peT"], writes=[pk(1)])
                tk.op("dve", lambda e: e.tensor_copy(out=hb[:], in_=PS[1][:, 0:2]), reads=[pk(1)], writes=["hb"])
                for g in range(2):
                    srck = [(src_d.tensor.name, g, tt) for tt in range(16)]
                    tk.dma("sp", src[:], src_d[g], reads=srck, writes=["srcc"], skey="srcc")
                    srcv = src[:].rearrange("p (i s) -> p i s", s=16)
                    for hm in range(2):
                        pb = 2 + hm
                        for p in range(32):
                            i0, sft = divmod(p, 16)
                            tk.op("pe", lambda e, hm=hm, p=p, pb=pb, i0=i0, sft=sft: e.matmul(
                                PS[pb][:, 0:511], lhsT=w1[:, p, hm * 128:(hm + 1) * 128], rhs=srcv[:, i0:i0 + 511, sft],
                                start=(p == 0), stop=(p == 31)),
                                reads=["w1c", "srcc"], writes=[pk(pb)])
                        tk.op("act", lambda e, hm=hm, pb=pb: e.activation(out=hx[:, 0:511], in_=PS[pb][:, 0:511], func=AF.Identity,
                                                                         bias=hb[:, hm:hm + 1], scale=1.0),
                              reads=[pk(pb), "hb"], writes=["hx"])
                        tk.op("act", lambda e: e.activation(out=hx2[:, 0:511], in_=hx[:, 0:511], func=AF.Square), reads=["hx"], writes=["hx2"])
                        tk.op("dve", lambda e: e.tensor_scalar(out=hx2[:, 0:511], in0=hx2[:, 0:511], scalar1=0.044715, scalar2=1.0,
                                                               op0=ALU.mult, op1=ALU.add), reads=["hx2"], writes=["hx2"])
                        tk.op("dve", lambda e: e.tensor_tensor(out=hx2[:, 0:511], in0=hx2[:, 0:511], in1=hx[:, 0:511], op=ALU.mult),
                              reads=["hx2", "hx"], writes=["hx2"])
                        tk.op("act", lambda e: e.activation(out=hx3[:, 0:511], in_=hx2[:, 0:511], func=AF.Tanh, scale=0.7978845608028654),
                              reads=["hx2"], writes=["hx3"])
                        tk.op("dve", lambda e: e.tensor_scalar(out=hx3[:, 0:511], in0=hx3[:, 0:511], scalar1=1.0, scalar2=0.5,
                                                               op0=ALU.add, op1=ALU.mult), reads=["hx3"], writes=["hx3"])
                        tk.op("dve", lambda e, hm=hm: e.tensor_tensor(out=gl[:, hm, 0:511], in0=hx3[:, 0:511], in1=hx[:, 0:511], op=ALU.mult),
                              reads=["hx3", "hx"], writes=["gelu"])
                    if kv == 0:
                        for hm in range(2):
                            tk.op("pe", lambda e, hm=hm: e.matmul(PS[4][:], lhsT=w2[:, hm, :], rhs=gl[:, hm, :], start=(hm == 0), stop=(hm == 1)),
                                  reads=["w2c", "gelu"], writes=[pk(4)])
                        tk.op("act", lambda e, g=g: e.activation(out=kcmpT[:, g, :], in_=PS[4][:], func=AF.Copy),
                              reads=[pk(4)], writes=["kcmpT"])
                    else:
                        for c in range(4):
                            for hm in range(2):
                                tk.op("pe", lambda e, hm=hm, c=c: e.matmul(PS[5][:, c * 128:(c + 1) * 128], lhsT=gl[:, hm, c * 128:(c + 1) * 128],
                                                                          rhs=w2[:, hm, :], start=(hm == 0), stop=(hm == 1)),
                                      reads=["w2c", "gelu"], writes=[pk(5)])
                        tk.op("act", lambda e, g=g: e.activation(out=vcmp[:, g, :, 0:128], in_=PS[5][:].rearrange("p (c d) -> p c d", c=4),
                                                                func=AF.Copy),
                              reads=[pk(5)], writes=["vcmp"])
        if True:
            if "dbg_cmp" in debug:
                dk = nc.dram_tensor("dbg_kcmp", [128, 2, 512], BF16, kind="ExternalOutput").ap()
                dv = nc.dram_tensor("dbg_vcmp", [128, 2, 4, 257], BF16, kind="ExternalOutput").ap()
                tk.dma("sp", dk, kcmpT[:], reads=["kcmpT"], writes=["dbgk"], skey="dbgk")
                tk.dma("sp", dv, vcmp[:], reads=["vcmp", ("vcmpC", 0), ("vcmpC", 1), "vcmp1"], writes=["dbgv"], skey="dbgv")
        tk.barrier()
        if stop_after <= 2:
            return _finish(nc, tk, out, es, [])
        def bc4(ap2d):
            return ap2d.rearrange("p (o n) -> p o n", o=1).broadcast_to([128, 4, 128])

        with contextlib.ExitStack() as p3:
            hTo = sb("hTo", [128, KC, NOWN + NHALO], BF16, p3)
            with contextlib.ExitStack() as p3x:
                xraw = sb("xraw3", [128, 2, D], F32, p3x)
                xh = sb("xh3", [128, 2, D], F32, p3x)
                junk = None
                ssq = sb("ssq3", [128, 32], F32, p3x)
                xov = x_own[0:NOWN, :].rearrange("(cb p) d -> cb p d", p=128)
                for tt in range(16):
                    for blk in range(2):
                        tk.dma("sp", xraw[:, blk, :], xov[tt * 2 + blk], writes=[("xraw", blk)], skey=("xraw", blk))
                    norm_transpose(xraw, "xraw", 128, 2, A1, B1, hTo, lambda kc: ("hTo", kc), tt * 256, xh, "xh", junk, ssq, 0)
                tk.dma("sp", xraw[:NHALO, 0, :], x_own[NOWN:NOWN + NHALO, :], writes=[("xraw", 0)], skey=("xraw", 0))
                norm_transpose(xraw, "xraw", NHALO, 1, A1, B1, hTo, lambda kc: ("hTo", kc), NOWN, xh, "xh", junk, ssq, 0)
            tk.barrier()
            wbuf = [sb("wbuf3_%d" % i, [128, KC, 512], BF16, p3) for i in range(2)]
            cq = sb("cq3", [128, 512], F32, p3)
            sq = sb("sq3", [128, 512], F32, p3)
            hs = sb("hs3", [128, 512], F32, p3)
            vv = sb("vv3", [128, 4, 130], F32, p3)
            zt = sb("zt3", [128, 512], F32, p3)
            t1 = sb("t1_3", [128, 512], F32, p3)
            t2 = sb("t2_3", [128, 512], F32, p3)
            yo = [sb("yo3_%d" % i, [128, 512], BF16, p3) for i in range(2)]
            qo = [sb("qo3_%d" % i, [128, 512], BF16, p3) for i in range(2)]
            vhalo = sb("vhalo", [128, 8, NHALO], F32, p3)
            hmask = sb("hmask", [128, NHALO], F32, p3)
            cw = sb("cw3", [128, 8, 3], F32, p3)
            cb_ = sb("cb3", [128, 8], F32, p3)
            wg = sb("wg3", [128, KC, 24], BF16, p3)
            tk.dma("sp", hmask[:], halomask_in, writes=["hmask"], skey="hmask")
            tk.dma("sp", cw[:], convw_fm, writes=["cw"], skey="cw")
            tk.dma("sp", cb_[:], convb_fm, writes=["cb"], skey="cb")
            tk.dma("pool", wg[:], w_g.rearrange("(kc p) n -> p kc n", p=128), writes=["wg"], skey="wg")
            hkeys = [("hTo", kc) for kc in range(KC)]
            wov = w_own.rearrange("(kc p) n -> p kc n", p=128)
            groups = [(ch * 384, 3, "conv", ch) for ch in range(8)] + [(3072 + j * 512, 4, "q", j) for j in range(4)]
            it = 0
            yoc = 0
            qoc = 0
            for gi, (c0, nm, kind, idx) in enumerate(groups):
                s = gi % 2
                tk.dma("pool", wbuf[s][:, :, 0:nm * 128], wov[:, :, c0:c0 + nm * 128], writes=[("wb", s)], skey=("wb", s))
                tiles = ([8] if kind == "conv" else []) + list(range(8))
                for tt in tiles:
                    ncol = NHALO if tt == 8 else 512
                    cs = NOWN if tt == 8 else tt * 512
                    banks = [4 * (it % 2) + m for m in range(nm)]
                    it += 1
                    ms = [1, 2] if tt == 8 else list(range(nm))
                    for m in ms:
                        for k in range(KC):
                            tk.op("pe", lambda e, m=m, k=k, s=s, b=banks[m], cs=cs, ncol=ncol: e.matmul(
                                PS[b][:, 0:ncol], lhsT=wbuf[s][:, k, m * 128:(m + 1) * 128], rhs=hTo[:, k, cs:cs + ncol],
                                start=(k == 0), stop=(k == KC - 1)),
                                reads=[("wb", s), ("hTo", k)], writes=[pk(banks[m])])
                    if kind == "conv":
                        ch = idx
                        bB, bC, bH = banks
                        if tt == 8:
                            tk.op("act", lambda e, bH=bH: e.activation(out=hs[:, 0:NHALO], in_=PS[bH][:, 0:NHALO], func=AF.Copy),
                                  reads=[pk(bH)], writes=["hs"])
                            tk.op("dve", lambda e, bC=bC: e.tensor_tensor(out=t1[:, 0:NHALO], in0=PS[bC][:, 0:NHALO], in1=hs[:, 0:NHALO], op=ALU.mult),
                                  reads=[pk(bC), "hs"], writes=["t1"])
                            tk.op("dve", lambda e, ch=ch: e.tensor_tensor(out=vhalo[:, ch, :], in0=t1[:, 0:NHALO], in1=hmask[:], op=ALU.mult),
                                  reads=["t1", "hmask"], writes=[("vhalo", ch)])
                            continue
                        tk.op("act", lambda e, bH=bH: e.activation(out=hs[:], in_=PS[bH][:], func=AF.Copy), reads=[pk(bH)], writes=["hs"])
                        tk.op("pool", lambda e, ch=ch, tt=tt: e.tensor_copy(out=vv[:, :, 0:2],
                                                                            in_=vhalo[:, ch, tt * 8:(tt + 1) * 8].rearrange("p (a b) -> p a b", b=2)),
                              reads=[("vhalo", ch)], writes=["vvh"])
                        tk.op("dve", lambda e, bC=bC: e.tensor_tensor(out=vv[:, :, 2:130], in0=PS[bC][:].rearrange("p (a b) -> p a b", a=4),
                                                                      in1=hs[:].rearrange("p (a b) -> p a b", a=4), op=ALU.mult),
                              reads=[pk(bC), "hs"], writes=["vv"])
                        ztv = zt[:].rearrange("p (a b) -> p a b", a=4)
                        tk.op("dve", lambda e, ch=ch, ztv=ztv: e.tensor_scalar(out=ztv, in0=vv[:, :, 2:130], scalar1=cw[:, ch, 2:3], scalar2=cb_[:, ch:ch + 1],
                                                                           op0=ALU.mult, op1=ALU.add),
                              reads=["vv", "cw", "cb"], writes=["zt"])
                        tk.op("dve", lambda e, ch=ch, ztv=ztv: e.scalar_tensor_tensor(out=ztv, in0=vv[:, :, 1:129], scalar=cw[:, ch, 1:2], in1=ztv,
                                                                                  op0=ALU.mult, op1=ALU.add),
                              reads=["vv", "vvh", "cw", "zt"], writes=["zt"])
                        tk.op("dve", lambda e, ch=ch, ztv=ztv: e.scalar_tensor_tensor(out=ztv, in0=vv[:, :, 0:128], scalar=cw[:, ch, 0:1], in1=ztv,
                                                                                  op0=ALU.mult, op1=ALU.add),
                              reads=["vv", "vvh", "cw", "zt"], writes=["zt"])
                        ys = yoc % 2
                        yoc += 1
                        tk.op("dve", lambda e, bB=bB, ys=ys: e.tensor_tensor(out=yo[ys][:], in0=PS[bB][:], in1=zt[:], op=ALU.mult),
                              reads=[pk(bB), "zt"], writes=[("yo", ys)])
                        tk.dma("act", yc_d[ch, :, tt * 512:(tt + 1) * 512], yo[ys][:], reads=[("yo", ys)], writes=[], skey=("yo", ys))
                    else:
                        tk.dma("sp", cq[:], cosq_own[:, tt * 512:(tt + 1) * 512], writes=["cq"], skey="cq")
                        tk.dma("sp", sq[:], sinq_own[:, tt * 512:(tt + 1) * 512], writes=["sq"], skey="sq")
                        for hh in range(2):
                            head = idx * 2 + hh
                            bq, bs = banks[2 * hh], banks[2 * hh + 1]
                            tk.op("dve", lambda e, bq=bq: e.tensor_tensor(out=t1[:], in0=PS[bq][:], in1=cq[:], op=ALU.mult),
                                  reads=[pk(bq), "cq"], writes=["t1"])
                            tk.op("dve", lambda e, bs=bs: e.tensor_tensor(out=t2[:], in0=PS[bs][:], in1=sq[:], op=ALU.mult),
                                  reads=[pk(bs), "sq"], writes=["t2"])
                            qs_ = qoc % 2
                            qoc += 1
                            tk.op("pool", lambda e, qs_=qs_: e.tensor_tensor(out=qo[qs_][:], in0=t1[:], in1=t2[:], op=ALU.add),
                                  reads=["t1", "t2"], writes=[("qo", qs_)])
                            tk.dma("act", q_d[tt * 4:(tt + 1) * 4, :, head, :].rearrange("b p q -> p b q"),
                                   qo[qs_][:].rearrange("p (b q) -> p b q", b=4), reads=[("qo", qs_)], writes=[], skey=("qo", qs_))
            for blk in range(NOB):
                b = 4 * (it % 2)
                it += 1
                for k in range(KC):
                    tk.op("pe", lambda e, k=k, b=b, blk=blk: e.matmul(PS[b][:, 0:24], lhsT=hTo[:, k, blk * 128:(blk + 1) * 128], rhs=wg[:, k, :],
                                                                     start=(k == 0), stop=(k == KC - 1)),
                          reads=["wg", ("hTo", k)], writes=[pk(b)])
                tk.op("act", lambda e, b=b, blk=blk: e.activation(out=gates_sb[:, blk, :], in_=PS[b][:, 0:24], func=AF.Sigmoid),
                      reads=[pk(b)], writes=["gates"])
        tk.barrier()
        if stop_after <= 3:
            return _finish(nc, tk, out, es, [])

        with contextlib.ExitStack() as p4:
            ksl = sb("ksl", [128, S], BF16, p4)
            vsl = sb("vsl", [128, 64, 129], BF16, p4)
            Et = sb("Et", [128, S], BF16, p4)
            winb = sb("winb", [128, 6, 128], BF16, p4)
            diagb = sb("diagb", [128, 2, 128], BF16, p4)
            qT = [sb("qT%d" % i, [128, 512], BF16, p4) for i in range(2)]
            kw = [sb("kw%d" % i, [128, 768], BF16, p4) for i in range(2)]
            vw = [sb("vw%d" % i, [128, 6, 129], BF16, p4) for i in range(2)]
            cmpb = [sb("cmpb%d" % i, [128, 2, 128], BF16, p4) for i in range(2)]
            vmt = [sb("vmt%d" % i, [128, 128], F32, p4) for i in range(2)]
            bvt = [sb("bvt%d" % i, [128, 128], F32, p4) for i in range(2)]
            pT = [sb("pT%d" % i, [128, 512], BF16, p4) for i in range(4)]
            ya = [sb("ya%d" % i, [128, 512], F32, p4) for i in range(2)]
            imp = sb("imp", [128, 128], F32, p4)
            score = sb("score", [128, 128], F32, p4)
            wk = sb("wk", [128, 128], F32, p4)
            m8 = sb("m8", [128, 16], F32, p4)
            selb = sb("selb", [128, 128], F32, p4)
            selbT = sb("selbT", [128, 128], BF16, p4)
            rs = sb("rs", [128, 16], F32, p4)
            tmpo = sb("tmpo", [128, 2, 256], F32, p4)
            tmpC = sb("tmpC", [128, 4, 127], F32, p4)
            obset = [0]
            tk.op("dve", lambda e: e.memset(imp[:], 0.0), writes=["imp"])
            tk.dma("sp", Et[:], E_in, writes=["Et"], skey="Et")
            tk.dma("sp", winb[:], winbias_in, writes=["winb"], skey="winb")
            tk.dma("sp", diagb[:], diagbias_in, writes=["diagb"], skey="diagb")
            pcount = [0]
            for g in range(2):
                tk.dma("sp", ksl[:], kslc_d[g], writes=["ksl"], skey="ksl")
                tk.dma("sp", vsl[:], vslc_d[g], writes=["vsl"], skey="vsl")
                for n in range(NOB):
                    s = n % 2
                    lo = max(0, 2 * n - 4)
                    w0 = lo - (2 * n - 4)
                    tk.dma("sp", qT[s][:].rearrange("p (h q) -> p h q", h=4), q_d[n, :, g * 4:(g + 1) * 4, :], writes=[("qT", s)], skey=("qT", s))
                    tk.dma("sp", kw[s][:, w0 * 128:768], kwin_d[g, :, lo * 128:(2 * n + 2) * 128], writes=[("kw", s)], skey=("kw", s))
                    tk.dma("sp", vw[s][:, w0:6, :], vwin_d[g, :, lo:2 * n + 2, :], writes=[("vw", s)], skey=("vw", s))
                    tk.dma("sp", cmpb[s][:], cmpbias_in[n], writes=[("cmpb", s)], skey=("cmpb", s))
                    tk.dma("sp", vmt[s][:], vm_in[n], writes=[("vmt", s)], skey=("vmt", s))
                    tk.dma("sp", bvt[s][:], bv_in[n], writes=[("bvt", s)], skey=("bvt", s))

                    def branch(chunks, br, ncolO, yas, first):
                        cmpmode = (ncolO == 257)
                        W = 256 if cmpmode else 129
                        ob = (2, 3) if obset[0] % 2 == 0 else (4, 5)
                        obset[0] += 1
                        nch = len(chunks)
                        slots = []

                        def qk(ci):
                            (kT, kkeys, biases, vap, vkeys) = chunks[ci]
                            sbk = (0, 1, 7)[pcount[0] % 3]
                            ps_ = pcount[0] % 4
                            pcount[0] += 1
                            slots.append(ps_)
                            mms = [(kT, qT[s][:], PS[sbk][:], kkeys + [("qT", s)])]
                            for (bl, br_, bkeys) in biases:
                                mms.append((bl, bc4(br_), PS[sbk][:].rearrange("p (o n) -> p o n", o=4), bkeys))
                            for mi, (l_, r_, o_, keys_) in enumerate(mms):
                                tk.op("pe", lambda e, l_=l_, r_=r_, o_=o_, mi=mi, nm_=len(mms): e.matmul(o_, lhsT=l_, rhs=r_, start=(mi == 0), stop=(mi == nm_ - 1)),
                                      reads=keys_, writes=[pk(sbk)])
                            tk.op("act", lambda e, sbk=sbk, ps_=ps_: e.activation(out=pT[ps_][:], in_=PS[sbk][:], func=AF.Exp),
                                  reads=[pk(sbk)], writes=[("pT", ps_)])

                        def pv(ci):
                            (kT, kkeys, biases, vap, vkeys) = chunks[ci]
                            ps_ = slots[ci]
                            for r in range(4):
                                b_ = ob[r // 2]
                                c0 = (r % 2) * W
                                tk.op("pe", lambda e, r=r, ps_=ps_, vap=vap, ci=ci, b_=b_, c0=c0: e.matmul(
                                    PS[b_][:, c0:c0 + W], lhsT=pT[ps_][:, r * 128:(r + 1) * 128], rhs=vap[:, 0:W],
                                    start=(ci == 0 and r % 2 == 0), stop=(ci == nch - 1), skip_group_check=True),
                                    reads=[("pT", ps_)] + vkeys, writes=[pk(b_)])

                        qk(0)
                        if nch > 1:
                            qk(1)
                        for ci in range(nch):
                            if ci + 2 < nch:
                                qk(ci + 2)
                            pv(ci)

                        def hv(b_):
                            return PS[b_][:, 0:2 * W].rearrange("p (h c) -> p h c", c=W)

                        for bi, b_ in enumerate(ob):
                            tk.op("dve", lambda e, bi=bi, b_=b_: e.tensor_scalar(out=rs[:, 2 * bi:2 * bi + 2], in0=hv(b_)[:, :, 128], scalar1=1e-30, scalar2=None, op0=ALU.max),
                                  reads=[pk(b_)], writes=[("rs", bi)])
                        tk.op("dve", lambda e: e.reciprocal(out=rs[:, 4:8], in_=rs[:, 0:4]), reads=[("rs", 0), ("rs", 1)], writes=["rinv"])
                        gview = gates_sb[:, n, g * 12:(g + 1) * 12].rearrange("p (h b) -> p h b", b=3)[:, :, br]
                        tk.op("dve", lambda e: e.tensor_tensor(out=rs[:, 8:12], in0=rs[:, 4:8], in1=gview, op=ALU.mult), reads=["rinv", "gates"], writes=["rg"])
                        for bi, b_ in enumerate(ob):
                            src = hv(b_)[:, :, 0:128]
                            scb = rs[:, 8 + 2 * bi:10 + 2 * bi].rearrange("p (h o) -> p h o", o=1).broadcast_to([128, 2, 128])
                            dst = ya[yas][:, bi * 256:(bi + 1) * 256].rearrange("p (h c) -> p h c", c=128)
                            if first:
                                tk.op("dve", lambda e, src=src, scb=scb, dst=dst: e.tensor_tensor(out=dst, in0=src, in1=scb, op=ALU.mult),
                                      reads=[pk(b_), "rg"], writes=[("ya", yas)])
                                srcC = hv(b_)[:, :, 129:256]
                                rb = rs[:, 4 + 2 * bi:6 + 2 * bi].rearrange("p (h o) -> p h o", o=1).broadcast_to([128, 2, 127])
                                tk.op("dve", lambda e, srcC=srcC, rb=rb, bi=bi: e.tensor_tensor(out=tmpC[:, 2 * bi:2 * bi + 2, :], in0=srcC, in1=rb, op=ALU.mult),
                                      reads=[pk(b_), "rinv"], writes=[("tmpC", bi)])
                            else:
                                tv = tmpo[:, bi, :].rearrange("p (h c) -> p h c", c=128)
                                tk.op("dve", lambda e, src=src, scb=scb, tv=tv: e.tensor_tensor(out=tv, in0=src, in1=scb, op=ALU.mult),
                                      reads=[pk(b_), "rg"], writes=[("tmpo", bi)])
                                tk.op("pool", lambda e, dst=dst, tv=tv: e.tensor_tensor(out=dst, in0=dst, in1=tv, op=ALU.add),
                                      reads=[("tmpo", bi), ("ya", yas)], writes=[("ya", yas)])
                        if first:
                            tk.op("pool", lambda e: e.tensor_tensor(out=tmpC[:, 0:2, :], in0=tmpC[:, 0:2, :], in1=tmpC[:, 2:4, :], op=ALU.add),
                                  reads=[("tmpC", 0), ("tmpC", 1)], writes=[("tmpC", 0)])
                            tk.op("pool", lambda e: e.tensor_tensor(out=imp[:, 0:127], in0=tmpC[:, 0, :], in1=tmpC[:, 1, :], op=ALU.add),
                                  reads=[("tmpC", 0)], writes=["imp"])

                    yas = n % 2
                    C = n // 8 + 1
                    chunks = []
                    for c in range(C):
                        j = c - (C - 2)
                        biases = [(ident_b[:], cmpb[s][:, j, :], ["ident_b", ("cmpb", s)])] if j >= 0 else []
                        chunks.append((kcmpT[:, g, c * 128:(c + 1) * 128], ["kcmpT"], biases, vcmp[:, g, c, :], ["vcmp", ("vcmpC", g), "vcmp1"]))
                    branch(chunks, 0, 257, yas, True)
                    tk.op("dve", lambda e: e.tensor_tensor(out=score[:], in0=imp[:], in1=vmt[s][:], op=ALU.mult), reads=["imp", ("vmt", s)], writes=["score"])
                    tk.op("dve", lambda e: e.tensor_tensor(out=score[:], in0=score[:], in1=bvt[s][:], op=ALU.add), reads=["score", ("bvt", s)], writes=["score"])
                    tk.op("dve", lambda e: e.max(out=m8[:, 0:8], in_=score[:]), reads=["score"], writes=["m8a"])
                    tk.op("dve", lambda e: e.match_replace(out=wk[:], in_to_replace=m8[:, 0:8], in_values=score[:], imm_value=-1e30),
                          reads=["score", "m8a"], writes=["wk"])
                    tk.op("dve", lambda e: e.max(out=m8[:, 8:16], in_=wk[:]), reads=["wk"], writes=["m8b"])
                    tk.op("dve", lambda e: e.tensor_scalar(out=wk[:], in0=score[:], scalar1=m8[:, 15:16], scalar2=None, op0=ALU.is_ge),
                          reads=["score", "m8b", "wk"], writes=["wk"])
                    tk.op("dve", lambda e: e.tensor_scalar(out=selb[:], in0=wk[:], scalar1=-1.0, scalar2=-NEG, op0=ALU.add, op1=ALU.mult),
                          reads=["wk"], writes=["selb"])
                    chunks = []
                    for w in range(w0, 6):
                        chunks.append((kw[s][:, w * 128:(w + 1) * 128], [("kw", s)], [(ident_b[:], winb[:, w, :], ["ident_b", "winb"])],
                                       vw[s][:, w, :], [("vw", s)]))
                    branch(chunks, 2, 129, yas, False)
                    tk.op("pe", lambda e: e.transpose(out=PS[6][:, 0:128], in_=selb[:], identity=ident_f[:]), reads=["selb", "ident_f"], writes=[pk(6)])
                    tk.op("act", lambda e: e.activation(out=selbT[:], in_=PS[6][:, 0:128], func=AF.Copy), reads=[pk(6)], writes=["selbT"])
                    chunks = []
                    for c in range(2 * n + 2):
                        biases = [(Et[:, c * 128:(c + 1) * 128], selbT[:], ["Et", "selbT"])]
                        if c >= 2 * n:
                            biases.append((ident_b[:], diagb[:, c - 2 * n, :], ["ident_b", "diagb"]))
                        chunks.append((ksl[:, c * 128:(c + 1) * 128], ["ksl"], biases, vsl[:, c, :], ["vsl"]))
                    branch(chunks, 1, 129, yas, False)
                    tk.dma("act", ya_d[n, :, g * 512:(g + 1) * 512], ya[yas][:], reads=[("ya", yas)], writes=[], skey=("ya", yas))
        tk.barrier()
        if stop_after <= 4:
            return _finish(nc, tk, out, es, [])

        with contextlib.ExitStack() as p5:
            wout = sb("wout", [128, KC, D], BF16, p5)
            gcg = sb("gcg", [128, 8], F32, p5)
            gag = sb("gag", [128, 8], F32, p5)
            yat = [sb("yat%d" % i, [128, 1024], F32, p5) for i in range(2)]
            yct = [sb("yct%d" % i, [128, 8, 128], BF16, p5) for i in range(2)]
            xt = [sb("xt5_%d" % i, [128, D], F32, p5) for i in range(2)]
            x1 = [sb("x1t%d" % i, [128, 1, D], F32, p5) for i in range(2)]
            xh = sb("xh5", [128, 1, D], F32, p5)
            junk = sb("junk5", [128, D], F32, p5)
            ssq = sb("ssq5", [128, 32], F32, p5)
            st = sb("st5", [128, 16], F32, p5)
            sqc = [sb("sqc%d" % i, [128, 8, 128], BF16, p5) for i in range(2)]
            ycg = [sb("ycg%d" % i, [128, 8, 128], BF16, p5) for i in range(2)]
            yaT = [sb("yaT%d" % i, [128, 8, 128], BF16, p5) for i in range(2)]
            tA = [sb("tA%d" % i, [128, 512], F32, p5) for i in range(2)]
            tB = [sb("tB%d" % i, [128, 512], F32, p5) for i in range(2)]
            h2T = [sb("h2T%d" % i, [128, KC, 128], BF16, p5) for i in range(2)]
            wouv = w_out.rearrange("(kc p) n -> p kc n", p=128)
            for j in range(0, D, 512):
                tk.dma("pool", wout[:, :, j:j + 512], wouv[:, :, j:j + 512], writes=[("wout", j)], skey=("wout", j))
            wout_keys = [("wout", j) for j in range(0, D, 512)]
            tk.dma("sp", gcg[:], gcg_fm, writes=["gcg"], skey="gcg")
            tk.dma("sp", gag[:], gag_fm, writes=["gag"], skey="gag")

            def rstd_chain(src_col, dst_col, nfeat):
                tk.op("dve", lambda e: e.tensor_scalar(out=st[:, dst_col:dst_col + 1], in0=src_col, scalar1=1.0 / nfeat, scalar2=EPS,
                                                       op0=ALU.mult, op1=ALU.add), reads=["st_in"], writes=["st_a"])
                tk.op("act", lambda e: e.activation(out=st[:, dst_col + 1:dst_col + 2], in_=st[:, dst_col:dst_col + 1], func=AF.Sqrt),
                      reads=["st_a"], writes=["st_b"])
                tk.op("dve", lambda e: e.reciprocal(out=st[:, dst_col + 2:dst_col + 3], in_=st[:, dst_col + 1:dst_col + 2]),
                      reads=["st_b"], writes=[("st_r", dst_col)])
                return st[:, dst_col + 2:dst_col + 3]

            bkc = [0]

            def front(n):
                s = n % 2
                tk.dma("sp", yat[s][:], ya_d[n], writes=[("yat", s)], skey=("yat", s))
                tk.dma("sp", yct[s][:], yc_d[:, :, n * 128:(n + 1) * 128].rearrange("c p t -> p c t"), writes=[("yct", s)], skey=("yct", s))
                tk.dma("sp", xt[s][:], x_own[n * 128:(n + 1) * 128, :], writes=[("xt", s)], skey=("xt", s))
                tk.op("act", lambda e: e.activation(out=junk[:, 0:1024], in_=yat[s][:], func=AF.Square, accum_out=st[:, 0:1]),
                      reads=[("yat", s)], writes=["junk", "st_in"])
                rstd_a = rstd_chain(st[:, 0:1], 1 + 8 * s, 1024)
                for cc in range(8):
                    pb = (bkc[0] // 4) % 2
                    tk.op("pe", lambda e, cc=cc, pb=pb: e.transpose(out=PS[pb][:, (cc % 4) * 128:(cc % 4 + 1) * 128], in_=yat[s][:, cc * 128:(cc + 1) * 128],
                                                                   identity=ident_f[:]), reads=[("yat", s), "ident_f"], writes=[pk(pb)])
                    tk.op("dve", lambda e, cc=cc, pb=pb: e.tensor_scalar(out=yaT[s][:, cc, :], in0=PS[pb][:, (cc % 4) * 128:(cc % 4 + 1) * 128],
                                                                        scalar1=gag[:, cc:cc + 1], scalar2=None, op0=ALU.mult),
                          reads=[pk(pb), "gag"], writes=[("yaT", s)])
                    bkc[0] += 1
                tk.op("act", lambda e: e.activation(out=sqc[s][:], in_=yct[s][:], func=AF.Square), reads=[("yct", s)], writes=[("sqc", s)])
                for cc in range(8):
                    tk.op("pe", lambda e, cc=cc: e.matmul(PS[2][:, 0:1], lhsT=sqc[s][:, cc, :], rhs=ones_b[:, 0:1], start=(cc == 0), stop=(cc == 7)),
                          reads=[("sqc", s), "ones_b"], writes=[pk(2)])
                tk.op("dve", lambda e: e.tensor_copy(out=st[:, 4:5], in_=PS[2][:, 0:1]), reads=[pk(2), ("st_r", 1 + 8 * s)], writes=["st_in"])
                rstd_c = rstd_chain(st[:, 4:5], 5 + 8 * s, 1024)
                for cc in range(8):
                    tk.op("act", lambda e, cc=cc: e.activation(out=ycg[s][:, cc, :], in_=yct[s][:, cc, :], func=AF.Copy, scale=gcg[:, cc:cc + 1]),
                          reads=[("yct", s), "gcg"], writes=[("ycg", s)])
                return rstd_a, rstd_c

            def front2(n, rstd_a, rstd_c):
                s = n % 2
                for dc in range(4):
                    d0 = dc * 512
                    pc_, pa_ = 3 + 2 * (dc % 2), 4 + 2 * (dc % 2)
                    ts_ = dc % 2
                    for cc in range(8):
                        tk.op("pe", lambda e, cc=cc, d0=d0, pc_=pc_: e.matmul(PS[pc_][:], lhsT=ycg[s][:, cc, :], rhs=wout[:, cc, d0:d0 + 512], start=(cc == 0), stop=(cc == 7)),
                              reads=[("ycg", s)] + wout_keys, writes=[pk(pc_)])
                    for cc in range(8):
                        tk.op("pe", lambda e, cc=cc, d0=d0, pa_=pa_: e.matmul(PS[pa_][:], lhsT=yaT[s][:, cc, :], rhs=wout[:, 8 + cc, d0:d0 + 512], start=(cc == 0), stop=(cc == 7)),
                              reads=[("yaT", s)] + wout_keys, writes=[pk(pa_)])
                    tk.op("act", lambda e, pc_=pc_, ts_=ts_: e.activation(out=tA[ts_][:], in_=PS[pc_][:], func=AF.Copy, scale=rstd_c),
                          reads=[pk(pc_), ("st_r", 5 + 8 * s)], writes=[("tA", ts_)])
                    tk.op("dve", lambda e, pa_=pa_, ts_=ts_: e.scalar_tensor_tensor(out=tB[ts_][:], in0=PS[pa_][:], scalar=rstd_a, in1=tA[ts_][:], op0=ALU.mult, op1=ALU.add),
                          reads=[pk(pa_), ("st_r", 1 + 8 * s), ("tA", ts_)], writes=[("tB", ts_)])
                    tk.op("dve", lambda e, d0=d0, ts_=ts_: e.tensor_tensor(out=tB[ts_][:], in0=tB[ts_][:], in1=G1row[:, d0:d0 + 512], op=ALU.mult),
                          reads=[("tB", ts_), "G1row"], writes=[("tB", ts_)])
                    tk.op("pool", lambda e, d0=d0, ts_=ts_: e.tensor_tensor(out=x1[s][:, 0, d0:d0 + 512], in0=tB[ts_][:], in1=xt[s][:, d0:d0 + 512], op=ALU.add),
                          reads=[("tB", ts_), ("xt", s)], writes=[(("x1", s), 0)])

            def back(n):
                s = n % 2
                tk.dma("act", x1_d[n * 128:(n + 1) * 128, :], x1[s][:, 0, :], reads=[(("x1", s), 0)], writes=[], skey=("x1", s))
                norm_transpose(x1[s], ("x1", s), 128, 1, A2, B2, h2T[s], lambda kc, s=s: ("h2T", s), 0, xh, "xh", junk, ssq, 0)
                tk.dma("act", h2_d[:, :, n * 128:(n + 1) * 128].rearrange("k p t -> p k t"), h2T[s][:], reads=[("h2T", s)], writes=[], skey=("h2T", s))

            rr = {}
            for n in range(NOB + 2):
                if n < NOB:
                    rr[n] = front(n)
                if 1 <= n <= NOB:
                    front2(n - 1, *rr[n - 1])
                if n >= 2:
                    back(n - 2)
        tk.barrier()
        if stop_after <= 5:
            return _finish(nc, tk, out, es, [])

        with contextlib.ExitStack() as p6:
            h2 = sb("h2", [128, KC, 512], BF16, p6)
            aT = sb("aT", [128, 64, 512], BF16, p6)
            w1b = [sb("w1b%d" % i, [128, KC, 512], BF16, p6) for i in range(2)]
            w2b = [sb("w2b%d" % i, [128, 16, 512], BF16, p6) for i in range(2)]
            rl = [sb("rl%d" % i, [128, 512], F32, p6) for i in range(2)]
            ytg = sb("ytg", [128, 4, 512], F32, p6)
            yo2 = [sb("yo2_%d" % i, [128, 512], F32, p6) for i in range(2)]
            w1v = w_ff1.rearrange("(kc p) n -> p kc n", p=128)
            w2v = w_ff2.rearrange("(fc p) n -> p fc n", p=128)
            cnt1 = 0
            cnt2 = 0
            cnt3 = 0
            for tt in range(8):
                tk.dma("sp", h2[:], h2_d[:, :, tt * 512:(tt + 1) * 512].rearrange("k p t -> p k t"), writes=["h2"], skey="h2")
                for fg in range(16):
                    s = cnt1 % 2
                    tk.dma("pool", w1b[s][:], w1v[:, :, fg * 512:(fg + 1) * 512], writes=[("w1b", s)], skey=("w1b", s))
                    for m in range(4):
                        f = fg * 4 + m
                        pb = cnt1 * 4 % 4 + m
                        for k in range(KC):
                            tk.op("pe", lambda e, k=k, m=m, s=s, pb=pb: e.matmul(PS[pb][:], lhsT=w1b[s][:, k, m * 128:(m + 1) * 128], rhs=h2[:, k, :],
                                                                               start=(k == 0), stop=(k == KC - 1)),
                                  reads=[("w1b", s), "h2"], writes=[pk(pb)])
                        rs_ = f % 2
                        tk.op("act", lambda e, pb=pb, rs_=rs_: e.activation(out=rl[rs_][:], in_=PS[pb][:], func=AF.Relu), reads=[pk(pb)], writes=[("rl", rs_)])
                        tk.op("dve", lambda e, f=f, rs_=rs_: e.tensor_tensor(out=aT[:, f, :], in0=rl[rs_][:], in1=rl[rs_][:], op=ALU.mult),
                              reads=[("rl", rs_)], writes=[("aT", f)])
                    cnt1 += 1
                for dg in range(4):
                    for fq in range(4):
                        s = cnt2 % 2
                        cnt2 += 1
                        tk.dma("pool", w2b[s][:], w2v[:, fq * 16:(fq + 1) * 16, dg * 512:(dg + 1) * 512], writes=[("w2b", s)], skey=("w2b", s))
                        for dc in range(4):
                            for fi in range(16):
                                f = fq * 16 + fi
                                tk.op("pe", lambda e, dc=dc, fi=fi, f=f, s=s: e.matmul(PS[4 + dc][:], lhsT=w2b[s][:, fi, dc * 128:(dc + 1) * 128], rhs=aT[:, f, :],
                                                                                     start=(f == 0), stop=(f == 63)),
                                      reads=[("w2b", s), ("aT", f)], writes=[pk(4 + dc)])
                    for dc in range(4):
                        tk.op("dve", lambda e, dc=dc, dg=dg: e.tensor_scalar(out=ytg[:, dc, :], in0=PS[4 + dc][:], scalar1=G2[:, dg * 4 + dc:dg * 4 + dc + 1],
                                                                           scalar2=None, op0=ALU.mult),
                              reads=[pk(4 + dc), "G2"], writes=[("ytg", dc)])
                    for blk in range(4):
                        pb = cnt3 % 4
                        ys = cnt3 % 2
                        cnt3 += 1
                        for dc in range(4):
                            tk.op("pe", lambda e, dc=dc, blk=blk, pb=pb: e.transpose(out=PS[pb][:, dc * 128:(dc + 1) * 128], in_=ytg[:, dc, blk * 128:(blk + 1) * 128],
                                                                                   identity=ident_f[:]),
                                  reads=[("ytg", dc), "ident_f"], writes=[pk(pb)])
                        tk.op("act", lambda e, pb=pb, ys=ys: e.activation(out=yo2[ys][:], in_=PS[pb][:], func=AF.Copy), reads=[pk(pb)], writes=[("yo2", ys)])
                        r0 = (tt * 4 + blk) * 128
                        tk.dma("act", y2_d[r0:r0 + 128, dg * 512:(dg + 1) * 512], yo2[ys][:], reads=[("yo2", ys)], writes=[], skey=("yo2", ys))
        tk.barrier()
        if stop_after <= 6:
            return _finish(nc, tk, out, es, [])

        with contextlib.ExitStack() as p7:
            xa = [sb("xa%d" % i, [128, D], F32, p7) for i in range(3)]
            yb = [sb("yb%d" % i, [128, D], F32, p7) for i in range(3)]
            ot = [sb("ot%d" % i, [128, D], F32, p7) for i in range(3)]
            nfg = sb("nfg", [128, D], F32, p7)
            junk = sb("junk7", [128, D], F32, p7)
            st = sb("st7", [128, 8], F32, p7)
            tk.dma("sp", nfg[:], nfg_row, writes=["nfg"], skey="nfg")
            for n in range(NOB):
                s = n % 3
                tk.dma("sp", xa[s][:], x1_d[n * 128:(n + 1) * 128, :], writes=[("xa", s)], skey=("xa", s))
                tk.dma("sp", yb[s][:], y2_d[n * 128:(n + 1) * 128, :], writes=[("yb", s)], skey=("yb", s))
                tk.op("dve", lambda e: e.tensor_tensor(out=xa[s][:], in0=xa[s][:], in1=yb[s][:], op=ALU.add), reads=[("xa", s), ("yb", s)], writes=[("xa", s)])
                tk.op("act", lambda e: e.activation(out=junk[:], in_=xa[s][:], func=AF.Square, accum_out=st[:, 0:1]), reads=[("xa", s)], writes=["junk", "st0"])
                tk.op("dve", lambda e: e.tensor_scalar(out=st[:, 1:2], in0=st[:, 0:1], scalar1=1.0 / D, scalar2=EPS, op0=ALU.mult, op1=ALU.add),
                      reads=["st0"], writes=["st1"])
                tk.op("act", lambda e: e.activation(out=st[:, 2:3], in_=st[:, 1:2], func=AF.Sqrt), reads=["st1"], writes=["st2"])
                tk.op("dve", lambda e: e.reciprocal(out=st[:, 3:4], in_=st[:, 2:3]), reads=["st2"], writes=["st3"])
                tk.op("act", lambda e: e.activation(out=ot[s][:], in_=xa[s][:], func=AF.Copy, scale=st[:, 3:4]), reads=[("xa", s), "st3"], writes=[("ot", s)])
                tk.op("dve", lambda e: e.tensor_tensor(out=ot[s][:], in0=ot[s][:], in1=nfg[:], op=ALU.mult), reads=[("ot", s), "nfg"], writes=[("ot", s)])
                tk.dma("act", out[n * 128:(n + 1) * 128, :], ot[s][:], reads=[("ot", s)], writes=[], skey=("ot", s))
        return _finish(nc, tk, out, es, [])


def _finish(nc, tk, out, es, keys):
    tk.final_wait("sp", keys)
    for e in tk.eng.values():
        if e["name"] != "sp" and e["cnt"] > 0:
            tk.eng["sp"]["h"].wait_ge(e["sem"], e["cnt"])
    for s in list(tk.dsem.values()) + tk.free_sems:
        if s["cnt"] > 0:
            tk.eng["sp"]["h"].wait_ge(s["sem"], s["cnt"])
    return nc


def _const_tables(hf):
    inv = (10000.0 ** (-np.arange(0, 128, 2, dtype=np.float32) / 128)).astype(np.float32)
    ang = np.arange(S, dtype=np.float32)[:, None] * inv[None, :]
    cos = np.cos(ang).astype(np.float32).T
    sin = np.sin(ang).astype(np.float32).T
    cosT = np.concatenate([cos, cos], 0)
    sinT = np.concatenate([-sin, sin], 0)
    own_pos = np.concatenate([np.arange(128) + 128 * (2 * n + hf) for n in range(NOB)])
    sc = np.float32(128 ** -0.5)
    t = {}
    t["cos_ctx"] = np.ascontiguousarray(cosT)
    t["sin_ctx"] = np.ascontiguousarray(sinT)
    t["cosq_own"] = np.ascontiguousarray(cosT[:, own_pos] * sc)
    t["sinq_own"] = np.ascontiguousarray(sinT[:, own_pos] * sc)
    key = np.arange(S)
    t["E_in"] = (key[None, :] // 64 == np.arange(128)[:, None]).astype(np.float32).astype(BF)
    t["ident_f_in"] = np.eye(128, dtype=np.float32)
    t["ident_b_in"] = np.eye(128, dtype=np.float32).astype(BF)
    ci = np.arange(512)[:, None]
    sj = np.arange(128)[None, :]
    C = ((ci * 16 <= sj * 64 + 63) & (ci * 16 + 31 >= sj * 64) & (ci < 511)).astype(np.float32)
    t["C_in"] = np.ascontiguousarray(C.reshape(4, 128, 128).transpose(1, 0, 2)).astype(BF)
    kl = np.arange(128)[:, None]
    ql = np.arange(128)[None, :]
    cmpb = np.zeros((NOB, 128, 2, 128), np.float32)
    vm = np.zeros((NOB, 128, 128), np.float32)
    bv = np.zeros((NOB, 128, 128), np.float32)
    for n in range(NOB):
        tq = 128 * (2 * n + hf) + ql
        for j in range(2):
            c = n // 8 - 1 + j
            i = 128 * c + kl
            valid = (c >= 0) & (16 * i + 31 <= tq) & (i <= 510)
            cmpb[n, :, j, :] = np.where(valid, 0.0, NEG)
        tcol = (128 * (2 * n + hf) + np.arange(128))[:, None]
        jb = np.arange(128)[None, :]
        valid = (jb * 64 <= tcol)
        cur = tcol // 64
        forced = (jb == 0) | (jb == cur) | (jb == cur - 1)
        vmn = valid.astype(np.float32)
        vm[n] = vmn
        bv[n] = vmn * 1e4 * forced + (vmn - 1.0)
    t["cmpbias_in"] = cmpb.astype(BF)
    t["vm_in"] = vm
    t["bv_in"] = bv
    wb = np.zeros((128, 6, 128), np.float32)
    for w in range(6):
        off = 128 * (w - 4 - hf)
        valid = (off + kl <= ql) & (off + kl > ql - 512)
        wb[:, w, :] = np.where(valid, 0.0, NEG)
    t["winbias_in"] = wb.astype(BF)
    db = np.zeros((128, 2, 128), np.float32)
    for j in range(2):
        valid = (128 * (j - hf) + kl <= ql)
        db[:, j, :] = np.where(valid, 0.0, NEG)
    t["diagbias_in"] = db.astype(BF)
    hm = np.ones((128, NHALO), np.float32)
    if hf == 0:
        hm[:, 0:2] = 0.0
    t["halomask_in"] = hm
    return t


def _swap_halves(cols):
    c = cols.reshape(-1, 2, 64)
    return c[:, ::-1, :].reshape(-1)


def prep_inputs(inp):
    f = lambda a: np.ascontiguousarray(np.asarray(a, dtype=np.float32))
    x = f(inp["x"])
    w_in = f(inp["w_in"])[0]
    cu = np.arange(5656)
    ub, uc, uh, q = cu[0:1024], cu[1024:2048], cu[2048:3072], cu[3072:4096]
    kc, vc, ksl, vsl, kwn, vwn, gl = (cu[4096:4352], cu[4352:4608], cu[4608:4864], cu[4864:5120],
                                      cu[5120:5376], cu[5376:5632], cu[5632:5656])
    kv_cols = np.concatenate([kc, _swap_halves(kc), ksl, _swap_halves(ksl), kwn, _swap_halves(kwn), vc, vsl, vwn])
    own_cols = []
    for ch in range(8):
        own_cols += [ub[ch * 128:(ch + 1) * 128], uc[ch * 128:(ch + 1) * 128], uh[ch * 128:(ch + 1) * 128]]
    for h in range(8):
        qh = q[h * 128:(h + 1) * 128]
        own_cols += [qh, _swap_halves(qh)]
    own_cols = np.concatenate(own_cols)
    fm = lambda v, n: np.ascontiguousarray(f(v).reshape(n, 128).T)
    shared = {
        "w_ada": f(inp["w_ada"])[0],
        "b_ada_fm": fm(inp["b_ada"][0], 96),
        "n1g_fm": fm(inp["norm1_g"][0], 16),
        "n2g_fm": fm(inp["norm2_g"][0], 16),
        "w_kv": np.ascontiguousarray(w_in[:, kv_cols]),
        "w_own": np.ascontiguousarray(w_in[:, own_cols]),
        "w_g": np.ascontiguousarray(w_in[:, gl]),
        "convw_fm": np.ascontiguousarray(f(inp["conv_w"])[0].reshape(3, 8, 128).transpose(2, 1, 0)),
        "convb_fm": fm(inp["conv_b"][0], 8),
        "pe_k": f(inp["cmp_pe_k"])[0], "pe_v": f(inp["cmp_pe_v"])[0],
        "w1_k": f(inp["cmp_w1_k"])[0], "w1_v": f(inp["cmp_w1_v"])[0],
        "w2_k": f(inp["cmp_w2_k"])[0], "w2_v": f(inp["cmp_w2_v"])[0],
        "gcg_fm": fm(inp["gnorm_conv_g"][0], 8), "gag_fm": fm(inp["gnorm_attn_g"][0], 8),
        "w_out": f(inp["w_out"])[0], "w_ff1": f(inp["w_ff1"])[0], "w_ff2": f(inp["w_ff2"])[0],
        "nfg_row": np.ascontiguousarray(np.broadcast_to(f(inp["normf_g"])[None, :], (128, D))),
    }
    tabs = [_const_tables(0), _const_tables(1)]
    c = f(inp["c"])
    in_maps = []
    for core in range(8):
        b, hf = core // 2, core % 2
        xb = x[b]
        own_pos = np.concatenate([np.arange(128) + 128 * (2 * n + hf) for n in range(NOB)])
        halo_pos = np.concatenate([np.array([128 * (2 * n + hf) - 2, 128 * (2 * n + hf) - 1]) for n in range(NOB)])
        xo = np.empty((NOWN + NHALO, D), np.float32)
        xo[:NOWN] = xb[own_pos]
        xo[NOWN:] = xb[np.maximum(halo_pos, 0)]
        m = dict(shared)
        m.update(tabs[hf])
        m["x_ctx"] = xb
        m["x_own"] = xo
        m["c_fm"] = np.ascontiguousarray(c[b].reshape(16, 128).T)
        in_maps.append(m)
    return in_maps


def kernel(**inputs):
    in_maps = prep_inputs(inputs)
    nc = build_nc()
    res = run_bass_kernel_spmd(nc, in_maps, core_ids=list(range(8)))
    outf = np.empty((4, S, D), np.float32)
    for core in range(8):
        b, hf = core // 2, core % 2
        o = np.asarray(res.results[core]["out"]).reshape(NOB, 128, D)
        outf[b].reshape(64, 128, D)[hf::2] = o
    return outf
```

```python
import contextlib
import numpy as np
import ml_dtypes
import concourse.bass as bass
import concourse.mybir as mybir
from concourse.bass_utils import run_bass_kernel_spmd

F32 = mybir.dt.float32
BF16 = mybir.dt.bfloat16
AF = mybir.ActivationFunctionType
ALU = mybir.AluOpType
BF = ml_dtypes.bfloat16

D = 2048
S = 8192
NOB = 32
NOWN = 4096
NHALO = 64
DFF = 8192
NEG = -30000.0
EPS = 1e-6
KC = 16


class TK:
    def __init__(self, nc, es):
        self.nc, self.es = nc, es
        self.eng = {}
        for nm, h in (("pe", nc.tensor), ("act", nc.scalar), ("dve", nc.vector),
                      ("pool", nc.gpsimd), ("sp", nc.sync)):
            self.eng[nm] = dict(h=h, sem=es.enter_context(nc.semaphore("se_" + nm)), cnt=0,
                                seen={}, name=nm)
        self.lastw = {}
        self.readers = {}
        self.dsem = {}
        self.free_sems = []
        self.nsd = 0

    def _deps(self, reads, writes):
        d = []
        for k in reads:
            t = self.lastw.get(k)
            if t:
                d.append(t)
        for k in writes:
            t = self.lastw.get(k)
            if t:
                d.append(t)
            d.extend(self.readers.get(k, {}).values())
        return d

    def _wait(self, e, deps):
        for (sid, sem, val, src) in deps:
            if src == "pe" and e["name"] == "pe":
                continue
            if e["seen"].get(sid, 0) >= val:
                continue
            e["h"].wait_ge(sem, val)
            e["seen"][sid] = val

    def _commit(self, tok, reads, writes):
        for k in writes:
            self.lastw[k] = tok
            self.readers[k] = {}
        for k in reads:
            self.readers.setdefault(k, {})[tok[0]] = tok

    def op(self, en, fn, reads=(), writes=()):
        e = self.eng[en]
        self._wait(e, self._deps(reads, writes))
        inst = fn(e["h"])
        e["cnt"] += 1
        inst.then_inc(e["sem"], 1)
        self._commit((en, e["sem"], e["cnt"], en), reads, writes)

    def dma(self, qn, out, in_, reads=(), writes=(), skey=None):
        e = self.eng[qn]
        self._wait(e, self._deps(reads, writes))
        if skey not in self.dsem:
            if self.free_sems:
                self.dsem[skey] = self.free_sems.pop()
            else:
                nm = "sd%d" % self.nsd
                self.nsd += 1
                self.dsem[skey] = dict(sem=self.es.enter_context(self.nc.semaphore(nm)), cnt=0, name=nm)
        s = self.dsem[skey]
        s["cnt"] += 16
        e["h"].dma_start(out=out, in_=in_).then_inc(s["sem"], 16)
        self._commit((s["name"], s["sem"], s["cnt"], "dma"), reads, writes)

    def barrier(self):
        allsems = list(self.dsem.values()) + self.free_sems
        for e in self.eng.values():
            for o in self.eng.values():
                if o is not e and o["cnt"] > 0 and e["seen"].get(o["name"], 0) < o["cnt"]:
                    e["h"].wait_ge(o["sem"], o["cnt"])
                    e["seen"][o["name"]] = o["cnt"]
            for s_ in allsems:
                if s_["cnt"] > 0 and e["seen"].get(s_["name"], 0) < s_["cnt"]:
                    e["h"].wait_ge(s_["sem"], s_["cnt"])
                    e["seen"][s_["name"]] = s_["cnt"]
        self.free_sems = allsems
        self.dsem = {}
        self.lastw = {}
        self.readers = {}

    def final_wait(self, en, keys):
        e = self.eng[en]
        self._wait(e, self._deps(keys, ()))


def build_nc(stop_after=99, debug=(), p1_tiles=16):
    nc = bass.Bass("TRN2", target_bir_lowering=False)

    def din(name, shape, dt=F32):
        return nc.dram_tensor(name, list(shape), dt, kind="ExternalInput").ap()

    def dscr(name, shape, dt, out=False):
        kind = "ExternalOutput" if (out or name in debug) else "Internal"
        return nc.dram_tensor(name, list(shape), dt, kind=kind).ap()

    x_ctx = din("x_ctx", [S, D])
    x_own = din("x_own", [NOWN + NHALO, D])
    c_fm = din("c_fm", [128, KC])
    w_ada = din("w_ada", [D, 6 * D])
    b_ada_fm = din("b_ada_fm", [128, 96])
    n1g_fm = din("n1g_fm", [128, KC])
    n2g_fm = din("n2g_fm", [128, KC])
    w_kv = din("w_kv", [D, 2304])
    w_own = din("w_own", [D, 5120])
    w_g = din("w_g", [D, 24])
    convw_fm = din("convw_fm", [128, 8, 3])
    convb_fm = din("convb_fm", [128, 8])
    pe_k = din("pe_k", [32, 128])
    pe_v = din("pe_v", [32, 128])
    w1_k = din("w1_k", [4096, 256])
    w1_v = din("w1_v", [4096, 256])
    w2_k = din("w2_k", [256, 128])
    w2_v = din("w2_v", [256, 128])
    gcg_fm = din("gcg_fm", [128, 8])
    gag_fm = din("gag_fm", [128, 8])
    w_out = din("w_out", [D, D])
    w_ff1 = din("w_ff1", [D, DFF])
    w_ff2 = din("w_ff2", [DFF, D])
    nfg_row = din("nfg_row", [128, D])
    cos_ctx = din("cos_ctx", [128, S])
    sin_ctx = din("sin_ctx", [128, S])
    cosq_own = din("cosq_own", [128, NOWN])
    sinq_own = din("sinq_own", [128, NOWN])
    E_in = din("E_in", [128, S], BF16)
    ident_f_in = din("ident_f_in", [128, 128])
    ident_b_in = din("ident_b_in", [128, 128], BF16)
    C_in = din("C_in", [128, 4, 128], BF16)
    cmpbias_in = din("cmpbias_in", [NOB, 128, 2, 128], BF16)
    winbias_in = din("winbias_in", [128, 6, 128], BF16)
    diagbias_in = din("diagbias_in", [128, 2, 128], BF16)
    vm_in = din("vm_in", [NOB, 128, 128])
    bv_in = din("bv_in", [NOB, 128, 128])
    halomask_in = din("halomask_in", [128, NHALO])

    out = nc.dram_tensor("out", [NOWN, D], F32, kind="ExternalOutput").ap()

    kslc_d = dscr("kslc_d", [2, 128, S], BF16)
    kwin_d = dscr("kwin_d", [2, 128, S], BF16)
    kc_d = dscr("kc_d", [2, 128, S], BF16)
    vc_d = dscr("vc_d", [2, 128, S], BF16)
    vslc_d = dscr("vslc_d", [2, 128, 64, 129], BF16)
    vwin_d = dscr("vwin_d", [2, 128, 64, 129], BF16)
    q_d = dscr("q_d", [NOB, 128, 8, 128], BF16)
    yc_d = dscr("yc_d", [8, 128, NOWN], BF16)
    ya_d = dscr("ya_d", [NOB, 128, 1024], F32)
    x1_d = dscr("x1_d", [NOWN, D], F32)
    h2_d = dscr("h2_d", [KC, 128, NOWN], BF16)
    y2_d = dscr("y2_d", [NOWN, D], F32)
    w1s_d = dscr("w1s_d", [D, DFF], BF16)
    w2s_d = dscr("w2s_d", [DFF, D], BF16)
    dbg_d = dscr("dbg_d", [128, 4096], F32, out=True) if "dbg_d" in debug else None

    with contextlib.ExitStack() as es:
        tk = TK(nc, es)

        def sb(name, shape, dt, stack=es):
            return stack.enter_context(nc.sbuf_tensor(name, list(shape), dt))

        PS = [es.enter_context(nc.psum_tensor("ps%d" % i, [128, 512], F32)) for i in range(8)]

        def pk(i):
            return ("ps", i)

        def dump(name, tile, shape, dt, keys):
            if name not in debug:
                return
            d = nc.dram_tensor(name, list(shape), dt, kind="ExternalOutput").ap()
            tk.dma("sp", d, tile, reads=keys, writes=[name], skey=name)

        ident_f = sb("ident_f", [128, 128], F32)
        ident_b = sb("ident_b", [128, 128], BF16)
        ones_f = sb("ones_f", [128, 128], F32)
        ones_b = sb("ones_b", [128, 128], BF16)
        A1 = sb("A1", [128, KC], F32)
        B1 = sb("B1", [128, KC], F32)
        A2 = sb("A2", [128, KC], F32)
        B2 = sb("B2", [128, KC], F32)
        G2 = sb("G2", [128, KC], F32)
        G1row = sb("G1row", [128, D], F32)
        gates_sb = sb("gates_sb", [128, NOB, 24], F32)
        kcmpT = sb("kcmpT", [128, 2, 512], BF16)
        vcmp = sb("vcmp", [128, 2, 4, 257], BF16)
        small = sb("small", [128, 64], F32)

        tk.dma("sp", ident_f[:], ident_f_in, writes=["ident_f"], skey="ident_f")
        tk.dma("sp", ident_b[:], ident_b_in, writes=["ident_b"], skey="ident_b")
        tk.op("dve", lambda e: e.memset(ones_f[:], 1.0), writes=["ones_f"])
        tk.op("dve", lambda e: e.memset(ones_b[:], 1.0), writes=["ones_b"])

        def norm_transpose(xt, xkey, npart, nblk, Asc, Bsc, hT, hkeyf, col0, xh, xhkey, junk, ssq, pb0, do_transposes=True):
            for blk in range(nblk):
                tk.op("act", lambda e, blk=blk: e.activation(out=xh[:npart, blk, :], in_=xt[:npart, blk, :], func=AF.Square,
                                                            accum_out=ssq[:npart, blk:blk + 1]),
                      reads=[(xkey, blk)], writes=[(xhkey, blk), ("ssq", blk)])
                tk.op("dve", lambda e, blk=blk: e.tensor_scalar(out=ssq[:npart, 8 + blk:9 + blk], in0=ssq[:npart, blk:blk + 1],
                                                               scalar1=1.0 / D, scalar2=EPS, op0=ALU.mult, op1=ALU.add),
                      reads=[("ssq", blk)], writes=[("ssq2", blk)])
                tk.op("act", lambda e, blk=blk: e.activation(out=ssq[:npart, 16 + blk:17 + blk], in_=ssq[:npart, 8 + blk:9 + blk],
                                                            func=AF.Sqrt),
                      reads=[("ssq2", blk)], writes=[("ssq3", blk)])
                tk.op("dve", lambda e, blk=blk: e.reciprocal(out=ssq[:npart, 24 + blk:25 + blk], in_=ssq[:npart, 16 + blk:17 + blk]),
                      reads=[("ssq3", blk)], writes=[("rstd", blk)])
                tk.op("act", lambda e, blk=blk: e.activation(out=xh[:npart, blk, :], in_=xt[:npart, blk, :], func=AF.Copy,
                                                            scale=ssq[:npart, 24 + blk:25 + blk]),
                      reads=[(xkey, blk), ("rstd", blk)], writes=[(xhkey, blk)])
            if not do_transposes:
                return
            for kc in range(KC):
                transpose_piece(kc, npart, nblk, Asc, Bsc, hT, hkeyf, col0, xh, xhkey, pb0)

        def transpose_piece(kc, npart, nblk, Asc, Bsc, hT, hkeyf, col0, xh, xhkey, pb0):
            ncol = nblk * npart
            if True:
                pb = pb0 + (kc % 2)
                for blk in range(nblk):
                    tk.op("pe", lambda e, kc=kc, blk=blk, pb=pb: e.transpose(
                        out=PS[pb][:, blk * npart:(blk + 1) * npart], in_=xh[:npart, blk, kc * 128:(kc + 1) * 128],
                        identity=ident_f[:npart, :npart]),
                        reads=[(xhkey, blk), "ident_f"], writes=[pk(pb)])
                tk.op("dve", lambda e, kc=kc, pb=pb: e.tensor_scalar(
                    out=hT[:, kc, col0:col0 + ncol], in0=PS[pb][:, 0:ncol], scalar1=Asc[:, kc:kc + 1], scalar2=Bsc[:, kc:kc + 1],
                    op0=ALU.mult, op1=ALU.add),
                    reads=[pk(pb), "AB"], writes=[hkeyf(kc)])

        with contextlib.ExitStack() as p0:
            cf = sb("cf", [128, KC], F32, p0)
            siluc = sb("siluc", [128, KC], F32, p0)
            modfm = sb("modfm", [128, 96], F32, p0)
            bada = sb("bada", [128, 96], F32, p0)
            n1g = sb("n1g", [128, KC], F32, p0)
            n2g = sb("n2g", [128, KC], F32, p0)
            diag = sb("diag", [128, 2, 128], F32, p0)
            wa = [sb("wa%d" % i, [128, KC, 1024], BF16, p0) for i in range(2)]
            silub = sb("silub", [128, KC], BF16, p0)
            tk.dma("sp", cf[:], c_fm, writes=["cf"], skey="cf")
            tk.dma("sp", bada[:], b_ada_fm, writes=["bada"], skey="bada")
            tk.dma("sp", n1g[:], n1g_fm, writes=["n1g"], skey="n1g")
            tk.dma("sp", n2g[:], n2g_fm, writes=["n2g"], skey="n2g")
            tk.op("act", lambda e: e.activation(out=siluc[:], in_=cf[:], func=AF.Silu), reads=["cf"], writes=["siluc"])
            tk.op("act", lambda e: e.activation(out=silub[:], in_=cf[:], func=AF.Silu), reads=["cf"], writes=["silub"])
            wav = w_ada.rearrange("(kc p) n -> p kc n", p=128)
            for G in range(12):
                s = G % 2
                tk.dma("pool", wa[s][:], wav[:, :, G * 1024:(G + 1) * 1024], writes=[("wa", s)], skey=("wa", s))
                for fc in range(8):
                    col = G * 8 + fc
                    for k in range(KC):
                        tk.op("pe", lambda e, s=s, fc=fc, k=k, col=col: e.matmul(
                            PS[0][:, col:col + 1], lhsT=wa[s][:, k, fc * 128:(fc + 1) * 128], rhs=silub[:, k:k + 1],
                            start=(k == 0), stop=(k == KC - 1)),
                            reads=[("wa", s), "silub"], writes=[pk(0)])
            tk.op("dve", lambda e: e.tensor_tensor(out=modfm[:], in0=PS[0][:, 0:96], in1=bada[:], op=ALU.add),
                  reads=[pk(0), "bada"], writes=["modfm"])
            tk.op("dve", lambda e: e.tensor_copy(out=B1[:], in_=modfm[:, 0:16]), reads=["modfm"], writes=["B1"])
            tk.op("dve", lambda e: e.scalar_tensor_tensor(out=A1[:], in0=modfm[:, 16:32], scalar=1.0, in1=n1g[:], op0=ALU.add, op1=ALU.mult),
                  reads=["modfm", "n1g"], writes=["A1"])
            tk.op("dve", lambda e: e.tensor_copy(out=B2[:], in_=modfm[:, 48:64]), reads=["modfm"], writes=["B2"])
            tk.op("dve", lambda e: e.scalar_tensor_tensor(out=A2[:], in0=modfm[:, 64:80], scalar=1.0, in1=n2g[:], op0=ALU.add, op1=ALU.mult),
                  reads=["modfm", "n2g"], writes=["A2"])
            tk.op("dve", lambda e: e.tensor_copy(out=G2[:], in_=modfm[:, 80:96]), reads=["modfm"], writes=["G2", "AB"])
            for kc in range(KC):
                s = kc % 2
                tk.op("dve", lambda e, kc=kc, s=s: e.tensor_scalar(out=diag[:, s, :], in0=ident_f[:], scalar1=modfm[:, 32 + kc:33 + kc],
                                                                 scalar2=None, op0=ALU.mult),
                      reads=["ident_f", "modfm"], writes=[("diag", s)])
                pb = 1 + (kc // 4) % 2
                tk.op("pe", lambda e, kc=kc, s=s, pb=pb: e.matmul(PS[pb][:, (kc % 4) * 128:(kc % 4 + 1) * 128], lhsT=ones_f[:], rhs=diag[:, s, :],
                                                                 start=True, stop=True),
                      reads=["ones_f", ("diag", s)], writes=[pk(pb)])
                if kc % 4 == 3:
                    tk.op("act", lambda e, kc=kc, pb=pb: e.activation(out=G1row[:, (kc - 3) * 128:(kc + 1) * 128], in_=PS[pb][:], func=AF.Copy),
                          reads=[pk(pb)], writes=["G1row"])
            dump("dbg_modfm", modfm[:], [128, 96], F32, ["modfm"])
            dump("dbg_A1", A1[:], [128, KC], F32, ["A1"])
            dump("dbg_G1row", G1row[:], [128, D], F32, ["G1row"])
            dump("dbg_siluc", siluc[:], [128, KC], F32, ["siluc"])
        tk.barrier()
        if stop_after <= 0:
            return _finish(nc, tk, out, es, ["G1row", "AB"])

        with contextlib.ExitStack() as p1:
            wkv = sb("wkv", [128, KC, 2304], BF16, p1)
            xraw = sb("xraw1", [128, 4, D], F32, p1)
            xh = sb("xh1", [128, 4, D], F32, p1)
            hT = [sb("hT1_%d" % i, [128, KC, 512], BF16, p1) for i in range(2)]
            junk = None
            ssq = sb("ssq1", [128, 32], F32, p1)
            cst = [sb("cst1_%d" % i, [128, 512], F32, p1) for i in range(2)]
            snt = [sb("snt1_%d" % i, [128, 512], F32, p1) for i in range(2)]
            t1 = sb("t1_1", [128, 512], F32, p1)
            t2 = sb("t2_1", [128, 512], F32, p1)
            ko = [sb("ko1_%d" % i, [128, 512], BF16, p1) for i in range(2)]
            vaug = [sb("vaug1_%d" % i, [128, 4, 129], BF16, p1) for i in range(2)]
            wkvv = w_kv.rearrange("(kc p) n -> p kc n", p=128)
            for j in range(0, 2304, 384):
                tk.dma("pool", wkv[:, :, j:j + 384], wkvv[:, :, j:j + 384], writes=[("wkv", j)], skey=("wkv", j))
            wkv_keys = [("wkv", j) for j in range(0, 2304, 384)]
            for i in range(2):
                tk.op("dve", lambda e, i=i: e.memset(vaug[i][:, :, 128:129], 1.0), writes=[("vaug", i)])
            xcv = x_ctx.rearrange("(cb p) d -> cb p d", p=128)
            kout = 0
            def front1(tt):
                hs_ = tt % 2
                for blk in range(4):
                    tk.dma("sp", xraw[:, blk, :], xcv[tt * 4 + blk], writes=[("xraw", blk)], skey=("xraw", blk))
                tk.dma("sp", cst[hs_][:], cos_ctx[:, tt * 512:(tt + 1) * 512], writes=[("cst", hs_)], skey=("cst", hs_))
                tk.dma("sp", snt[hs_][:], sin_ctx[:, tt * 512:(tt + 1) * 512], writes=[("snt", hs_)], skey=("snt", hs_))
                norm_transpose(xraw, "xraw", 128, 4, A1, B1, hT[hs_], lambda kc, hs_=hs_: ("hT", hs_, kc), 0, xh, "xh", junk, ssq, 0,
                               do_transposes=False)
                pend = [(lambda kc=kc, hs_=hs_: transpose_piece(kc, 128, 4, A1, B1, hT[hs_], lambda kc2, hs_=hs_: ("hT", hs_, kc2), 0, xh, "xh", 0))
                        for kc in range(KC)]
                if tt == 0:
                    for f_ in pend:
                        f_()
                    pend = []
                pending[0] = pend
                if tt == 0:
                    dump("dbg_ssq", ssq[:], [128, 32], F32, [("rstd", b_) for b_ in range(4)])
                    dump("dbg_xh", xh[:, 0, :], [128, D], F32, [("xh", 0)])
                    dump("dbg_hT", hT[0][:], [128, KC, 512], BF16, [("hT", 0, kc) for kc in range(KC)])

            pending = [[]]
            ncall = [0]

            def maybe_piece():
                ncall[0] += 1
                if ncall[0] >= 3 and pending[0]:
                    pending[0].pop(0)()

            def back1(tt):
                nonlocal kout
                hs_ = tt % 2
                ncall[0] = 0
                for kind, dst in ((0, kc_d), (2, kslc_d), (4, kwin_d)):
                    for g in range(2):
                        m_main = kind * 2 + g
                        m_sw = (kind + 1) * 2 + g
                        pa, pb_ = 2 + 2 * (kout % 3), 3 + 2 * (kout % 3)
                        for (m, pbk) in ((m_main, pa), (m_sw, pb_)):
                            for k in range(KC):
                                tk.op("pe", lambda e, m=m, pbk=pbk, k=k: e.matmul(PS[pbk][:], lhsT=wkv[:, k, m * 128:(m + 1) * 128], rhs=hT[hs_][:, k, :],
                                                                                 start=(k == 0), stop=(k == KC - 1)),
                                      reads=wkv_keys + [("hT", hs_, k)], writes=[pk(pbk)])
                            maybe_piece()
                        tk.op("dve", lambda e, pa=pa: e.tensor_tensor(out=t1[:], in0=PS[pa][:], in1=cst[hs_][:], op=ALU.mult),
                              reads=[pk(pa), ("cst", hs_)], writes=["t1"])
                        tk.op("dve", lambda e, pb_=pb_: e.tensor_tensor(out=t2[:], in0=PS[pb_][:], in1=snt[hs_][:], op=ALU.mult),
                              reads=[pk(pb_), ("snt", hs_)], writes=["t2"])
                        ks = kout % 2
                        tk.op("pool", lambda e, ks=ks: e.tensor_tensor(out=ko[ks][:], in0=t1[:], in1=t2[:], op=ALU.add),
                              reads=["t1", "t2"], writes=[("ko", ks)])
                        tk.dma("act", dst[g, :, tt * 512:(tt + 1) * 512], ko[ks][:], reads=[("ko", ks)],
                               writes=[(dst.tensor.name, g, tt)], skey=("ko", ks))
                        kout += 1
                for g in range(2):
                    m = 12 + g
                    pa = 2 + 2 * (kout % 3)
                    for k in range(KC):
                        tk.op("pe", lambda e, m=m, pa=pa, k=k: e.matmul(PS[pa][:], lhsT=wkv[:, k, m * 128:(m + 1) * 128], rhs=hT[hs_][:, k, :],
                                                                       start=(k == 0), stop=(k == KC - 1)),
                              reads=wkv_keys + [("hT", hs_, k)], writes=[pk(pa)])
                    maybe_piece()
                    ks = kout % 2
                    tk.op("act", lambda e, ks=ks, pa=pa: e.activation(out=ko[ks][:], in_=PS[pa][:], func=AF.Copy),
                          reads=[pk(pa)], writes=[("ko", ks)])
                    tk.dma("act", vc_d[g, :, tt * 512:(tt + 1) * 512], ko[ks][:], reads=[("ko", ks)],
                           writes=[("vc_d", g, tt)], skey=("ko", ks))
                    kout += 1
                for blk in range(4):
                    cb = tt * 4 + blk
                    pa = 2 + 2 * (kout % 3)
                    vs = kout % 2
                    for k in range(KC):
                        tk.op("pe", lambda e, pa=pa, k=k, blk=blk: e.matmul(PS[pa][:], lhsT=hT[hs_][:, k, blk * 128:(blk + 1) * 128], rhs=wkv[:, k, 1792:2304],
                                                                           start=(k == 0), stop=(k == KC - 1)),
                              reads=wkv_keys + [("hT", hs_, k)], writes=[pk(pa)])
                    maybe_piece()
                    tk.op("act", lambda e, vs=vs, pa=pa: e.activation(out=vaug[vs][:, :, 0:128], in_=PS[pa][:].rearrange("p (a b) -> p a b", a=4),
                                                                     func=AF.Copy),
                          reads=[pk(pa)], writes=[("vaug", vs)])
                    tk.dma("act", vslc_d[:, :, cb, :].rearrange("g p e -> p g e"), vaug[vs][:, 0:2, :], reads=[("vaug", vs)],
                           writes=[("vslc_d", cb)], skey=("vaug", vs))
                    tk.dma("act", vwin_d[:, :, cb, :].rearrange("g p e -> p g e"), vaug[vs][:, 2:4, :], reads=[("vaug", vs)],
                           writes=[("vwin_d", cb)], skey=("vaug", vs, 1))
                    kout += 1
            front1(0)
            for tt in range(p1_tiles):
                if tt + 1 < p1_tiles:
                    front1(tt + 1)
                back1(tt)
                while pending[0]:
                    pending[0].pop(0)()
        tk.barrier()
        if stop_after <= 1:
            return _finish(nc, tk, out, es, [])

        with contextlib.ExitStack() as p2:
            w1 = sb("w1c", [128, 32, 256], BF16, p2)
            w2 = sb("w2c", [128, 2, 128], BF16, p2)
            pet = sb("pet", [32, 128], F32, p2)
            peT = sb("peT", [128, 32], BF16, p2)
            src = sb("srcc", [128, S], BF16, p2)
            hb = sb("hbias", [128, 2], F32, p2)
            hx = sb("hx", [128, 512], F32, p2)
            hx2 = sb("hx2", [128, 512], F32, p2)
            hx3 = sb("hx3", [128, 512], F32, p2)
            gl = sb("gelu", [128, 2, 512], BF16, p2)
            tk.op("dve", lambda e: e.memset(gl[:], 0.0), writes=["gelu"])
            for g in range(2):
                tk.dma("sp", vcmp[:, g, :, 129:257], C_in, writes=[("vcmpC", g)], skey=("vcmpC", g))
            tk.op("dve", lambda e: e.memset(vcmp[:, :, :, 128:129], 1.0), writes=["vcmp1"])
            for (kv, w1_in, w2_in, pe_in, src_d) in ((0, w1_k, w2_k, pe_k, kc_d), (1, w1_v, w2_v, pe_v, vc_d)):
                tk.dma("pool", w1[:, :, :], w1_in.rearrange("(p d) h -> d p h", d=128), writes=["w1c"], skey="w1c")
                tk.dma("pool", w2[:, :, :], w2_in.rearrange("(a h) d -> h a d", h=128), writes=["w2c"], skey="w2c")
                tk.dma("sp", pet[:], pe_in, writes=["pet"], skey="pet")
                tk.op("pe", lambda e: e.transpose(out=PS[0][:, 0:32], in_=pet[:, :], identity=ident_f[:32, :32]),
                      reads=["pet", "ident_f"], writes=[pk(0)])
                tk.op("dve", lambda e: e.tensor_copy(out=peT[:], in_=PS[0][:, 0:32]), reads=[pk(0)], writes=["peT"])
                for hm in range(2):
                    for p in range(32):
                        tk.op("pe", lambda e, hm=hm, p=p: e.matmul(PS[1][:, hm:hm + 1], lhsT=w1[:, p, hm * 128:(hm + 1) * 128], rhs=peT[:, p:p + 1],
                                                                  start=(p == 0), stop=(p == 31)),
                              reads=["w1c", "peT"], writes=[pk(1)])
                tk.op("dve", lambda e: e.tensor_copy(out=hb[:], in_=PS[1][:, 0:2]), reads=[pk(1)], writes=["hb"])
                for g in range(2):
                    srck = [(src_d.tensor.name, g, tt) for tt in range(16)]
                    tk.dma("sp", src[:], src_d[g], reads=srck, writes=["srcc"], skey="srcc")
                    srcv = src[:].rearrange("p (i s) -> p i s", s=16)
                    for hm in range(2):
                        pb = 2 + hm
                        for p in range(32):
                            i0, sft = divmod(p, 16)
                            tk.op("pe", lambda e, hm=hm, p=p, pb=pb, i0=i0, sft=sft: e.matmul(
                                PS[pb][:, 0:511], lhsT=w1[:, p, hm * 128:(hm + 1) * 128], rhs=srcv[:, i0:i0 + 511, sft],
                                start=(p == 0), stop=(p == 31)),
                                reads=["w1c", "srcc"], writes=[pk(pb)])
                        tk.op("act", lambda e, hm=hm, pb=pb: e.activation(out=hx[:, 0:511], in_=PS[pb][:, 0:511], func=AF.Identity,
                                                                         bias=hb[:, hm:hm + 1], scale=1.0),
                              reads=[pk(pb), "hb"], writes=["hx"])
                        tk.op("act", lambda e: e.activation(out=hx2[:, 0:511], in_=hx[:, 0:511], func=AF.Square), reads=["hx"], writes=["hx2"])
                        tk.op("dve", lambda e: e.tensor_scalar(out=hx2[:, 0:511], in0=hx2[:, 0:511], scalar1=0.044715, scalar2=1.0,
                                                               op0=ALU.mult, op1=ALU.add), reads=["hx2"], writes=["hx2"])
                        tk.op("dve", lambda e: e.tensor_tensor(out=hx2[:, 0:511], in0=hx2[:, 0:511], in1=hx[:, 0:511], op=ALU.mult),
                              reads=["hx2", "hx"], writes=["hx2"])
                        tk.op("act", lambda e: e.activation(out=hx3[:, 0:511], in_=hx2[:, 0:511], func=AF.Tanh, scale=0.7978845608028654),
                              reads=["hx2"], writes=["hx3"])
                        tk.op("dve", lambda e: e.tensor_scalar(out=hx3[:, 0:511], in0=hx3[:, 0:511], scalar1=1.0, scalar2=0.5,
                                                               op0=ALU.add, op1=ALU.mult), reads=["hx3"], writes=["hx3"])
                        tk.op("dve", lambda e, hm=hm: e.tensor_tensor(out=gl[:, hm, 0:511], in0=hx3[:, 0:511], in1=hx[:, 0:511], op=ALU.mult),
                              reads=["hx3", "hx"], writes=["gelu"])
                    if kv == 0:
                        for hm in range(2):
                            tk.op("pe", lambda e, hm=hm: e.matmul(PS[4][:], lhsT=w2[:, hm, :], rhs=gl[:, hm, :], start=(hm == 0), stop=(hm == 1)),
                                  reads=["w2c", "gelu"], writes=[pk(4)])
                        tk.op("act", lambda e, g=g: e.activation(out=kcmpT[:, g, :], in_=PS[4][:], func=AF.Copy),
                              reads=[pk(4)], writes=["kcmpT"])
                    else:
                        for c in range(4):
                            for hm in range(2):
                                tk.op("pe", lambda e, hm=hm, c=c: e.matmul(PS[5][:, c * 128:(c + 1) * 128], lhsT=gl[:, hm, c * 128:(c + 1) * 128],
                                                                          rhs=w2[:, hm, :], start=(hm == 0), stop=(hm == 1)),
                                      reads=["w2c", "gelu"], writes=[pk(5)])
                        tk.op("act", lambda e, g=g: e.activation(out=vcmp[:, g, :, 0:128], in_=PS[5][:].rearrange("p (c d) -> p c d", c=4),
                                                                func=AF.Copy),
                              reads=[pk(5)], writes=["vcmp"])
        if True:
            if "dbg_cmp" in debug:
                dk = nc.dram_tensor("dbg_kcmp", [128, 2, 512], BF16, kind="ExternalOutput").ap()
                dv = nc.dram_tensor("dbg_vcmp", [128, 2, 4, 257], BF16, kind="ExternalOutput").ap()
                tk.dma("sp", dk, kcmpT[:], reads=["kcmpT"], writes=["dbgk"], skey="dbgk")
                tk.dma("sp", dv, vcmp[:], reads=["vcmp", ("vcmpC", 0), ("vcmpC", 1), "vcmp1"], writes=["dbgv"], skey="dbgv")
        tk.barrier()
        if stop_after <= 2:
            return _finish(nc, tk, out, es, [])
        def bc4(ap2d):
            return ap2d.rearrange("p (o n) -> p o n", o=1).broadcast_to([128, 4, 128])

        with contextlib.ExitStack() as p3:
            hTo = sb("hTo", [128, KC, NOWN + NHALO], BF16, p3)
            with contextlib.ExitStack() as p3x:
                xraw = sb("xraw3", [128, 2, D], F32, p3x)
                xh = sb("xh3", [128, 2, D], F32, p3x)
                junk = None
                ssq = sb("ssq3", [128, 32], F32, p3x)
                xov = x_own[0:NOWN, :].rearrange("(cb p) d -> cb p d", p=128)
                for tt in range(16):
                    for blk in range(2):
                        tk.dma("sp", xraw[:, blk, :], xov[tt * 2 + blk], writes=[("xraw", blk)], skey=("xraw", blk))
                    norm_transpose(xraw, "xraw", 128, 2, A1, B1, hTo, lambda kc: ("hTo", kc), tt * 256, xh, "xh", junk, ssq, 0)
                tk.dma("sp", xraw[:NHALO, 0, :], x_own[NOWN:NOWN + NHALO, :], writes=[("xraw", 0)], skey=("xraw", 0))
                norm_transpose(xraw, "xraw", NHALO, 1, A1, B1, hTo, lambda kc: ("hTo", kc), NOWN, xh, "xh", junk, ssq, 0)
            tk.barrier()
            wbuf = [sb("wbuf3_%d" % i, [128, KC, 512], BF16, p3) for i in range(2)]
            cq = sb("cq3", [128, 512], F32, p3)
            sq = sb("sq3", [128, 512], F32, p3)
            hs = sb("hs3", [128, 512], F32, p3)
            vv = sb("vv3", [128, 4, 130], F32, p3)
            zt = sb("zt3", [128, 512], F32, p3)
            t1 = sb("t1_3", [128, 512], F32, p3)
            t2 = sb("t2_3", [128, 512], F32, p3)
            yo = [sb("yo3_%d" % i, [128, 512], BF16, p3) for i in range(2)]
            qo = [sb("qo3_%d" % i, [128, 512], BF16, p3) for i in range(2)]
            vhalo = sb("vhalo", [128, 8, NHALO], F32, p3)
            hmask = sb("hmask", [128, NHALO], F32, p3)
            cw = sb("cw3", [128, 8, 3], F32, p3)
            cb_ = sb("cb3", [128, 8], F32, p3)
            wg = sb("wg3", [128, KC, 24], BF16, p3)
            tk.dma("sp", hmask[:], halomask_in, writes=["hmask"], skey="hmask")
            tk.dma("sp", cw[:], convw_fm, writes=["cw"], skey="cw")
            tk.dma("sp", cb_[:], convb_fm, writes=["cb"], skey="cb")
            tk.dma("pool", wg[:], w_g.rearrange("(kc p) n -> p kc n", p=128), writes=["wg"], skey="wg")
            hkeys = [("hTo", kc) for kc in range(KC)]
            wov = w_own.rearrange("(kc p) n -> p kc n", p=128)
            groups = [(ch * 384, 3, "conv", ch) for ch in range(8)] + [(3072 + j * 512, 4, "q", j) for j in range(4)]
            it = 0
            yoc = 0
            qoc = 0
            for gi, (c0, nm, kind, idx) in enumerate(groups):
                s = gi % 2
                tk.dma("pool", wbuf[s][:, :, 0:nm * 128], wov[:, :, c0:c0 + nm * 128], writes=[("wb", s)], skey=("wb", s))
                tiles = ([8] if kind == "conv" else []) + list(range(8))
                for tt in tiles:
                    ncol = NHALO if tt == 8 else 512
                    cs = NOWN if tt == 8 else tt * 512
                    banks = [4 * (it % 2) + m for m in range(nm)]
                    it += 1
                    ms = [1, 2] if tt == 8 else list(range(nm))
                    for m in ms:
                        for k in range(KC):
                            tk.op("pe", lambda e, m=m, k=k, s=s, b=banks[m], cs=cs, ncol=ncol: e.matmul(
                                PS[b][:, 0:ncol], lhsT=wbuf[s][:, k, m * 128:(m + 1) * 128], rhs=hTo[:, k, cs:cs + ncol],
                                start=(k == 0), stop=(k == KC - 1)),
                                reads=[("wb", s), ("hTo", k)], writes=[pk(banks[m])])
                    if kind == "conv":
                        ch = idx
                        bB, bC, bH = banks
                        if tt == 8:
                            tk.op("act", lambda e, bH=bH: e.activation(out=hs[:, 0:NHALO], in_=PS[bH][:, 0:NHALO], func=AF.Copy),
                                  reads=[pk(bH)], writes=["hs"])
                            tk.op("dve", lambda e, bC=bC: e.tensor_tensor(out=t1[:, 0:NHALO], in0=PS[bC][:, 0:NHALO], in1=hs[:, 0:NHALO], op=ALU.mult),
                                  reads=[pk(bC), "hs"], writes=["t1"])
                            tk.op("dve", lambda e, ch=ch: e.tensor_tensor(out=vhalo[:, ch, :], in0=t1[:, 0:NHALO], in1=hmask[:], op=ALU.mult),
                                  reads=["t1", "hmask"], writes=[("vhalo", ch)])
                            continue
                        tk.op("act", lambda e, bH=bH: e.activation(out=hs[:], in_=PS[bH][:], func=AF.Copy), reads=[pk(bH)], writes=["hs"])
                        tk.op("pool", lambda e, ch=ch, tt=tt: e.tensor_copy(out=vv[:, :, 0:2],
                                                                            in_=vhalo[:, ch, tt * 8:(tt + 1) * 8].rearrange("p (a b) -> p a b", b=2)),
                              reads=[("vhalo", ch)], writes=["vvh"])
                        tk.op("dve", lambda e, bC=bC: e.tensor_tensor(out=vv[:, :, 2:130], in0=PS[bC][:].rearrange("p (a b) -> p a b", a=4),
                                                                      in1=hs[:].rearrange("p (a b) -> p a b", a=4), op=ALU.mult),
                              reads=[pk(bC), "hs"], writes=["vv"])
                        ztv = zt[:].rearrange("p (a b) -> p a b", a=4)
                        tk.op("dve", lambda e, ch=ch, ztv=ztv: e.tensor_scalar(out=ztv, in0=vv[:, :, 2:130], scalar1=cw[:, ch, 2:3], scalar2=cb_[:, ch:ch + 1],
                                                                           op0=ALU.mult, op1=ALU.add),
                              reads=["vv", "cw", "cb"], writes=["zt"])
                        tk.op("dve", lambda e, ch=ch, ztv=ztv: e.scalar_tensor_tensor(out=ztv, in0=vv[:, :, 1:129], scalar=cw[:, ch, 1:2], in1=ztv,
                                                                                  op0=ALU.mult, op1=ALU.add),
                              reads=["vv", "vvh", "cw", "zt"], writes=["zt"])
                        tk.op("dve", lambda e, ch=ch, ztv=ztv: e.scalar_tensor_tensor(out=ztv, in0=vv[:, :, 0:128], scalar=cw[:, ch, 0:1], in1=ztv,
                                                                                  op0=ALU.mult, op1=ALU.add),
                              reads=["vv", "vvh", "cw", "zt"], writes=["zt"])
                        ys = yoc % 2
                        yoc += 1
                        tk.op("dve", lambda e, bB=bB, ys=ys: e.tensor_tensor(out=yo[ys][:], in0=PS[bB][:], in1=zt[:], op=ALU.mult),
                              reads=[pk(bB), "zt"], writes=[("yo", ys)])
                        tk.dma("act", yc_d[ch, :, tt * 512:(tt + 1) * 512], yo[ys][:], reads=[("yo", ys)], writes=[], skey=("yo", ys))
                    else:
                        tk.dma("sp", cq[:], cosq_own[:, tt * 512:(tt + 1) * 512], writes=["cq"], skey="cq")
                        tk.dma("sp", sq[:], sinq_own[:, tt * 512:(tt + 1) * 512], writes=["sq"], skey="sq")
                        for hh in range(2):
                            head = idx * 2 + hh
                            bq, bs = banks[2 * hh], banks[2 * hh + 1]
                            tk.op("dve", lambda e, bq=bq: e.tensor_tensor(out=t1[:], in0=PS[bq][:], in1=cq[:], op=ALU.mult),
                                  reads=[pk(bq), "cq"], writes=["t1"])
                            tk.op("dve", lambda e, bs=bs: e.tensor_tensor(out=t2[:], in0=PS[bs][:], in1=sq[:], op=ALU.mult),
                                  reads=[pk(bs), "sq"], writes=["t2"])
                            qs_ = qoc % 2
                            qoc += 1
                            tk.op("pool", lambda e, qs_=qs_: e.tensor_tensor(out=qo[qs_][:], in0=t1[:], in1=t2[:], op=ALU.add),
                                  reads=["t1", "t2"], writes=[("qo", qs_)])
                            tk.dma("act", q_d[tt * 4:(tt + 1) * 4, :, head, :].rearrange("b p q -> p b q"),
                                   qo[qs_][:].rearrange("p (b q) -> p b q", b=4), reads=[("qo", qs_)], writes=[], skey=("qo", qs_))
            for blk in range(NOB):
                b = 4 * (it % 2)
                it += 1
                for k in range(KC):
                    tk.op("pe", lambda e, k=k, b=b, blk=blk: e.matmul(PS[b][:, 0:24], lhsT=hTo[:, k, blk * 128:(blk + 1) * 128], rhs=wg[:, k, :],
                                                                     start=(k == 0), stop=(k == KC - 1)),
                          reads=["wg", ("hTo", k)], writes=[pk(b)])
                tk.op("act", lambda e, b=b, blk=blk: e.activation(out=gates_sb[:, blk, :], in_=PS[b][:, 0:24], func=AF.Sigmoid),
                      reads=[pk(b)], writes=["gates"])
        tk.barrier()
        if stop_after <= 3:
            return _finish(nc, tk, out, es, [])

        with contextlib.ExitStack() as p4:
            ksl = sb("ksl", [128, S], BF16, p4)
            vsl = sb("vsl", [128, 64, 129], BF16, p4)
            Et = sb("Et", [128, S], BF16, p4)
            winb = sb("winb", [128, 6, 128], BF16, p4)
            diagb = sb("diagb", [128, 2, 128], BF16, p4)
            qT = [sb("qT%d" % i, [128, 512], BF16, p4) for i in range(2)]
            kw = [sb("kw%d" % i, [128, 768], BF16, p4) for i in range(2)]
            vw = [sb("vw%d" % i, [128, 6, 129], BF16, p4) for i in range(2)]
            cmpb = [sb("cmpb%d" % i, [128, 2, 128], BF16, p4) for i in range(2)]
            vmt = [sb("vmt%d" % i, [128, 128], F32, p4) for i in range(2)]
            bvt = [sb("bvt%d" % i, [128, 128], F32, p4) for i in range(2)]
            pT = [sb("pT%d" % i, [128, 512], BF16, p4) for i in range(4)]
            ya = [sb("ya%d" % i, [128, 512], F32, p4) for i in range(2)]
            imp = sb("imp", [128, 128], F32, p4)
            score = sb("score", [128, 128], F32, p4)
            wk = sb("wk", [128, 128], F32, p4)
            m8 = sb("m8", [128, 16], F32, p4)
            selb = sb("selb", [128, 128], F32, p4)
            selbT = sb("selbT", [128, 128], BF16, p4)
            rs = sb("rs", [128, 16], F32, p4)
            tmpo = sb("tmpo", [128, 2, 256], F32, p4)
            tmpC = sb("tmpC", [128, 4, 127], F32, p4)
            obset = [0]
            tk.op("dve", lambda e: e.memset(imp[:], 0.0), writes=["imp"])
            castq = []
            for r in range(0, D, 128):
                for c in range(0, DFF, 2048):
                    castq.append((w1s_d[r:r + 128, c:c + 2048], w_ff1[r:r + 128, c:c + 2048]))
            for r in range(0, DFF, 128):
                castq.append((w2s_d[r:r + 128, :], w_ff2[r:r + 128, :]))
            castn = [0]

            def issue_casts(k):
                for _ in range(k):
                    if castq:
                        o_, i_ = castq.pop(0)
                        tk.dma("pool", o_, i_, writes=[("wcast", castn[0])], skey=("wcast", castn[0] % 4))
                        castn[0] += 1

            tk.dma("sp", Et[:], E_in, writes=["Et"], skey="Et")
            tk.dma("sp", winb[:], winbias_in, writes=["winb"], skey="winb")
            tk.dma("sp", diagb[:], diagbias_in, writes=["diagb"], skey="diagb")
            pcount = [0]
            for g in range(2):
                tk.dma("sp", ksl[:], kslc_d[g], writes=["ksl"], skey="ksl")
                tk.dma("sp", vsl[:], vslc_d[g], writes=["vsl"], skey="vsl")
                for n in range(NOB):
                    s = n % 2
                    lo = max(0, 2 * n - 4)
                    w0 = lo - (2 * n - 4)
                    issue_casts(2)
                    tk.dma("sp", qT[s][:].rearrange("p (h q) -> p h q", h=4), q_d[n, :, g * 4:(g + 1) * 4, :], writes=[("qT", s)], skey=("qT", s))
                    tk.dma("sp", kw[s][:, w0 * 128:768], kwin_d[g, :, lo * 128:(2 * n + 2) * 128], writes=[("kw", s)], skey=("kw", s))
                    tk.dma("sp", vw[s][:, w0:6, :], vwin_d[g, :, lo:2 * n + 2, :], writes=[("vw", s)], skey=("vw", s))
                    tk.dma("sp", cmpb[s][:], cmpbias_in[n], writes=[("cmpb", s)], skey=("cmpb", s))
                    tk.dma("sp", vmt[s][:], vm_in[n], writes=[("vmt", s)], skey=("vmt", s))
                    tk.dma("sp", bvt[s][:], bv_in[n], writes=[("bvt", s)], skey=("bvt", s))

                    def branch(chunks, br, ncolO, yas, first):
                        cmpmode = (ncolO == 257)
                        W = 256 if cmpmode else 129
                        ob = (2, 3) if obset[0] % 2 == 0 else (4, 5)
                        obset[0] += 1
                        nch = len(chunks)
                        slots = []

                        def qk(ci):
                            (kT, kkeys, biases, vap, vkeys) = chunks[ci]
                            sbk = (0, 1, 7)[pcount[0] % 3]
                            ps_ = pcount[0] % 4
                            pcount[0] += 1
                            slots.append(ps_)
                            mms = [(kT, qT[s][:], PS[sbk][:], kkeys + [("qT", s)])]
                            for (bl, br_, bkeys) in biases:
                                mms.append((bl, bc4(br_), PS[sbk][:].rearrange("p (o n) -> p o n", o=4), bkeys))
                            for mi, (l_, r_, o_, keys_) in enumerate(mms):
                                tk.op("pe", lambda e, l_=l_, r_=r_, o_=o_, mi=mi, nm_=len(mms): e.matmul(o_, lhsT=l_, rhs=r_, start=(mi == 0), stop=(mi == nm_ - 1)),
                                      reads=keys_, writes=[pk(sbk)])
                            tk.op("act", lambda e, sbk=sbk, ps_=ps_: e.activation(out=pT[ps_][:], in_=PS[sbk][:], func=AF.Exp),
                                  reads=[pk(sbk)], writes=[("pT", ps_)])

                        def pv(ci):
                            (kT, kkeys, biases, vap, vkeys) = chunks[ci]
                            ps_ = slots[ci]
                            for r in range(4):
                                b_ = ob[r // 2]
                                c0 = (r % 2) * W
                                tk.op("pe", lambda e, r=r, ps_=ps_, vap=vap, ci=ci, b_=b_, c0=c0: e.matmul(
                                    PS[b_][:, c0:c0 + W], lhsT=pT[ps_][:, r * 128:(r + 1) * 128], rhs=vap[:, 0:W],
                                    start=(ci == 0 and r % 2 == 0), stop=(ci == nch - 1), skip_group_check=True),
                                    reads=[("pT", ps_)] + vkeys, writes=[pk(b_)])

                        qk(0)
                        if nch > 1:
                            qk(1)
                        for ci in range(nch):
                            if ci + 2 < nch:
                                qk(ci + 2)
                            pv(ci)

                        def hv(b_):
                            return PS[b_][:, 0:2 * W].rearrange("p (h c) -> p h c", c=W)

                        for bi, b_ in enumerate(ob):
                            tk.op("dve", lambda e, bi=bi, b_=b_: e.tensor_scalar(out=rs[:, 2 * bi:2 * bi + 2], in0=hv(b_)[:, :, 128], scalar1=1e-30, scalar2=None, op0=ALU.max),
                                  reads=[pk(b_)], writes=[("rs", bi)])
                        tk.op("dve", lambda e: e.reciprocal(out=rs[:, 4:8], in_=rs[:, 0:4]), reads=[("rs", 0), ("rs", 1)], writes=["rinv"])
                        gview = gates_sb[:, n, g * 12:(g + 1) * 12].rearrange("p (h b) -> p h b", b=3)[:, :, br]
                        tk.op("dve", lambda e: e.tensor_tensor(out=rs[:, 8:12], in0=rs[:, 4:8], in1=gview, op=ALU.mult), reads=["rinv", "gates"], writes=["rg"])
                        for bi, b_ in enumerate(ob):
                            src = hv(b_)[:, :, 0:128]
                            scb = rs[:, 8 + 2 * bi:10 + 2 * bi].rearrange("p (h o) -> p h o", o=1).broadcast_to([128, 2, 128])
                            dst = ya[yas][:, bi * 256:(bi + 1) * 256].rearrange("p (h c) -> p h c", c=128)
                            if first:
                                tk.op("dve", lambda e, src=src, scb=scb, dst=dst: e.tensor_tensor(out=dst, in0=src, in1=scb, op=ALU.mult),
                                      reads=[pk(b_), "rg"], writes=[("ya", yas)])
                                srcC = hv(b_)[:, :, 129:256]
                                rb = rs[:, 4 + 2 * bi:6 + 2 * bi].rearrange("p (h o) -> p h o", o=1).broadcast_to([128, 2, 127])
                                tk.op("dve", lambda e, srcC=srcC, rb=rb, bi=bi: e.tensor_tensor(out=tmpC[:, 2 * bi:2 * bi + 2, :], in0=srcC, in1=rb, op=ALU.mult),
                                      reads=[pk(b_), "rinv"], writes=[("tmpC", bi)])
                            else:
                                tv = tmpo[:, bi, :].rearrange("p (h c) -> p h c", c=128)
                                tk.op("dve", lambda e, src=src, scb=scb, tv=tv: e.tensor_tensor(out=tv, in0=src, in1=scb, op=ALU.mult),
                                      reads=[pk(b_), "rg"], writes=[("tmpo", bi)])
                                tk.op("pool", lambda e, dst=dst, tv=tv: e.tensor_tensor(out=dst, in0=dst, in1=tv, op=ALU.add),
                                      reads=[("tmpo", bi), ("ya", yas)], writes=[("ya", yas)])
                        if first:
                            tk.op("pool", lambda e: e.tensor_tensor(out=tmpC[:, 0:2, :], in0=tmpC[:, 0:2, :], in1=tmpC[:, 2:4, :], op=ALU.add),
                                  reads=[("tmpC", 0), ("tmpC", 1)], writes=[("tmpC", 0)])
                            tk.op("pool", lambda e: e.tensor_tensor(out=imp[:, 0:127], in0=tmpC[:, 0, :], in1=tmpC[:, 1, :], op=ALU.add),
                                  reads=[("tmpC", 0)], writes=["imp"])

                    yas = n % 2
                    C = n // 8 + 1
                    chunks = []
                    for c in range(C):
                        j = c - (C - 2)
                        biases = [(ident_b[:], cmpb[s][:, j, :], ["ident_b", ("cmpb", s)])] if j >= 0 else []
                        chunks.append((kcmpT[:, g, c * 128:(c + 1) * 128], ["kcmpT"], biases, vcmp[:, g, c, :], ["vcmp", ("vcmpC", g), "vcmp1"]))
                    branch(chunks, 0, 257, yas, True)
                    tk.op("dve", lambda e: e.tensor_tensor(out=score[:], in0=imp[:], in1=vmt[s][:], op=ALU.mult), reads=["imp", ("vmt", s)], writes=["score"])
                    tk.op("dve", lambda e: e.tensor_tensor(out=score[:], in0=score[:], in1=bvt[s][:], op=ALU.add), reads=["score", ("bvt", s)], writes=["score"])
                    tk.op("dve", lambda e: e.max(out=m8[:, 0:8], in_=score[:]), reads=["score"], writes=["m8a"])
                    tk.op("dve", lambda e: e.match_replace(out=wk[:], in_to_replace=m8[:, 0:8], in_values=score[:], imm_value=-1e30),
                          reads=["score", "m8a"], writes=["wk"])
                    tk.op("dve", lambda e: e.max(out=m8[:, 8:16], in_=wk[:]), reads=["wk"], writes=["m8b"])
                    tk.op("dve", lambda e: e.tensor_scalar(out=wk[:], in0=score[:], scalar1=m8[:, 15:16], scalar2=None, op0=ALU.is_ge),
                          reads=["score", "m8b", "wk"], writes=["wk"])
                    tk.op("dve", lambda e: e.tensor_scalar(out=selb[:], in0=wk[:], scalar1=-1.0, scalar2=-NEG, op0=ALU.add, op1=ALU.mult),
                          reads=["wk"], writes=["selb"])
                    chunks = []
                    for w in range(w0, 6):
                        chunks.append((kw[s][:, w * 128:(w + 1) * 128], [("kw", s)], [(ident_b[:], winb[:, w, :], ["ident_b", "winb"])],
                                       vw[s][:, w, :], [("vw", s)]))
                    branch(chunks, 2, 129, yas, False)
                    tk.op("pe", lambda e: e.transpose(out=PS[6][:, 0:128], in_=selb[:], identity=ident_f[:]), reads=["selb", "ident_f"], writes=[pk(6)])
                    tk.op("act", lambda e: e.activation(out=selbT[:], in_=PS[6][:, 0:128], func=AF.Copy), reads=[pk(6)], writes=["selbT"])
                    chunks = []
                    for c in range(2 * n + 2):
                        biases = [(Et[:, c * 128:(c + 1) * 128], selbT[:], ["Et", "selbT"])]
                        if c >= 2 * n:
                            biases.append((ident_b[:], diagb[:, c - 2 * n, :], ["ident_b", "diagb"]))
                        chunks.append((ksl[:, c * 128:(c + 1) * 128], ["ksl"], biases, vsl[:, c, :], ["vsl"]))
                    branch(chunks, 1, 129, yas, False)
                    tk.dma("act", ya_d[n, :, g * 512:(g + 1) * 512], ya[yas][:], reads=[("ya", yas)], writes=[], skey=("ya", yas))
        tk.barrier()
        if stop_after <= 4:
            return _finish(nc, tk, out, es, [])

        with contextlib.ExitStack() as p5:
            wout = sb("wout", [128, KC, D], BF16, p5)
            gcg = sb("gcg", [128, 8], F32, p5)
            gag = sb("gag", [128, 8], F32, p5)
            yat = [sb("yat%d" % i, [128, 1024], F32, p5) for i in range(2)]
            yct = [sb("yct%d" % i, [128, 8, 128], BF16, p5) for i in range(2)]
            xt = [sb("xt5_%d" % i, [128, D], F32, p5) for i in range(2)]
            x1 = [sb("x1t%d" % i, [128, 1, D], F32, p5) for i in range(2)]
            xh = sb("xh5", [128, 1, D], F32, p5)
            junk = sb("junk5", [128, D], F32, p5)
            ssq = sb("ssq5", [128, 32], F32, p5)
            st = sb("st5", [128, 16], F32, p5)
            sqc = [sb("sqc%d" % i, [128, 8, 128], BF16, p5) for i in range(2)]
            ycg = [sb("ycg%d" % i, [128, 8, 128], BF16, p5) for i in range(2)]
            yaT = [sb("yaT%d" % i, [128, 8, 128], BF16, p5) for i in range(2)]
            tA = [sb("tA%d" % i, [128, 512], F32, p5) for i in range(2)]
            tB = [sb("tB%d" % i, [128, 512], F32, p5) for i in range(2)]
            h2T = [sb("h2T%d" % i, [128, KC, 128], BF16, p5) for i in range(2)]
            wouv = w_out.rearrange("(kc p) n -> p kc n", p=128)
            for j in range(0, D, 512):
                tk.dma("pool", wout[:, :, j:j + 512], wouv[:, :, j:j + 512], writes=[("wout", j)], skey=("wout", j))
            wout_keys = [("wout", j) for j in range(0, D, 512)]
            tk.dma("sp", gcg[:], gcg_fm, writes=["gcg"], skey="gcg")
            tk.dma("sp", gag[:], gag_fm, writes=["gag"], skey="gag")

            def rstd_chain(src_col, dst_col, nfeat):
                tk.op("dve", lambda e: e.tensor_scalar(out=st[:, dst_col:dst_col + 1], in0=src_col, scalar1=1.0 / nfeat, scalar2=EPS,
                                                       op0=ALU.mult, op1=ALU.add), reads=["st_in"], writes=["st_a"])
                tk.op("act", lambda e: e.activation(out=st[:, dst_col + 1:dst_col + 2], in_=st[:, dst_col:dst_col + 1], func=AF.Sqrt),
                      reads=["st_a"], writes=["st_b"])
                tk.op("dve", lambda e: e.reciprocal(out=st[:, dst_col + 2:dst_col + 3], in_=st[:, dst_col + 1:dst_col + 2]),
                      reads=["st_b"], writes=[("st_r", dst_col)])
                return st[:, dst_col + 2:dst_col + 3]

            bkc = [0]

            def front(n):
                s = n % 2
                tk.dma("sp", yat[s][:], ya_d[n], writes=[("yat", s)], skey=("yat", s))
                tk.dma("sp", yct[s][:], yc_d[:, :, n * 128:(n + 1) * 128].rearrange("c p t -> p c t"), writes=[("yct", s)], skey=("yct", s))
                tk.dma("sp", xt[s][:], x_own[n * 128:(n + 1) * 128, :], writes=[("xt", s)], skey=("xt", s))
                tk.op("act", lambda e: e.activation(out=junk[:, 0:1024], in_=yat[s][:], func=AF.Square, accum_out=st[:, 0:1]),
                      reads=[("yat", s)], writes=["junk", "st_in"])
                rstd_a = rstd_chain(st[:, 0:1], 1 + 8 * s, 1024)
                for cc in range(8):
                    pb = (bkc[0] // 4) % 2
                    tk.op("pe", lambda e, cc=cc, pb=pb: e.transpose(out=PS[pb][:, (cc % 4) * 128:(cc % 4 + 1) * 128], in_=yat[s][:, cc * 128:(cc + 1) * 128],
                                                                   identity=ident_f[:]), reads=[("yat", s), "ident_f"], writes=[pk(pb)])
                    tk.op("dve", lambda e, cc=cc, pb=pb: e.tensor_scalar(out=yaT[s][:, cc, :], in0=PS[pb][:, (cc % 4) * 128:(cc % 4 + 1) * 128],
                                                                        scalar1=gag[:, cc:cc + 1], scalar2=None, op0=ALU.mult),
                          reads=[pk(pb), "gag"], writes=[("yaT", s)])
                    bkc[0] += 1
                tk.op("act", lambda e: e.activation(out=sqc[s][:], in_=yct[s][:], func=AF.Square), reads=[("yct", s)], writes=[("sqc", s)])
                for cc in range(8):
                    tk.op("pe", lambda e, cc=cc: e.matmul(PS[2][:, 0:1], lhsT=sqc[s][:, cc, :], rhs=ones_b[:, 0:1], start=(cc == 0), stop=(cc == 7)),
                          reads=[("sqc", s), "ones_b"], writes=[pk(2)])
                tk.op("dve", lambda e: e.tensor_copy(out=st[:, 4:5], in_=PS[2][:, 0:1]), reads=[pk(2), ("st_r", 1 + 8 * s)], writes=["st_in"])
                rstd_c = rstd_chain(st[:, 4:5], 5 + 8 * s, 1024)
                for cc in range(8):
                    tk.op("act", lambda e, cc=cc: e.activation(out=ycg[s][:, cc, :], in_=yct[s][:, cc, :], func=AF.Copy, scale=gcg[:, cc:cc + 1]),
                          reads=[("yct", s), "gcg"], writes=[("ycg", s)])
                return rstd_a, rstd_c

            def front2(n, rstd_a, rstd_c):
                s = n % 2
                for dc in range(4):
                    d0 = dc * 512
                    pc_, pa_ = 3 + 2 * (dc % 2), 4 + 2 * (dc % 2)
                    ts_ = dc % 2
                    for cc in range(8):
                        tk.op("pe", lambda e, cc=cc, d0=d0, pc_=pc_: e.matmul(PS[pc_][:], lhsT=ycg[s][:, cc, :], rhs=wout[:, cc, d0:d0 + 512], start=(cc == 0), stop=(cc == 7)),
                              reads=[("ycg", s)] + wout_keys, writes=[pk(pc_)])
                    for cc in range(8):
                        tk.op("pe", lambda e, cc=cc, d0=d0, pa_=pa_: e.matmul(PS[pa_][:], lhsT=yaT[s][:, cc, :], rhs=wout[:, 8 + cc, d0:d0 + 512], start=(cc == 0), stop=(cc == 7)),
                              reads=[("yaT", s)] + wout_keys, writes=[pk(pa_)])
                    tk.op("act", lambda e, pc_=pc_, ts_=ts_: e.activation(out=tA[ts_][:], in_=PS[pc_][:], func=AF.Copy, scale=rstd_c),
                          reads=[pk(pc_), ("st_r", 5 + 8 * s)], writes=[("tA", ts_)])
                    tk.op("dve", lambda e, pa_=pa_, ts_=ts_: e.scalar_tensor_tensor(out=tB[ts_][:], in0=PS[pa_][:], scalar=rstd_a, in1=tA[ts_][:], op0=ALU.mult, op1=ALU.add),
                          reads=[pk(pa_), ("st_r", 1 + 8 * s), ("tA", ts_)], writes=[("tB", ts_)])
                    tk.op("dve", lambda e, d0=d0, ts_=ts_: e.tensor_tensor(out=tB[ts_][:], in0=tB[ts_][:], in1=G1row[:, d0:d0 + 512], op=ALU.mult),
                          reads=[("tB", ts_), "G1row"], writes=[("tB", ts_)])
                    tk.op("pool", lambda e, d0=d0, ts_=ts_: e.tensor_tensor(out=x1[s][:, 0, d0:d0 + 512], in0=tB[ts_][:], in1=xt[s][:, d0:d0 + 512], op=ALU.add),
                          reads=[("tB", ts_), ("xt", s)], writes=[(("x1", s), 0)])

            def back(n):
                s = n % 2
                tk.dma("act", x1_d[n * 128:(n + 1) * 128, :], x1[s][:, 0, :], reads=[(("x1", s), 0)], writes=[], skey=("x1", s))
                norm_transpose(x1[s], ("x1", s), 128, 1, A2, B2, h2T[s], lambda kc, s=s: ("h2T", s), 0, xh, "xh", junk, ssq, 0)
                tk.dma("act", h2_d[:, :, n * 128:(n + 1) * 128].rearrange("k p t -> p k t"), h2T[s][:], reads=[("h2T", s)], writes=[], skey=("h2T", s))

            rr = {}
            for n in range(NOB + 2):
                if n < NOB:
                    rr[n] = front(n)
                if 1 <= n <= NOB:
                    front2(n - 1, *rr[n - 1])
                if n >= 2:
                    back(n - 2)
        tk.barrier()
        if stop_after <= 5:
            return _finish(nc, tk, out, es, [])

        with contextlib.ExitStack() as p6:
            h2s = [sb("h2_%d" % i, [128, KC, 512], BF16, p6) for i in range(2)]
            aT = sb("aT", [128, 64, 512], BF16, p6)
            w1b = [sb("w1b%d" % i, [128, KC, 512], BF16, p6) for i in range(2)]
            w2b = [sb("w2b%d" % i, [128, 16, 512], BF16, p6) for i in range(2)]
            rl = [sb("rl%d" % i, [128, 512], F32, p6) for i in range(2)]
            ytg = sb("ytg", [128, 4, 512], F32, p6)
            yo2 = [sb("yo2_%d" % i, [128, 512], F32, p6) for i in range(2)]
            w1sv = w1s_d.rearrange("(kc p) n -> p kc n", p=128)
            w2sv = w2s_d.rearrange("(fc p) n -> p fc n", p=128)
            cnt1 = 0
            cnt2 = 0
            cnt3 = 0
            tk.dma("sp", h2s[0][:], h2_d[:, :, 0:512].rearrange("k p t -> p k t"), writes=[("h2", 0)], skey=("h2", 0))
            for tt in range(8):
                h2 = h2s[tt % 2]
                h2k = ("h2", tt % 2)
                if tt + 1 < 8:
                    tk.dma("sp", h2s[(tt + 1) % 2][:], h2_d[:, :, (tt + 1) * 512:(tt + 2) * 512].rearrange("k p t -> p k t"),
                           writes=[("h2", (tt + 1) % 2)], skey=("h2", (tt + 1) % 2))
                for fg in range(16):
                    s = cnt1 % 2
                    tk.dma("sp", w1b[s][:], w1sv[:, :, fg * 512:(fg + 1) * 512], writes=[("w1b", s)], skey=("w1b", s))
                    for m in range(4):
                        f = fg * 4 + m
                        pb = cnt1 * 4 % 4 + m
                        for k in range(KC):
                            tk.op("pe", lambda e, k=k, m=m, s=s, pb=pb: e.matmul(PS[pb][:], lhsT=w1b[s][:, k, m * 128:(m + 1) * 128], rhs=h2[:, k, :],
                                                                               start=(k == 0), stop=(k == KC - 1)),
                                  reads=[("w1b", s), h2k], writes=[pk(pb)])
                        rs_ = f % 2
                        tk.op("act", lambda e, pb=pb, rs_=rs_: e.activation(out=rl[rs_][:], in_=PS[pb][:], func=AF.Relu), reads=[pk(pb)], writes=[("rl", rs_)])
                        tk.op("dve", lambda e, f=f, rs_=rs_: e.tensor_tensor(out=aT[:, f, :], in0=rl[rs_][:], in1=rl[rs_][:], op=ALU.mult),
                              reads=[("rl", rs_)], writes=[("aT", f)])
                    cnt1 += 1
                for dg in range(4):
                    for fq in range(4):
                        s = cnt2 % 2
                        cnt2 += 1
                        tk.dma("sp", w2b[s][:], w2sv[:, fq * 16:(fq + 1) * 16, dg * 512:(dg + 1) * 512], writes=[("w2b", s)], skey=("w2b", s))
                        for dc in range(4):
                            for fi in range(16):
                                f = fq * 16 + fi
                                tk.op("pe", lambda e, dc=dc, fi=fi, f=f, s=s: e.matmul(PS[4 + dc][:], lhsT=w2b[s][:, fi, dc * 128:(dc + 1) * 128], rhs=aT[:, f, :],
                                                                                     start=(f == 0), stop=(f == 63)),
                                      reads=[("w2b", s), ("aT", f)], writes=[pk(4 + dc)])
                    for dc in range(4):
                        tk.op("dve", lambda e, dc=dc, dg=dg: e.tensor_scalar(out=ytg[:, dc, :], in0=PS[4 + dc][:], scalar1=G2[:, dg * 4 + dc:dg * 4 + dc + 1],
                                                                           scalar2=None, op0=ALU.mult),
                              reads=[pk(4 + dc), "G2"], writes=[("ytg", dc)])
                    for blk in range(4):
                        pb = cnt3 % 4
                        ys = cnt3 % 2
                        cnt3 += 1
                        for dc in range(4):
                            tk.op("pe", lambda e, dc=dc, blk=blk, pb=pb: e.transpose(out=PS[pb][:, dc * 128:(dc + 1) * 128], in_=ytg[:, dc, blk * 128:(blk + 1) * 128],
                                                                                   identity=ident_f[:]),
                                  reads=[("ytg", dc), "ident_f"], writes=[pk(pb)])
                        tk.op("act", lambda e, pb=pb, ys=ys: e.activation(out=yo2[ys][:], in_=PS[pb][:], func=AF.Copy), reads=[pk(pb)], writes=[("yo2", ys)])
                        r0 = (tt * 4 + blk) * 128
                        tk.dma("act", y2_d[r0:r0 + 128, dg * 512:(dg + 1) * 512], yo2[ys][:], reads=[("yo2", ys)], writes=[], skey=("yo2", ys))
        tk.barrier()
        if stop_after <= 6:
            return _finish(nc, tk, out, es, [])

        with contextlib.ExitStack() as p7:
            xa = [sb("xa%d" % i, [128, D], F32, p7) for i in range(3)]
            yb = [sb("yb%d" % i, [128, D], F32, p7) for i in range(3)]
            ot = [sb("ot%d" % i, [128, D], F32, p7) for i in range(3)]
            nfg = sb("nfg", [128, D], F32, p7)
            junk = sb("junk7", [128, D], F32, p7)
            st = sb("st7", [128, 8], F32, p7)
            tk.dma("sp", nfg[:], nfg_row, writes=["nfg"], skey="nfg")
            for n in range(NOB):
                s = n % 3
                tk.dma("sp", xa[s][:], x1_d[n * 128:(n + 1) * 128, :], writes=[("xa", s)], skey=("xa", s))
                tk.dma("sp", yb[s][:], y2_d[n * 128:(n + 1) * 128, :], writes=[("yb", s)], skey=("yb", s))
                tk.op("dve", lambda e: e.tensor_tensor(out=xa[s][:], in0=xa[s][:], in1=yb[s][:], op=ALU.add), reads=[("xa", s), ("yb", s)], writes=[("xa", s)])
                tk.op("act", lambda e: e.activation(out=junk[:], in_=xa[s][:], func=AF.Square, accum_out=st[:, 0:1]), reads=[("xa", s)], writes=["junk", "st0"])
                tk.op("dve", lambda e: e.tensor_scalar(out=st[:, 1:2], in0=st[:, 0:1], scalar1=1.0 / D, scalar2=EPS, op0=ALU.mult, op1=ALU.add),
                      reads=["st0"], writes=["st1"])
                tk.op("act", lambda e: e.activation(out=st[:, 2:3], in_=st[:, 1:2], func=AF.Sqrt), reads=["st1"], writes=["st2"])
                tk.op("dve", lambda e: e.reciprocal(out=st[:, 3:4], in_=st[:, 2:3]), reads=["st2"], writes=["st3"])
                tk.op("act", lambda e: e.activation(out=ot[s][:], in_=xa[s][:], func=AF.Copy, scale=st[:, 3:4]), reads=[("xa", s), "st3"], writes=[("ot", s)])
                tk.op("dve", lambda e: e.tensor_tensor(out=ot[s][:], in0=ot[s][:], in1=nfg[:], op=ALU.mult), reads=[("ot", s), "nfg"], writes=[("ot", s)])
                tk.dma("act", out[n * 128:(n + 1) * 128, :], ot[s][:], reads=[("ot", s)], writes=[], skey=("ot", s))
        return _finish(nc, tk, out, es, [])


def _finish(nc, tk, out, es, keys):
    tk.final_wait("sp", keys)
    for e in tk.eng.values():
        if e["name"] != "sp" and e["cnt"] > 0:
            tk.eng["sp"]["h"].wait_ge(e["sem"], e["cnt"])
    for s in list(tk.dsem.values()) + tk.free_sems:
        if s["cnt"] > 0:
            tk.eng["sp"]["h"].wait_ge(s["sem"], s["cnt"])
    return nc


def _const_tables(hf):
    inv = (10000.0 ** (-np.arange(0, 128, 2, dtype=np.float32) / 128)).astype(np.float32)
    ang = np.arange(S, dtype=np.float32)[:, None] * inv[None, :]
    cos = np.cos(ang).astype(np.float32).T
    sin = np.sin(ang).astype(np.float32).T
    cosT = np.concatenate([cos, cos], 0)
    sinT = np.concatenate([-sin, sin], 0)
    own_pos = np.concatenate([np.arange(128) + 128 * (2 * n + hf) for n in range(NOB)])
    sc = np.float32(128 ** -0.5)
    t = {}
    t["cos_ctx"] = np.ascontiguousarray(cosT)
    t["sin_ctx"] = np.ascontiguousarray(sinT)
    t["cosq_own"] = np.ascontiguousarray(cosT[:, own_pos] * sc)
    t["sinq_own"] = np.ascontiguousarray(sinT[:, own_pos] * sc)
    key = np.arange(S)
    t["E_in"] = (key[None, :] // 64 == np.arange(128)[:, None]).astype(np.float32).astype(BF)
    t["ident_f_in"] = np.eye(128, dtype=np.float32)
    t["ident_b_in"] = np.eye(128, dtype=np.float32).astype(BF)
    ci = np.arange(512)[:, None]
    sj = np.arange(128)[None, :]
    C = ((ci * 16 <= sj * 64 + 63) & (ci * 16 + 31 >= sj * 64) & (ci < 511)).astype(np.float32)
    t["C_in"] = np.ascontiguousarray(C.reshape(4, 128, 128).transpose(1, 0, 2)).astype(BF)
    kl = np.arange(128)[:, None]
    ql = np.arange(128)[None, :]
    cmpb = np.zeros((NOB, 128, 2, 128), np.float32)
    vm = np.zeros((NOB, 128, 128), np.float32)
    bv = np.zeros((NOB, 128, 128), np.float32)
    for n in range(NOB):
        tq = 128 * (2 * n + hf) + ql
        for j in range(2):
            c = n // 8 - 1 + j
            i = 128 * c + kl
            valid = (c >= 0) & (16 * i + 31 <= tq) & (i <= 510)
            cmpb[n, :, j, :] = np.where(valid, 0.0, NEG)
        tcol = (128 * (2 * n + hf) + np.arange(128))[:, None]
        jb = np.arange(128)[None, :]
        valid = (jb * 64 <= tcol)
        cur = tcol // 64
        forced = (jb == 0) | (jb == cur) | (jb == cur - 1)
        vmn = valid.astype(np.float32)
        vm[n] = vmn
        bv[n] = vmn * 1e4 * forced + (vmn - 1.0)
    t["cmpbias_in"] = cmpb.astype(BF)
    t["vm_in"] = vm
    t["bv_in"] = bv
    wb = np.zeros((128, 6, 128), np.float32)
    for w in range(6):
        off = 128 * (w - 4 - hf)
        valid = (off + kl <= ql) & (off + kl > ql - 512)
        wb[:, w, :] = np.where(valid, 0.0, NEG)
    t["winbias_in"] = wb.astype(BF)
    db = np.zeros((128, 2, 128), np.float32)
    for j in range(2):
        valid = (128 * (j - hf) + kl <= ql)
        db[:, j, :] = np.where(valid, 0.0, NEG)
    t["diagbias_in"] = db.astype(BF)
    hm = np.ones((128, NHALO), np.float32)
    if hf == 0:
        hm[:, 0:2] = 0.0
    t["halomask_in"] = hm
    return t


def _swap_halves(cols):
    c = cols.reshape(-1, 2, 64)
    return c[:, ::-1, :].reshape(-1)


def prep_inputs(inp):
    f = lambda a: np.ascontiguousarray(np.asarray(a, dtype=np.float32))
    x = f(inp["x"])
    w_in = f(inp["w_in"])[0]
    cu = np.arange(5656)
    ub, uc, uh, q = cu[0:1024], cu[1024:2048], cu[2048:3072], cu[3072:4096]
    kc, vc, ksl, vsl, kwn, vwn, gl = (cu[4096:4352], cu[4352:4608], cu[4608:4864], cu[4864:5120],
                                      cu[5120:5376], cu[5376:5632], cu[5632:5656])
    kv_cols = np.concatenate([kc, _swap_halves(kc), ksl, _swap_halves(ksl), kwn, _swap_halves(kwn), vc, vsl, vwn])
    own_cols = []
    for ch in range(8):
        own_cols += [ub[ch * 128:(ch + 1) * 128], uc[ch * 128:(ch + 1) * 128], uh[ch * 128:(ch + 1) * 128]]
    for h in range(8):
        qh = q[h * 128:(h + 1) * 128]
        own_cols += [qh, _swap_halves(qh)]
    own_cols = np.concatenate(own_cols)
    fm = lambda v, n: np.ascontiguousarray(f(v).reshape(n, 128).T)
    shared = {
        "w_ada": f(inp["w_ada"])[0],
        "b_ada_fm": fm(inp["b_ada"][0], 96),
        "n1g_fm": fm(inp["norm1_g"][0], 16),
        "n2g_fm": fm(inp["norm2_g"][0], 16),
        "w_kv": np.ascontiguousarray(w_in[:, kv_cols]),
        "w_own": np.ascontiguousarray(w_in[:, own_cols]),
        "w_g": np.ascontiguousarray(w_in[:, gl]),
        "convw_fm": np.ascontiguousarray(f(inp["conv_w"])[0].reshape(3, 8, 128).transpose(2, 1, 0)),
        "convb_fm": fm(inp["conv_b"][0], 8),
        "pe_k": f(inp["cmp_pe_k"])[0], "pe_v": f(inp["cmp_pe_v"])[0],
        "w1_k": f(inp["cmp_w1_k"])[0], "w1_v": f(inp["cmp_w1_v"])[0],
        "w2_k": f(inp["cmp_w2_k"])[0], "w2_v": f(inp["cmp_w2_v"])[0],
        "gcg_fm": fm(inp["gnorm_conv_g"][0], 8), "gag_fm": fm(inp["gnorm_attn_g"][0], 8),
        "w_out": f(inp["w_out"])[0], "w_ff1": f(inp["w_ff1"])[0], "w_ff2": f(inp["w_ff2"])[0],
        "nfg_row": np.ascontiguousarray(np.broadcast_to(f(inp["normf_g"])[None, :], (128, D))),
    }
    tabs = [_const_tables(0), _const_tables(1)]
    c = f(inp["c"])
    in_maps = []
    for core in range(8):
        b, hf = core // 2, core % 2
        xb = x[b]
        own_pos = np.concatenate([np.arange(128) + 128 * (2 * n + hf) for n in range(NOB)])
        halo_pos = np.concatenate([np.array([128 * (2 * n + hf) - 2, 128 * (2 * n + hf) - 1]) for n in range(NOB)])
        xo = np.empty((NOWN + NHALO, D), np.float32)
        xo[:NOWN] = xb[own_pos]
        xo[NOWN:] = xb[np.maximum(halo_pos, 0)]
        m = dict(shared)
        m.update(tabs[hf])
        m["x_ctx"] = xb
        m["x_own"] = xo
        m["c_fm"] = np.ascontiguousarray(c[b].reshape(16, 128).T)
        in_maps.append(m)
    return in_maps


def kernel(**inputs):
    in_maps = prep_inputs(inputs)
    nc = build_nc()
    res = run_bass_kernel_spmd(nc, in_maps, core_ids=list(range(8)))
    outf = np.empty((4, S, D), np.float32)
    for core in range(8):
        b, hf = core // 2, core % 2
        o = np.asarray(res.results[core]["out"]).reshape(NOB, 128, D)
        outf[b].reshape(64, 128, D)[hf::2] = o
    return outf
```
